# Optimizing a Trainium2 kernel written in Bass

```python
import jax, jax.numpy as jnp
from jax import lax
import numpy as np

D_MODEL = 1024
BATCH = 8
SEQ = 4096
DEPTH = 1

EXPAND = 2
D_MIX = EXPAND * D_MODEL
D_CONV = D_MIX // 2
D_GMLP = D_MIX - D_CONV
CONV_GROUPS = 8
CONV_GROUP_DIM = D_CONV // CONV_GROUPS
GMLP_HEADS = 8
GMLP_HEAD_DIM = D_GMLP // GMLP_HEADS
CONV_WIDTH = 31
CONV_HALF = CONV_WIDTH // 2
CHUNK = 128
EPS = 1e-6
D_IN = 3 * D_CONV + 3 * D_GMLP
SPLITS = (D_CONV, 2 * D_CONV, 3 * D_CONV, 3 * D_CONV + D_GMLP, 3 * D_CONV + 2 * D_GMLP)

kernel_name = "hybrid_conv_gmlp_adaln_encoder"


def rms_norm(x, g):
    xf = x.astype(jnp.float32)
    xf = xf * lax.rsqrt(jnp.mean(xf * xf, axis=-1, keepdims=True) + EPS)
    return (xf * g.astype(jnp.float32)).astype(x.dtype)


def layer_norm(x, g, b):
    xf = x.astype(jnp.float32)
    mu = jnp.mean(xf, axis=-1, keepdims=True)
    var = jnp.mean(jnp.square(xf - mu), axis=-1, keepdims=True)
    y = (xf - mu) * lax.rsqrt(var + EPS) * g.astype(jnp.float32) + b.astype(jnp.float32)
    return y.astype(x.dtype)


def setup_inputs(seed: int = 0) -> dict:
    key = jax.random.key(seed)
    ks = jax.random.split(key, 16)
    L = DEPTH
    nrm = jax.random.normal
    return {
        "x": nrm(ks[0], (BATCH, SEQ, D_MODEL), jnp.float32),
        "c": nrm(ks[1], (BATCH, D_MODEL), jnp.float32),
        "w_ada": nrm(ks[2], (L, D_MODEL, 3 * D_MODEL), jnp.float32) * D_MODEL ** -0.5,
        "b_ada": nrm(ks[3], (L, 3 * D_MODEL), jnp.float32) * 0.02,
        "norm_g": 1.0 + 0.02 * nrm(ks[4], (L, D_MODEL), jnp.float32),
        "w_in": nrm(ks[5], (L, D_MODEL, D_IN), jnp.float32) * D_MODEL ** -0.5,
        "conv_w": nrm(ks[6], (L, CONV_WIDTH, 1, D_CONV), jnp.float32) * CONV_WIDTH ** -0.5,
        "conv_b": nrm(ks[7], (L, D_CONV), jnp.float32) * 0.02,
        "conv_ln_g": 1.0 + 0.02 * nrm(ks[8], (L, D_CONV), jnp.float32),
        "conv_ln_b": nrm(ks[9], (L, D_CONV), jnp.float32) * 0.02,
        "sg_ln_g": 1.0 + 0.02 * nrm(ks[10], (L, D_GMLP), jnp.float32),
        "sg_ln_b": nrm(ks[11], (L, D_GMLP), jnp.float32) * 0.02,
        "w_s": nrm(ks[12], (L, GMLP_HEADS, CHUNK, CHUNK), jnp.float32) * CHUNK ** -0.5,
        "b_s": 1.0 + 0.02 * nrm(ks[13], (L, GMLP_HEADS, CHUNK), jnp.float32),
        "w_out": nrm(ks[14], (L, D_MIX, D_MODEL), jnp.float32) * D_MIX ** -0.5,
        "final_g": 1.0 + 0.02 * nrm(ks[15], (D_MODEL,), jnp.float32),
    }


def conv_group(a, a_glu, a_gate, conv_w, conv_b, ln_g, ln_b):
    a = a * jax.nn.sigmoid(a_glu)
    a = lax.conv_general_dilated(
        a, conv_w.astype(a.dtype), window_strides=(1,),
        padding=[(CONV_HALF, CONV_HALF)],
        dimension_numbers=("NWC", "WIO", "NWC"),
        feature_group_count=D_CONV) + conv_b
    a = jax.nn.silu(layer_norm(a, ln_g, ln_b))
    return a * jax.nn.silu(a_gate)


def gmlp_group(u, v, b_gate, ln_g, ln_b, w_s, b_s):
    B, S, _ = v.shape
    v = layer_norm(v, ln_g, ln_b)
    v = v.reshape(B, S // CHUNK, CHUNK, GMLP_HEADS, GMLP_HEAD_DIM)
    v = jnp.einsum("hpq,bnqhd->bnphd", w_s.astype(v.dtype), v) \
        + jnp.transpose(b_s)[None, None, :, :, None]
    v = v.reshape(B, S, D_GMLP)
    return u * v * jax.nn.silu(b_gate)


def reference(x, c, w_ada, b_ada, norm_g, w_in, conv_w, conv_b, conv_ln_g, conv_ln_b,
              sg_ln_g, sg_ln_b, w_s, b_s, w_out, final_g):
    c_act = jax.nn.silu(c)
    for l in range(DEPTH):
        mod = jnp.einsum("bd,de->be", c_act, w_ada[l]) + b_ada[l]
        shift, scale, gate = jnp.split(mod, 3, axis=-1)
        h = rms_norm(x, norm_g[l]) * (1.0 + scale[:, None, :]) + shift[:, None, :]
        z = jnp.einsum("bsd,de->bse", h, w_in[l])
        a, a_glu, a_gate, u, v, b_gate = jnp.split(z, SPLITS, axis=-1)
        y_a = conv_group(a, a_glu, a_gate, conv_w[l], conv_b[l], conv_ln_g[l], conv_ln_b[l])
        y_b = gmlp_group(u, v, b_gate, sg_ln_g[l], sg_ln_b[l], w_s[l], b_s[l])
        y = jnp.einsum("bse,ed->bsd", jnp.concatenate([y_a, y_b], axis=-1), w_out[l])
        x = x + gate[:, None, :] * y
    return rms_norm(x, final_g)
```

```python
import contextlib
import numpy as np
import concourse.bass as bass
import concourse.mybir as mybir
from concourse.bass_utils import run_bass_kernel_spmd

F32 = mybir.dt.float32
BF16 = mybir.dt.bfloat16
AF = mybir.ActivationFunctionType
ALU = mybir.AluOpType
AX = mybir.AxisListType

EPS = 1e-6
CONVW = 31
TB = 512
HB = 16
SEM_WINDOW = 1500


class Sched:
    ENGS = ("pe", "act", "dve", "pool", "sp")

    def __init__(self):
        self.prog = {e: [] for e in self.ENGS}
        self.res = {}
        self.dma_count = {}

    def _deps(self, reads, writes):
        deps = []
        for k in reads:
            r = self.res.get(k)
            if r and r[0] is not None:
                deps.append((r[0], "raw"))
        for k in writes:
            r = self.res.get(k)
            if r:
                if r[0] is not None:
                    deps.append((r[0], "waw"))
                for t in r[1]:
                    deps.append((t, "war"))
        return deps

    def _commit(self, tok, reads, writes):
        for k in reads:
            self.res.setdefault(k, [None, []])[1].append(tok)
        for k in writes:
            self.res[k] = [tok, []]

    def op(self, eng, fn, reads=(), writes=()):
        deps = self._deps(reads, writes)
        tok = ("c", eng, len(self.prog[eng]))
        self.prog[eng].append(dict(kind="c", fn=fn, deps=deps, tok=tok))
        self._commit(tok, reads, writes)
        return tok

    def dma(self, eng, key, fn, reads=(), writes=(), n=1):
        deps = self._deps(reads, writes)
        c = self.dma_count.get(key, 0) + n
        self.dma_count[key] = c
        tok = ("d", key, c)
        self.prog[eng].append(dict(kind="d", fn=fn, deps=deps, tok=tok, key=key))
        self._commit(tok, reads, writes)
        return tok

    def fence(self, eng, keys):
        deps = self._deps(keys, keys)
        self.prog[eng].append(dict(kind="f", fn=None, deps=deps, tok=None))

    def emit(self, nc):
        needed = set()
        for e in self.ENGS:
            for o in self.prog[e]:
                for (t, ty) in o["deps"]:
                    if t[0] != "c":
                        continue
                    if t[1] == e and e in ("pe", "sp"):
                        continue
                    needed.add((t[1], t[2]))
        incno = {}
        nwin = {}
        for e in self.ENGS:
            cnt = 0
            for i, o in enumerate(self.prog[e]):
                if o["kind"] == "c" and (e, i) in needed:
                    cnt += 1
                    incno[(e, i)] = cnt
            nwin[e] = (cnt + SEM_WINDOW - 1) // SEM_WINDOW
        with contextlib.ExitStack() as st:
            csem = {}
            for e in self.ENGS:
                for w in range(nwin[e]):
                    csem[(e, w)] = st.enter_context(nc.semaphore(f"c_{e}_{w}"))
            dsem = {}
            for i, k in enumerate(self.dma_count):
                dsem[k] = st.enter_context(nc.semaphore(f"d_{i}"))
            block = st.enter_context(nc.Block())

            def body(engobj, e):
                waited_c = {}
                waited_d = {}
                for i, o in enumerate(self.prog[e]):
                    wc = {}
                    wd = {}
                    for (t, ty) in o["deps"]:
                        if t[0] == "c":
                            if t[1] == e and e in ("pe", "sp"):
                                continue
                            n = incno[(t[1], t[2])]
                            if waited_c.get(t[1], 0) >= n:
                                continue
                            wc[t[1]] = max(wc.get(t[1], 0), n)
                        else:
                            if waited_d.get(t[1], 0) >= t[2]:
                                continue
                            wd[t[1]] = max(wd.get(t[1], 0), t[2])
                    for e2, n in wc.items():
                        waited_c[e2] = n
                        w = (n - 1) // SEM_WINDOW
                        engobj.wait_ge(csem[(e2, w)], (n - 1) % SEM_WINDOW + 1)
                    for k, n in wd.items():
                        waited_d[k] = n
                        engobj.wait_ge(dsem[k], 16 * n)
                    if o["kind"] == "c":
                        ins = o["fn"](engobj)
                        n = incno.get((e, i))
                        if n is not None:
                            w = (n - 1) // SEM_WINDOW
                            ins.then_inc(csem[(e, w)], 1)
                    elif o["kind"] == "d":
                        o["fn"](engobj, dsem[o["key"]])

            block.tensor(lambda eo: body(eo, "pe"))
            block.scalar(lambda eo: body(eo, "act"))
            block.vector(lambda eo: body(eo, "dve"))
            block.gpsimd(lambda eo: body(eo, "pool"))
            block.sync(lambda eo: body(eo, "sp"))


class PsumAlloc:
    def __init__(self, n):
        self.n = n
        self.p = 0

    def get(self, k=1):
        if self.p % k:
            self.p += k - self.p % k
        if self.p + k > self.n:
            self.p = 0
        b = self.p
        self.p += k
        return b


def build_program(D, S, w_scratch=True):
    KD = D // 128
    KC = KD
    H = KD
    NT = TB // 128
    NB = S // TB
    CW = min(512, D)
    UPS = D // CW
    CWP = CW // 2
    NPU = D // CWP
    NSLOT = 3
    NUNITS = 2 * NPU + 2 * UPS
    GW = TB + 2 * HB
    E6 = 6 * D
    assert S % TB == 0 and D % 128 == 0

    nc = bass.Bass("TRN2", target_bir_lowering=False)
    x_d = nc.dram_tensor("x", [S, D], F32, kind="ExternalInput")
    c_d = nc.dram_tensor("c", [128, KD], F32, kind="ExternalInput")
    wada_d = nc.dram_tensor("w_ada", [D, 3 * D], F32, kind="ExternalInput")
    bada_d = nc.dram_tensor("b_ada", [1, 3 * D], F32, kind="ExternalInput")
    vecs_d = nc.dram_tensor("vecs", [128, 6, KD], F32, kind="ExternalInput")
    win_d = nc.dram_tensor("w_in", [D, E6], F32, kind="ExternalInput")
    convw_d = nc.dram_tensor("convw3", [128, KC * 4 * 36], F32, kind="ExternalInput")
    wsT_d = nc.dram_tensor("w_sT", [128, H, 128], F32, kind="ExternalInput")
    bs_d = nc.dram_tensor("b_s", [1, H * 128], F32, kind="ExternalInput")
    wout_d = nc.dram_tensor("w_out", [2 * D, D], F32, kind="ExternalInput")
    fg_d = nc.dram_tensor("final_g", [1, D], F32, kind="ExternalInput")
    out_d = nc.dram_tensor("out", [S, D], F32, kind="ExternalOutput")
    wbf_d = nc.dram_tensor("wbf_scratch", [NUNITS, 128, KD, CW], BF16)

    x_ap, out_ap = x_d.ap(), out_d.ap()
    win_r = win_d.ap().rearrange("(j p) n -> p j n", p=128)
    wout_r = wout_d.ap().rearrange("(j p) n -> p j n", p=128)
    wada_r = wada_d.ap().rearrange("(j p) n -> p j n", p=128)
    wbf_ap = wbf_d.ap()

    def bcast_row(handle, off, n, parts=128):
        return bass.AP(handle, off, [[0, parts], [1, n]])

    S_ = Sched()
    ps_alloc = PsumAlloc(6)
    MEAN_BANK, EX2_BANK = 6, 7

    with contextlib.ExitStack() as st:
        def sb(name, shape, dt):
            return st.enter_context(nc.sbuf_tensor("sb_" + name, shape, dt))

        ps = st.enter_context(nc.psum_tensor("ps", [128, 8, 512], F32))

        xt = sb("xt", [128, 2, D], F32)
        junk = sb("junk", [128, D], BF16)
        ssx = sb("ssx", [128, 2], F32)
        rsx = sb("rsx", [128, 2], F32)
        xs = sb("xs", [128, NT, D], BF16)
        hT = sb("hT", [128, 2, KD, TB], BF16)
        wst = sb("wst", [128, NSLOT, KD, CW], BF16)
        G = sb("G", [128, 2, KC, GW], BF16)
        th = sb("th", [128, 2, TB], BF16)
        sga = sb("sga", [128, KC, TB], BF16)
        sgb = sb("sgb", [128, 2, TB], BF16)
        ub = sb("ub", [128, KC, TB], BF16)
        vst = sb("vst", [128, 2, 2, 6], F32)
        vmv = sb("vmv", [128, 2, 2], F32)
        vrs = sb("vrs", [128, 2], F32)
        vnm = sb("vnm", [128, 2], F32)
        vhat = sb("vhat", [128, NT, D], BF16)
        convb = sb("convb", [128, KC, TB], BF16)
        convsq = sb("convsq", [128, 3, TB], BF16)
        mean_sb = sb("mean_sb", [128, TB], F32)
        var_sb = sb("var_sb", [128, TB], F32)
        nmr_sb = sb("nmr_sb", [128, TB], F32)
        ntmp = sb("ntmp", [128, 2, TB], F32)
        sact = sb("sact", [128, 2, TB], BF16)
        tt = sb("tt", [128, 2, TB], BF16)
        xres = sb("xres", [128, NT, D], F32)
        ss2 = sb("ss2", [128, NT], F32)
        rs2 = sb("rs2", [128, NT], F32)
        LSLOTS = 5
        Lw = sb("Lw", [128, LSLOTS, 9, 32, 4], BF16)
        maskP = sb("maskP", [128, 9, 32, 4], BF16)
        W3b = sb("W3b", [128, KC, 4, 36], BF16)
        identP = sb("identP", [128, 32], BF16)
        TP = GW // 4
        Rsb = sb("Rsb", [128, 2, 4, TP], BF16)
        Csb = sb("Csb", [128, 2, TB], BF16)
        wout = sb("wout", [128, 2 * KC, D], BF16)
        ident_b = sb("ident_b", [128, 128], BF16)
        ones_b = sb("ones_b", [128, 128], BF16)
        ones_f = sb("ones_f", [128, 128], F32)
        negh = sb("negh", [128, 1], F32)
        fgmat = sb("fgmat", [128, D], F32)
        Bm = sb("Bm", [128, H, 128], F32)
        wsT = sb("wsT", [128, H, 128], BF16)
        vecs = sb("vecs", [128, 6, KD], F32)
        Gs = sb("Gs", [128, KD], F32)
        shiftv = sb("shiftv", [128, KD], F32)
        cact = sb("cact", [128, KD], F32)
        identrep = var_sb[:].rearrange("p (a b) -> p a b", b=128)
        crep = nmr_sb[:, 0:256].rearrange("p (a b) -> p a b", b=128)
        stg = ntmp
        exttmp = mean_sb

        V_NORM_G, V_CONV_B, V_CLN_G, V_CLN_B, V_SLN_G, V_SLN_B = range(6)

        def bank(b, n=512):
            return ps[:, b, 0:n]

        def bank_bf(b):
            return ps[:, b, :].bitcast(BF16)

        def dma1(eng, key, out, in_, reads=(), writes=()):
            def fn(e, sem, out=out, in_=in_):
                e.dma_start(out=out, in_=in_).then_inc(sem, 16)
            return S_.dma(eng, key, fn, reads=reads, writes=writes)

        dma1("sp", "ld_c", cact[:], c_d.ap(), writes=["cact"])
        dma1("sp", "ld_vecs", vecs[:], vecs_d.ap(), writes=["vecs"])
        w3stg = ub[:].rearrange("p a b -> p (a b)").bitcast(F32)
        dma1("sp", "ld_convw", w3stg[:, 0:KC * 144], convw_d.ap(), writes=[("ub", c) for c in range(KC)])
        dma1("sp", "ld_fg", fgmat[:], bcast_row(fg_d, 0, D), writes=["fgmat"])

        S_.op("pool", lambda e: e.memset(identrep[:], 0.0), writes=["var_sb"])
        for r in range(4):
            S_.op("pool", lambda e, r=r: e.affine_select(
                out=identrep[:, r, :], in_=identrep[:, r, :], compare_op=ALU.not_equal,
                fill=1.0, base=0, pattern=[[-1, 128]], channel_multiplier=1),
                reads=["var_sb"], writes=["var_sb"])
        S_.op("pool", lambda e: e.tensor_copy(out=ident_b[:], in_=identrep[:, 0, :]),
              reads=["var_sb"], writes=["ident_b"])
        S_.op("pool", lambda e: e.memset(ones_b[:], 1.0 / D), writes=["ones_b"])
        S_.op("pool", lambda e: e.memset(ones_f[:], 1.0), writes=["ones_f"])
        S_.op("pool", lambda e: e.memset(negh[:], -0.5), writes=["negh"])
        S_.op("dve", lambda e: e.tensor_scalar(
            out=W3b[:].rearrange("p a b c -> p (a b c)"), in0=w3stg[:, 0:KC * 144], scalar1=0.5,
            scalar2=None, op0=ALU.mult),
            reads=[("ub", c) for c in range(KC)], writes=["W3b"])
        S_.op("dve", lambda e: e.tensor_tensor(out=identP[:], in0=ident_b[:, 0:32], in1=ident_b[:, 32:64],
                                               op=ALU.add), reads=["ident_b"], writes=["identP"])
        S_.op("dve", lambda e: e.tensor_tensor(out=identP[:], in0=identP[:], in1=ident_b[:, 64:96],
                                               op=ALU.add), reads=["ident_b", "identP"], writes=["identP"])
        S_.op("dve", lambda e: e.tensor_tensor(out=identP[:], in0=identP[:], in1=ident_b[:, 96:128],
                                               op=ALU.add), reads=["ident_b", "identP"], writes=["identP"])
        S_.op("dve", lambda e: e.tensor_copy(
            out=maskP[:], in_=identP[:].unsqueeze(1).unsqueeze(3).to_broadcast([128, 9, 32, 4])),
            reads=["identP"], writes=["maskP"])
        S_.op("act", lambda e: e.activation(out=cact[:], in_=cact[:], func=AF.Silu),
              reads=["cact"], writes=["cact"])

        NMB = (3 * D + 511) // 512
        assert NMB <= 6
        stg_i = [0]

        def next_stg():
            s = stg_i[0] % 2
            stg_i[0] += 1
            return s

        def mwid(nb):
            return min(512, 3 * D - nb * 512)

        f32view = lambda t, pat: t[:].rearrange(pat).bitcast(F32)
        big_stg = [
            (f32view(convb, "p a b -> p (a b)"), [("convb", c) for c in range(KC)]),
            (f32view(sga, "p a b -> p (a b)"), [("sga", c) for c in range(KC)]),
            (f32view(ub, "p a b -> p (a b)"), [("ub", c) for c in range(KC)]),
            (f32view(vhat, "p a b -> p (a b)"), [("vhat", q) for q in range(NT)]),
        ]
        CREP_SL = NT - 1
        crep_all = xres[:, CREP_SL, :].rearrange("p (a b) -> p a b", b=128)
        crep_keys = [("xres", CREP_SL)]
        NSTG = NT - 1
        brow = G[:].rearrange("p a b c -> p (a b c)").bitcast(F32)
        brow_keys = [(k_, b_, c_) for k_ in ("G", "Gl", "Gr") for b_ in range(2) for c_ in range(KC)]
        dma1("sp", "ld_bada", brow[0:1, 0:2 * D], bada_d.ap()[:, 0:2 * D], writes=brow_keys)
        browg = Lw[:].rearrange("p a b c d -> p (a b c d)").bitcast(F32)
        browg_keys = [("Lw", sl_) for sl_ in range(LSLOTS)]
        dma1("sp", "ld_badag", browg[0:1, 0:D], bada_d.ap()[:, 2 * D:3 * D], writes=browg_keys)

        def prologue_part1():
            for j in range(KD):
                S_.op("dve", lambda e, j=j: e.tensor_copy(
                    out=crep_all[:, j, :], in_=cact[:, j:j + 1].to_broadcast([128, 128])),
                    reads=["cact"], writes=crep_keys)
            nb1 = (2 * D + 511) // 512
            for j in range(KD):
                stgv, skeys = big_stg[j % 4]
                dma1("sp", ("ld_bigstg", j % 4), stgv[:, 0:2 * D], wada_r[:, j, 0:2 * D], writes=skeys)
                for nb in range(nb1):
                    mw = min(512, 2 * D - nb * 512)
                    S_.op("pe", lambda e, j=j, nb=nb, mw=mw, stgv=stgv: e.matmul(
                        bank(nb, mw), lhsT=crep_all[:, j, :], rhs=stgv[:, nb * 512:nb * 512 + mw],
                        start=(j == 0), stop=False),
                        reads=crep_keys + skeys, writes=[("ps", nb)])
            for nb in range(nb1):
                mw = min(512, 2 * D - nb * 512)
                S_.op("pe", lambda e, nb=nb, mw=mw: e.matmul(
                    bank(nb, mw), lhsT=ones_f[0:1, :], rhs=brow[0:1, nb * 512:nb * 512 + mw],
                    start=False, stop=True),
                    reads=["ones_f"] + brow_keys, writes=[("ps", nb)])

        def gate_cols(n0, n):
            b_, o_ = divmod(n0, 512)
            assert o_ + n <= 512
            return ps[:, MEAN_BANK + b_, o_:o_ + n], ("ps", MEAN_BANK + b_)

        stg_loads = []
        stg_issued = [0]

        def stg_issue(upto):
            while stg_issued[0] < min(upto, len(stg_loads)):
                stg_loads[stg_issued[0]][1]()
                stg_issued[0] += 1

        def stg_register(src_ap):
            k = len(stg_loads)
            sl = k % NSTG
            stg_loads.append((sl, lambda sl=sl, src_ap=src_ap: dma1(
                "sp", ("ld_xres", sl), xres[:, sl, :], src_ap, writes=[("xres", sl)])))
            return k, sl

        def gate_task(j):
            k, sl = stg_register(wada_r[:, j, 2 * D:3 * D])

            def run():
                stg_issue(k + 1)
                for n0 in range(0, D, 512):
                    mw = min(512, D - n0)
                    dst, dkey = gate_cols(n0, mw)
                    S_.op("pe", lambda e, sl=sl, n0=n0, mw=mw, dst=dst: e.matmul(
                        dst, lhsT=crep_all[:, j, :], rhs=xres[:, sl, n0:n0 + mw],
                        start=(j == 0), stop=False),
                        reads=crep_keys + [("xres", sl)], writes=[dkey])
                stg_issue(k + 1 + NSTG)
            return run

        def gate_bias_task():
            for n0 in range(0, D, 512):
                mw = min(512, D - n0)
                dst, dkey = gate_cols(n0, mw)
                S_.op("pe", lambda e, n0=n0, mw=mw, dst=dst: e.matmul(
                    dst, lhsT=ones_f[0:1, :],
                    rhs=browg[0:1, n0:n0 + mw], start=False, stop=True),
                    reads=["ones_f"] + browg_keys, writes=[dkey])

        def mod_cols(c0, n):
            b, o = divmod(c0, 512)
            assert o + n <= 512, (c0, n)
            return ps[:, b, o:o + n], ("ps", b)

        def extract(dst, c0, dst_key):
            for j0 in range(0, KD, 4):
                nj = min(4, KD - j0)
                src, key = mod_cols(c0 + j0 * 128, nj * 128)
                S_.op("dve", lambda e, src=src, nj=nj: e.tensor_tensor(
                    out=exttmp[:, 0:nj * 128], in0=src,
                    in1=identrep[:, 0:nj, :].rearrange("p a b -> p (a b)"), op=ALU.mult),
                    reads=[key, "var_sb"], writes=["mean_sb"])
                S_.op("dve", lambda e, j0=j0, nj=nj: e.tensor_reduce(
                    out=dst[:, j0:j0 + nj],
                    in_=exttmp[:, 0:nj * 128].rearrange("p (a b) -> p a b", b=128),
                    axis=AX.X, op=ALU.add),
                    reads=["mean_sb"], writes=[dst_key])

        def prologue_extract():
            extract(shiftv, 0, "shiftv")
            extract(Gs, D, "Gs")
            S_.op("dve", lambda e: e.scalar_tensor_tensor(
                out=Gs[:], in0=Gs[:], scalar=1.0, in1=vecs[:, V_NORM_G, :], op0=ALU.add, op1=ALU.mult),
                reads=["Gs", "vecs"], writes=["Gs"])

        def fold_task(j):
            k, sl = stg_register(wout_r[:, j, :])

            def run():
                stg_issue(k + 1)
                for n0 in range(0, D, 512):
                    nn = min(512, D - n0)
                    src, key = gate_cols(n0, nn)
                    S_.op("dve", lambda e, n0=n0, nn=nn, sl=sl, src=src: e.tensor_tensor(
                        out=wout[:, j, n0:n0 + nn], in0=xres[:, sl, n0:n0 + nn], in1=src, op=ALU.mult),
                        reads=[("xres", sl), key], writes=[("wout", j)])
                stg_issue(k + 1 + NSTG)
            return run

        def spatial_setup_task(h0):
            def run():
                nh = min(4, H - h0)
                s = next_stg()
                dma1("sp", ("ld_ntmp", s), stg[:, s, 0:nh * 128].rearrange("p (a b) -> p a b", b=128),
                     wsT_d.ap()[:, h0:h0 + nh, :], writes=[("ntmp", s)])
                S_.op("act", lambda e, s=s: e.activation(
                    out=wsT[:, h0:h0 + nh, :].rearrange("p a b -> p (a b)"),
                    in_=stg[:, s, 0:nh * 128], func=AF.Copy),
                    reads=[("ntmp", s)], writes=["wsT"])
                S_.op("pe", lambda e, s=s: e.matmul(
                    bank(MEAN_BANK, nh * 128), lhsT=ones_f[:], rhs=stg[:, s, 0:nh * 128],
                    start=True, stop=True),
                    reads=["ones_f", ("ntmp", s)], writes=[("ps", MEAN_BANK)])
                s2 = next_stg()
                dma1("sp", ("ld_ntmp", s2), stg[:, s2, 0:nh * 128],
                     bcast_row(bs_d, h0 * 128, nh * 128), writes=[("ntmp", s2)])
                for hh in range(nh):
                    h = h0 + hh
                    S_.op("dve", lambda e, h=h, hh=hh, s2=s2: e.scalar_tensor_tensor(
                        out=Bm[:, h, :], in0=ps[:, MEAN_BANK, hh * 128:(hh + 1) * 128],
                        scalar=vecs[:, V_SLN_B, h:h + 1], in1=stg[:, s2, hh * 128:(hh + 1) * 128],
                        op0=ALU.mult, op1=ALU.add),
                        reads=[("ps", MEAN_BANK), "vecs", ("ntmp", s2)], writes=["Bm"])
            return run

        bg_tasks = []
        for h0 in range(0, H, 4):
            bg_tasks.append(spatial_setup_task(h0))
        for j in range(KD):
            bg_tasks.append(gate_task(j))
        bg_tasks.append(gate_bias_task)
        for j in range(2 * KC):
            bg_tasks.append(fold_task(j))

        def bg_step(n=1):
            for _ in range(n):
                if bg_tasks:
                    bg_tasks.pop(0)()

        unit_ctr = [0]

        prefetched = {}

        def load_unit(uid, pieces, blk, prefetch=False):
            if (uid, blk) in prefetched:
                return prefetched.pop((uid, blk))
            slot = unit_ctr[0] % NSLOT
            unit_ctr[0] += 1
            wkey = ("wst", slot)
            if blk == 0 or not w_scratch:
                hk = KD // 2 if KD >= 2 else KD

                def fn(e, sem, slot=slot, pieces=pieces, hk=hk):
                    o = 0
                    for (c0, ncol) in pieces:
                        for j0 in range(0, KD, hk):
                            e.dma_start(out=wst[:, slot, j0:j0 + hk, o:o + ncol],
                                        in_=win_r[:, j0:j0 + hk, c0:c0 + ncol]).then_inc(sem, 16)
                        o += ncol
                S_.dma("pool", ("ld_wst_sw", slot), fn, writes=[wkey],
                       n=len(pieces) * ((KD + hk - 1) // hk))
                if w_scratch:
                    dma1("sp", ("st_wbf", slot), wbf_ap[uid], wst[:, slot, :, :],
                         reads=[wkey], writes=[("wbf", uid)])
            else:
                dma1("sp", ("ld_wst", slot), wst[:, slot, :, :], wbf_ap[uid],
                     reads=[("wbf", uid)], writes=[wkey])
            if prefetch:
                prefetched[(uid, blk)] = slot
            return slot

        def load_pair(uid0, slab_a, slab_b, u, blk):
            return load_unit(uid0 + u, [(slab_a * D + u * CWP, CWP), (slab_b * D + u * CWP, CWP)], blk)

        def load_full(uid0, slab, u, blk):
            return load_unit(uid0 + u, [(slab * D + u * CW, CW)], blk)

        UID_CONV, UID_GATE, UID_UB, UID_V = 0, NPU, NPU + UPS, 2 * NPU + UPS

        def stage_Xpre(i, tiles=None):
            for q in (range(NT) if tiles is None else tiles):
                t0 = i * TB + q * 128
                sl = q % 2
                dma1("sp", ("ld_xt", sl), xt[:, sl, :], x_ap[t0:t0 + 128, :], writes=[("xt", sl)])
                S_.op("act", lambda e, sl=sl, q=q: e.activation(
                    out=xs[:, q, :], in_=xt[:, sl, :], func=AF.Square, accum_out=ssx[:, sl:sl + 1]),
                    reads=[("xt", sl)], writes=[("xs", q), ("ssx", sl)])
                S_.op("dve", lambda e, sl=sl: e.tensor_scalar(
                    out=rsx[:, sl:sl + 1], in0=ssx[:, sl:sl + 1], scalar1=1.0 / D, scalar2=EPS,
                    op0=ALU.mult, op1=ALU.add), reads=[("ssx", sl)], writes=[("rsx", sl)])
                S_.op("pool", lambda e, sl=sl: e.tensor_tensor(
                    out=rsx[:, sl:sl + 1], in0=rsx[:, sl:sl + 1], in1=negh[:], op=ALU.pow),
                    reads=[("rsx", sl), "negh"], writes=[("rsx", sl)])
                S_.op("dve", lambda e, sl=sl, q=q: e.tensor_scalar(
                    out=xs[:, q, :], in0=xt[:, sl, :], scalar1=rsx[:, sl:sl + 1], scalar2=None,
                    op0=ALU.mult), reads=[("xt", sl), ("rsx", sl)], writes=[("xs", q)])

        def stage_XT(i):
            nb4 = max(1, KD // 2)
            pb = ps_alloc.get(4 if nb4 > 2 else nb4)
            for q in range(NT):
                for j in range(KD):
                    b = pb + j // 2
                    col = (j % 2) * 512 + q * 128
                    S_.op("pe", lambda e, q=q, j=j, b=b, col=col: e.transpose(
                        bank_bf(b)[:, col:col + 128], xs[:, q, j * 128:(j + 1) * 128], ident_b[:]),
                        reads=[("xs", q), "ident_b"], writes=[("ps", b)])
            for j in range(KD):
                b = pb + j // 2
                col = (j % 2) * 512
                if (j // 2) % 2 == 0:
                    S_.op("dve", lambda e, j=j, b=b, col=col: e.tensor_scalar(
                        out=hT[:, i % 2, j, :], in0=bank_bf(b)[:, col:col + 512],
                        scalar1=Gs[:, j:j + 1], scalar2=shiftv[:, j:j + 1], op0=ALU.mult, op1=ALU.add),
                        reads=[("ps", b), "Gs", "shiftv"], writes=[("hT", i % 2, j)])
                else:
                    S_.op("act", lambda e, j=j, b=b, col=col: e.activation(
                        out=hT[:, i % 2, j, :], in_=bank_bf(b)[:, col:col + 512], func=AF.Identity,
                        bias=shiftv[:, j:j + 1], scale=Gs[:, j:j + 1]),
                        reads=[("ps", b), "Gs", "shiftv"], writes=[("hT", i % 2, j)])

        def zfm(slot, cc, b, hb):
            for j in range(KD):
                S_.op("pe", lambda e, j=j, slot=slot, cc=cc, b=b, hb=hb: e.matmul(
                    bank(b, TB), lhsT=wst[:, slot, j, cc * 128:(cc + 1) * 128], rhs=hT[:, hb, j, :],
                    start=(j == 0), stop=(j == KD - 1)),
                    reads=[("wst", slot), ("hT", hb, j)], writes=[("ps", b)])

        CPF = CW // 128
        CPP = CWP // 128

        zc_slots = {}

        def zc_chunk(i, c):
            gb = i % 2
            u, cc = divmod(c, CPP)
            if cc == 0:
                zc_slots[(i, u)] = load_pair(UID_CONV, 0, 1, u, i)
            sp_ = zc_slots[(i, u)]
            ba = ps_alloc.get()
            zfm(sp_, cc, ba, i % 2)
            bg = ps_alloc.get()
            zfm(sp_, CPP + cc, bg, i % 2)
            ts_ = c % 2
            S_.op("act", lambda e, bg=bg, ts_=ts_: e.activation(
                out=th[:, ts_, :], in_=bank(bg, TB), func=AF.Tanh, scale=0.5),
                reads=[("ps", bg)], writes=[("th", ts_)])
            S_.op("dve", lambda e, ba=ba, ts_=ts_, c=c, gb=gb: e.scalar_tensor_tensor(
                out=G[:, gb, c, HB:HB + TB], in0=th[:, ts_, :], scalar=1.0, in1=bank(ba, TB),
                op0=ALU.add, op1=ALU.mult),
                reads=[("th", ts_), ("ps", ba)], writes=[("G", gb, c)])
            bg_step()
            if i > 0:
                ob = (i - 1) % 2
                S_.op("dve", lambda e, gb=gb, ob=ob, c=c: e.tensor_copy(
                    out=G[:, ob, c, HB + TB:HB + TB + HB], in_=G[:, gb, c, HB:HB + HB]),
                    reads=[("G", gb, c)], writes=[("Gr", ob, c)])
                S_.op("dve", lambda e, gb=gb, ob=ob, c=c: e.tensor_copy(
                    out=G[:, gb, c, 0:HB], in_=G[:, ob, c, TB:TB + HB]),
                    reads=[("G", ob, c)], writes=[("Gl", gb, c)])

        def zc_edges(i):
            gb = i % 2
            if i == 0:
                S_.op("pool", lambda e, gb=gb: e.memset(G[:, gb, :, 0:HB], 0.0),
                      writes=[("Gl", gb, c) for c in range(KC)])
            if i == NB - 1:
                S_.op("pool", lambda e, gb=gb: e.memset(G[:, gb, :, HB + TB:HB + TB + HB], 0.0),
                      writes=[("Gr", gb, c) for c in range(KC)])

        def stage_Zc(i):
            zc_edges(i)
            for c in range(KC):
                zc_chunk(i, c)

        def ln_MA(c):
            ts_ = c % 2
            S_.op("dve", lambda e, c=c, ts_=ts_: e.tensor_tensor(
                out=ntmp[:, ts_, :], in0=convb[:, c, :], in1=var_sb[:], op=ALU.mult),
                reads=[("convb", c), "var_sb"], writes=[("ntmp", ts_)])
            S_.op("dve", lambda e, ts_=ts_: e.tensor_tensor(
                out=ntmp[:, ts_, :], in0=ntmp[:, ts_, :], in1=nmr_sb[:], op=ALU.add),
                reads=[("ntmp", ts_), "nmr_sb"], writes=[("ntmp", ts_)])
            S_.op("act", lambda e, c=c, ts_=ts_: e.activation(
                out=sact[:, ts_, :], in_=ntmp[:, ts_, :], func=AF.Silu,
                bias=vecs[:, V_CLN_B, c:c + 1], scale=vecs[:, V_CLN_G, c:c + 1]),
                reads=[("ntmp", ts_), "vecs"], writes=[("sact", ts_)])

        def ln_F(c):
            ts_ = c % 2
            S_.op("dve", lambda e, c=c, ts_=ts_: e.tensor_tensor(
                out=convb[:, c, :], in0=sact[:, ts_, :], in1=sga[:, c, :], op=ALU.mult),
                reads=[("sact", ts_), ("sga", c)], writes=[("convb", c)])

        def stage_lnapply_gate(i_ln, i_gate):
            slots = {}
            for step in range(KC + 1):
                if i_ln is not None and step < KC:
                    ln_MA(step)
                c = step - 1
                if c < 0:
                    continue
                if i_ln is not None:
                    ln_F(c)
                if i_gate is not None:
                    u, cc = divmod(c, CPF)
                    if cc == 0:
                        slots[u] = load_full(UID_GATE, 2, u, i_gate)
                    b = ps_alloc.get()
                    zfm(slots[u], cc, b, i_gate % 2)
                    S_.op("act", lambda e, b=b, c=c: e.activation(
                        out=sga[:, c, :], in_=bank(b, TB), func=AF.Silu),
                        reads=[("ps", b)], writes=[("sga", c)])
                    bg_step()

        def stage_Zr_ub(i, with_spatial=False):
            for u in range(NPU):
                sp_ = load_pair(UID_UB, 3, 5, u, i)
                for cc in range(CPP):
                    c = u * CPP + cc
                    if with_spatial and c >= 1:
                        stage_spatial(i, [c - 1])
                    bu = ps_alloc.get()
                    zfm(sp_, cc, bu, i % 2)
                    bb = ps_alloc.get()
                    zfm(sp_, CPP + cc, bb, i % 2)
                    ts_ = c % 2
                    S_.op("act", lambda e, bb=bb, ts_=ts_: e.activation(
                        out=sgb[:, ts_, :], in_=bank(bb, TB), func=AF.Silu),
                        reads=[("ps", bb)], writes=[("sgb", ts_)])
                    S_.op("dve", lambda e, bu=bu, ts_=ts_, c=c: e.tensor_tensor(
                        out=ub[:, c, :], in0=bank(bu, TB), in1=sgb[:, ts_, :], op=ALU.mult),
                        reads=[("ps", bu), ("sgb", ts_)], writes=[("ub", c)])
                    bg_step()
        def stage_Zr_v(i):
            assert UPS <= 2
            vslots = [load_full(UID_V, 4, u, i) for u in range(UPS)]
            for q in range(NT):
                pb = ps_alloc.get(UPS)
                sl = q % 2
                for u in range(UPS):
                    for j in range(KD):
                        S_.op("pe", lambda e, j=j, q=q, u=u, pb=pb: e.matmul(
                            bank(pb + u, CW), lhsT=hT[:, i % 2, j, q * 128:(q + 1) * 128],
                            rhs=wst[:, vslots[u], j, :], start=(j == 0), stop=(j == KD - 1)),
                            reads=[("wst", vslots[u]), ("hT", i % 2, j)], writes=[("ps", pb + u)])
                    S_.op("dve", lambda e, u=u, pb=pb, sl=sl: e.bn_stats(
                        out=vst[:, sl, u, :], in_=bank(pb + u, CW)),
                        reads=[("ps", pb + u)], writes=[("vst", sl, u)])
                S_.op("dve", lambda e, sl=sl: e.bn_aggr(
                    out=vmv[:, sl, :], in_=vst[:, sl, 0:UPS, :].rearrange("p a b -> p (a b)")),
                    reads=[("vst", sl, u) for u in range(UPS)], writes=[("vmv", sl)])
                S_.op("dve", lambda e, sl=sl: e.tensor_scalar(
                    out=vrs[:, sl:sl + 1], in0=vmv[:, sl, 1:2], scalar1=EPS, scalar2=None,
                    op0=ALU.add), reads=[("vmv", sl)], writes=[("vrs", sl)])
                S_.op("pool", lambda e, sl=sl: e.tensor_tensor(
                    out=vrs[:, sl:sl + 1], in0=vrs[:, sl:sl + 1], in1=negh[:], op=ALU.pow),
                    reads=[("vrs", sl), "negh"], writes=[("vrs", sl)])
                S_.op("dve", lambda e, sl=sl: e.scalar_tensor_tensor(
                    out=vnm[:, sl:sl + 1], in0=vmv[:, sl, 0:1], scalar=-1.0, in1=vrs[:, sl:sl + 1],
                    op0=ALU.mult, op1=ALU.mult), reads=[("vmv", sl), ("vrs", sl)], writes=[("vnm", sl)])
                for u in range(UPS):
                    S_.op("act", lambda e, u=u, pb=pb, sl=sl, q=q: e.activation(
                        out=vhat[:, q, u * CW:(u + 1) * CW], in_=bank(pb + u, CW), func=AF.Identity,
                        bias=vnm[:, sl:sl + 1], scale=vrs[:, sl:sl + 1]),
                        reads=[("ps", pb + u), ("vnm", sl), ("vrs", sl)], writes=[("vhat", q)])
                bg_step()

        lw_ctr = [0]

        def lw_gen(c, g):
            slot = lw_ctr[0] % LSLOTS
            lw_ctr[0] += 1
            S_.op("dve", lambda e, c=c, g=g, slot=slot: e.tensor_tensor(
                out=Lw[:, slot, :, :, :], in0=maskP[:],
                in1=W3b[:, c, g, :].rearrange("p (j r) -> p j r", r=4).unsqueeze(2).to_broadcast([128, 9, 32, 4]),
                op=ALU.mult),
                reads=["maskP", "W3b"], writes=[("Lw", slot)])
            return slot

        def stage_conv(i, xpre_blk=None, i_zc=None):
            gb = i % 2
            if i_zc is not None:
                zc_edges(i_zc)
            gkeys = lambda c: [("G", gb, c), ("Gl", gb, c), ("Gr", gb, c)]

            def stats_mm(c):
                sq = c % 3
                S_.op("pe", lambda e, c=c: e.matmul(
                    bank(MEAN_BANK, TB), lhsT=ones_b[:], rhs=convb[:, c, :],
                    start=(c == 0), stop=(c == KC - 1)),
                    reads=["ones_b", ("convb", c)], writes=[("ps", MEAN_BANK)])
                S_.op("pe", lambda e, c=c, sq=sq: e.matmul(
                    bank(EX2_BANK, TB), lhsT=ones_b[:], rhs=convsq[:, sq, :],
                    start=(c == 0), stop=(c == KC - 1)),
                    reads=["ones_b", ("convsq", sq)], writes=[("ps", EX2_BANK)])

            def conv_in(c):
                rs = c % 2
                pb = ps_alloc.get(2)
                gview = G[:, gb, c, :].rearrange("p (t s) -> p s t", s=4)
                for g in range(4):
                    bk, col0 = pb + g // 2, (g % 2) * TP
                    for s_ in range(4):
                        S_.op("pe", lambda e, g=g, s_=s_, bk=bk, col0=col0: e.matmul(
                            ps[32 * s_:32 * s_ + 32, bk, col0:col0 + TP],
                            lhsT=ident_b[:, 32 * g:32 * g + 32], rhs=gview[:, s_, :],
                            start=True, stop=True, tile_position=(0, 32 * s_)),
                            reads=["ident_b"] + gkeys(c), writes=[("ps", bk)])
                for h in range(2):
                    S_.op("act", lambda e, h=h, rs=rs, pb=pb: e.activation(
                        out=Rsb[:, rs, 2 * h:2 * h + 2, :],
                        in_=ps[:, pb + h, 0:2 * TP].rearrange("p (a b) -> p a b", b=TP), func=AF.Copy),
                        reads=[("ps", pb + h)], writes=[("Rsb", rs, h)])

            def conv_out(c):
                rs = c % 2
                sq = c % 3
                b2 = ps_alloc.get()
                for r in range(4):
                    for g in range(4):
                        S_.op("pe", lambda e, g=g, r=r, b2=b2, rs=rs: e.matmul(
                            ps[32 * g:32 * g + 32, b2, r * 128:(r + 1) * 128],
                            lhsT=ident_b[:].rearrange("p (c r) -> p r c", r=4)[:, r, :],
                            rhs=Csb[:, rs, g * 128:(g + 1) * 128],
                            start=True, stop=True, tile_position=(0, 32 * g)),
                            reads=["ident_b", ("Csb", rs)], writes=[("ps", b2)])
                src = ps[:, b2, :].rearrange("p (r t) -> p r t", r=4)
                S_.op("act", lambda e, c=c, src=src: e.activation(
                    out=convb[:, c, :].rearrange("p (t r) -> p r t", r=4), in_=src, func=AF.Identity,
                    bias=vecs[:, V_CONV_B, c:c + 1], scale=1.0),
                    reads=[("ps", b2), "vecs"], writes=[("convb", c)])
                S_.op("act", lambda e, c=c, sq=sq, src=src: e.activation(
                    out=convsq[:, sq, :].rearrange("p (t r) -> p r t", r=4), in_=src, func=AF.Square,
                    bias=vecs[:, V_CONV_B, c:c + 1], scale=1.0),
                    reads=[("ps", b2), "vecs"], writes=[("convsq", sq)])

            pend = {}
            gen_q = [(c, g) for c in range(KC) for g in range(4)]
            lw_slots = {}

            def gen_next(n):
                for _ in range(n):
                    if gen_q:
                        c_, g_ = gen_q.pop(0)
                        lw_slots[(c_, g_)] = lw_gen(c_, g_)

            gen_next(LSLOTS - 1)
            sk = 1 if i_zc is not None else 0
            for step in range(KC + 3 + sk):
                if step < KC and i_zc is not None:
                    zc_chunk(i_zc, step)
                c = step - sk
                if 0 <= c < KC:
                    conv_in(c)
                c = step - 1 - sk
                if 0 <= c < KC:
                    rs = c % 2
                    b = ps_alloc.get()
                    for g in range(4):
                        while (c, g) not in lw_slots:
                            gen_next(1)
                        sl_ = lw_slots[(c, g)]
                        for jj in range(9):
                            S_.op("pe", lambda e, g=g, jj=jj, b=b, rs=rs, sl_=sl_: e.matmul(
                                ps[:, b, g * 128:(g + 1) * 128], lhsT=Lw[:, sl_, jj, :, :].rearrange("p c r -> p (c r)"),
                                rhs=Rsb[:, rs, g, jj:jj + 128], start=(jj == 0), stop=(jj == 8)),
                                reads=[("Lw", sl_), ("Rsb", rs, g // 2)], writes=[("ps", b)])
                        gen_next(1)
                    S_.op("act", lambda e, b=b, rs=rs: e.activation(
                        out=Csb[:, rs, :], in_=bank(b, TB), func=AF.Copy),
                        reads=[("ps", b)], writes=[("Csb", rs)])
                    if xpre_blk is not None and c < NT:
                        tl = [c] if KC >= NT else list(range(c * NT // KC, (c + 1) * NT // KC))
                        stage_Xpre(xpre_blk, tl)
                c = step - 2 - sk
                if 0 <= c < KC:
                    conv_out(c)
                c = step - 3 - sk
                if 0 <= c < KC:
                    stats_mm(c)

            S_.op("act", lambda e: e.activation(out=mean_sb[:], in_=bank(MEAN_BANK, TB), func=AF.Copy),
                  reads=[("ps", MEAN_BANK)], writes=["mean_sb"])
            S_.op("dve", lambda e: e.scalar_tensor_tensor(
                out=nmr_sb[:], in0=mean_sb[:], scalar=-1.0, in1=mean_sb[:], op0=ALU.mult, op1=ALU.mult),
                reads=["mean_sb"], writes=["nmr_sb"])
            S_.op("dve", lambda e: e.scalar_tensor_tensor(
                out=var_sb[:], in0=bank(EX2_BANK, TB), scalar=EPS, in1=nmr_sb[:],
                op0=ALU.add, op1=ALU.add),
                reads=[("ps", EX2_BANK), "nmr_sb"], writes=["var_sb"])
            S_.op("act", lambda e: e.activation(out=var_sb[:], in_=var_sb[:], func=AF.Sqrt),
                  reads=["var_sb"], writes=["var_sb"])
            S_.op("dve", lambda e: e.reciprocal(out=var_sb[:], in_=var_sb[:]),
                  reads=["var_sb"], writes=["var_sb"])
            S_.op("dve", lambda e: e.scalar_tensor_tensor(
                out=nmr_sb[:], in0=mean_sb[:], scalar=-1.0, in1=var_sb[:], op0=ALU.mult, op1=ALU.mult),
                reads=["mean_sb", "var_sb"], writes=["nmr_sb"])

        def stage_spatial(i, heads=None):
            for h in (range(H) if heads is None else heads):
                b = ps_alloc.get()
                for q in range(NT):
                    S_.op("pe", lambda e, h=h, q=q, b=b: e.matmul(
                        ps[:, b, q * 128:(q + 1) * 128], lhsT=vhat[:, q, h * 128:(h + 1) * 128],
                        rhs=wsT[:, h, :], start=True, stop=True),
                        reads=[("vhat", q), "wsT"], writes=[("ps", b)])
                ts_ = h % 2
                S_.op("dve", lambda e, h=h, b=b, ts_=ts_: e.scalar_tensor_tensor(
                    out=tt[:, ts_, :].rearrange("p (a b) -> p a b", b=128),
                    in0=ps[:, b, :].rearrange("p (a b) -> p a b", b=128),
                    scalar=vecs[:, V_SLN_G, h:h + 1],
                    in1=Bm[:, h:h + 1, :].to_broadcast([128, NT, 128]),
                    op0=ALU.mult, op1=ALU.add),
                    reads=[("ps", b), "vecs", "Bm"], writes=[("tt", ts_)])
                S_.op("dve", lambda e, h=h, ts_=ts_: e.tensor_tensor(
                    out=ub[:, h, :], in0=tt[:, ts_, :], in1=ub[:, h, :], op=ALU.mult),
                    reads=[("tt", ts_), ("ub", h)], writes=[("ub", h)])
                bg_step()

        NO = (D + 511) // 512
        OW = min(512, D)

        def stage_out_pre(i):
            for q in range(NT):
                t0 = i * TB + q * 128
                dma1("sp", ("ld_xres", q), xres[:, q, :], x_ap[t0:t0 + 128, :], writes=[("xres", q)])

        def stage_out(i):
            for q in range(NT):
                t0 = i * TB + q * 128
                sl = q
                pb = ps_alloc.get(NO)
                for n in range(NO):
                    for j in range(2 * KC):
                        src = convb if j < KC else ub
                        jj = j % KC
                        key = ("convb", jj) if j < KC else ("ub", jj)
                        S_.op("pe", lambda e, src=src, jj=jj, j=j, q=q, n=n, pb=pb: e.matmul(
                            bank(pb + n, OW), lhsT=src[:, jj, q * 128:(q + 1) * 128],
                            rhs=wout[:, j, n * OW:(n + 1) * OW],
                            start=(j == 0), stop=(j == 2 * KC - 1)),
                            reads=[key, ("wout", j)], writes=[("ps", pb + n)])
                    S_.op("dve", lambda e, n=n, pb=pb, sl=sl: e.tensor_tensor(
                        out=xres[:, sl, n * OW:(n + 1) * OW], in0=bank(pb + n, OW),
                        in1=xres[:, sl, n * OW:(n + 1) * OW], op=ALU.add),
                        reads=[("ps", pb + n), ("xres", sl)], writes=[("xres", sl)])
                S_.op("act", lambda e, sl=sl: e.activation(
                    out=junk[:], in_=xres[:, sl, :], func=AF.Square, accum_out=ss2[:, sl:sl + 1]),
                    reads=[("xres", sl)], writes=["junk", ("ss2", sl)])
                S_.op("dve", lambda e, sl=sl: e.tensor_scalar(
                    out=rs2[:, sl:sl + 1], in0=ss2[:, sl:sl + 1], scalar1=1.0 / D, scalar2=EPS,
                    op0=ALU.mult, op1=ALU.add), reads=[("ss2", sl)], writes=[("rs2", sl)])
                S_.op("pool", lambda e, sl=sl: e.tensor_tensor(
                    out=rs2[:, sl:sl + 1], in0=rs2[:, sl:sl + 1], in1=negh[:], op=ALU.pow),
                    reads=[("rs2", sl), "negh"], writes=[("rs2", sl)])
                S_.op("dve", lambda e, sl=sl: e.scalar_tensor_tensor(
                    out=xres[:, sl, :], in0=xres[:, sl, :], scalar=rs2[:, sl:sl + 1], in1=fgmat[:],
                    op0=ALU.mult, op1=ALU.mult),
                    reads=[("xres", sl), ("rs2", sl), "fgmat"], writes=[("xres", sl)])
                dma1("act", ("st_out", sl), out_ap[t0:t0 + 128, :], xres[:, sl, :],
                     reads=[("xres", sl)], writes=[("out", i, q)])

        stage_Xpre(0)
        prologue_part1()
        prologue_extract()
        stage_XT(0)
        n_sp = (H + 3) // 4
        for _ in range(n_sp):
            bg_tasks.pop(0)()
        stg_issue(NSTG)
        bg_hold = bg_tasks[:]
        del bg_tasks[:]
        stage_Zc(0)
        bg_tasks.extend(bg_hold)
        load_unit(UID_GATE + 0, [(2 * D + 0 * CW, CW)], 0, prefetch=True)
        if NB > 1:
            stage_Xpre(1)
        for i in range(NB):
            ps_alloc.n = 8 if i > 0 else 6
            if i > 0:
                stage_out_pre(i - 1)
            stage_lnapply_gate(i - 1 if i > 0 else None, i)
            stage_Zr_v(i)
            if i > 0:
                stage_out(i - 1)
            if i + 1 < NB:
                stage_XT(i + 1)
            stage_Zr_ub(i, with_spatial=True)
            stage_spatial(i, [H - 1])
            bg_step(len(bg_tasks))
            ps_alloc.n = 6
            if ps_alloc.p >= 6:
                ps_alloc.p = 0
            stage_conv(i, i + 2 if i + 2 < NB else None, i + 1 if i + 1 < NB else None)
        ps_alloc.n = 8
        stage_out_pre(NB - 1)
        stage_lnapply_gate(NB - 1, None)
        stage_out(NB - 1)
        S_.fence("sp", [("out", i, q) for i in range(NB) for q in range(NT)])

        S_.emit(nc)
    return nc


def host_layout(D, x_b, c_b, w_ada, b_ada, norm_g, w_in, conv_w, conv_b, conv_ln_g, conv_ln_b,
                sg_ln_g, sg_ln_b, w_s, b_s, w_out, final_g):
    KD = D // 128
    f = lambda a: np.ascontiguousarray(a, dtype=np.float32)
    fm = lambda v: f(np.asarray(v).reshape(KD, 128).T)
    vecs = np.stack([fm(norm_g), fm(conv_b), fm(conv_ln_g), fm(conv_ln_b), fm(sg_ln_g), fm(sg_ln_b)],
                    axis=1)
    cw = np.asarray(conv_w)[:, 0, :]
    W3 = np.zeros((128, KD, 4, 9, 4), np.float32)
    for s_ in range(4):
        for jj in range(9):
            for r in range(4):
                k = 4 * jj + s_ - 1 - r
                if 0 <= k < CONVW:
                    W3[s_ * 32:(s_ + 1) * 32, :, :, jj, r] = cw[k].reshape(KD, 4, 32).transpose(2, 0, 1)
    return {
        "x": f(x_b),
        "c": fm(c_b),
        "w_ada": f(w_ada),
        "b_ada": f(np.asarray(b_ada).reshape(1, -1)),
        "vecs": f(vecs),
        "w_in": f(w_in),
        "convw3": f(W3.reshape(128, KD * 144)),
        "w_sT": f(np.asarray(w_s).transpose(2, 0, 1)),
        "b_s": f(np.asarray(b_s).reshape(1, -1)),
        "w_out": f(w_out),
        "final_g": f(np.asarray(final_g).reshape(1, -1)),
    }


def kernel(x, c, w_ada, b_ada, norm_g, w_in, conv_w, conv_b, conv_ln_g, conv_ln_b,
           sg_ln_g, sg_ln_b, w_s, b_s, w_out, final_g):
    x = np.asarray(x)
    B, S, D = x.shape
    nc = build_program(D, S)
    in_maps = []
    for b in range(B):
        in_maps.append(host_layout(
            D, x[b], np.asarray(c)[b], np.asarray(w_ada)[0], np.asarray(b_ada)[0],
            np.asarray(norm_g)[0], np.asarray(w_in)[0], np.asarray(conv_w)[0],
            np.asarray(conv_b)[0], np.asarray(conv_ln_g)[0], np.asarray(conv_ln_b)[0],
            np.asarray(sg_ln_g)[0], np.asarray(sg_ln_b)[0], np.asarray(w_s)[0],
            np.asarray(b_s)[0], np.asarray(w_out)[0], final_g))
    res = run_bass_kernel_spmd(nc, in_maps, core_ids=list(range(B)))
    return np.stack([np.asarray(r["out"], dtype=np.float32) for r in res.results], axis=0)
```

```python
import contextlib
import numpy as np
import concourse.bass as bass
import concourse.mybir as mybir
from concourse.bass_utils import run_bass_kernel_spmd

F32 = mybir.dt.float32
BF16 = mybir.dt.bfloat16
AF = mybir.ActivationFunctionType
ALU = mybir.AluOpType
AX = mybir.AxisListType

EPS = 1e-6
CONVW = 31
TB = 512
HB = 16
SEM_WINDOW = 1500


class Sched:
    ENGS = ("pe", "act", "dve", "pool", "sp")

    def __init__(self):
        self.prog = {e: [] for e in self.ENGS}
        self.res = {}
        self.dma_count = {}

    def _deps(self, reads, writes):
        deps = []
        for k in reads:
            r = self.res.get(k)
            if r and r[0] is not None:
                deps.append((r[0], "raw"))
        for k in writes:
            r = self.res.get(k)
            if r:
                if r[0] is not None:
                    deps.append((r[0], "waw"))
                for t in r[1]:
                    deps.append((t, "war"))
        return deps

    def _commit(self, tok, reads, writes):
        for k in reads:
            self.res.setdefault(k, [None, []])[1].append(tok)
        for k in writes:
            self.res[k] = [tok, []]

    def op(self, eng, fn, reads=(), writes=()):
        deps = self._deps(reads, writes)
        tok = ("c", eng, len(self.prog[eng]))
        self.prog[eng].append(dict(kind="c", fn=fn, deps=deps, tok=tok))
        self._commit(tok, reads, writes)
        return tok

    def dma(self, eng, key, fn, reads=(), writes=(), n=1):
        deps = self._deps(reads, writes)
        c = self.dma_count.get(key, 0) + n
        self.dma_count[key] = c
        tok = ("d", key, c)
        self.prog[eng].append(dict(kind="d", fn=fn, deps=deps, tok=tok, key=key))
        self._commit(tok, reads, writes)
        return tok

    def fence(self, eng, keys):
        deps = self._deps(keys, keys)
        self.prog[eng].append(dict(kind="f", fn=None, deps=deps, tok=None))

    def emit(self, nc):
        needed = set()
        for e in self.ENGS:
            for o in self.prog[e]:
                for (t, ty) in o["deps"]:
                    if t[0] != "c":
                        continue
                    if t[1] == e and e in ("pe", "sp"):
                        continue
                    needed.add((t[1], t[2]))
        incno = {}
        nwin = {}
        for e in self.ENGS:
            cnt = 0
            for i, o in enumerate(self.prog[e]):
                if o["kind"] == "c" and (e, i) in needed:
                    cnt += 1
                    incno[(e, i)] = cnt
            nwin[e] = (cnt + SEM_WINDOW - 1) // SEM_WINDOW
        with contextlib.ExitStack() as st:
            csem = {}
            for e in self.ENGS:
                for w in range(nwin[e]):
                    csem[(e, w)] = st.enter_context(nc.semaphore(f"c_{e}_{w}"))
            dsem = {}
            for i, k in enumerate(self.dma_count):
                dsem[k] = st.enter_context(nc.semaphore(f"d_{i}"))
            block = st.enter_context(nc.Block())

            def body(engobj, e):
                waited_c = {}
                waited_d = {}
                for i, o in enumerate(self.prog[e]):
                    wc = {}
                    wd = {}
                    for (t, ty) in o["deps"]:
                        if t[0] == "c":
                            if t[1] == e and e in ("pe", "sp"):
                                continue
                            n = incno[(t[1], t[2])]
                            if waited_c.get(t[1], 0) >= n:
                                continue
                            wc[t[1]] = max(wc.get(t[1], 0), n)
                        else:
                            if waited_d.get(t[1], 0) >= t[2]:
                                continue
                            wd[t[1]] = max(wd.get(t[1], 0), t[2])
                    for e2, n in wc.items():
                        waited_c[e2] = n
                        w = (n - 1) // SEM_WINDOW
                        engobj.wait_ge(csem[(e2, w)], (n - 1) % SEM_WINDOW + 1)
                    for k, n in wd.items():
                        waited_d[k] = n
                        engobj.wait_ge(dsem[k], 16 * n)
                    if o["kind"] == "c":
                        ins = o["fn"](engobj)
                        n = incno.get((e, i))
                        if n is not None:
                            w = (n - 1) // SEM_WINDOW
                            ins.then_inc(csem[(e, w)], 1)
                    elif o["kind"] == "d":
                        o["fn"](engobj, dsem[o["key"]])

            block.tensor(lambda eo: body(eo, "pe"))
            block.scalar(lambda eo: body(eo, "act"))
            block.vector(lambda eo: body(eo, "dve"))
            block.gpsimd(lambda eo: body(eo, "pool"))
            block.sync(lambda eo: body(eo, "sp"))


class PsumAlloc:
    def __init__(self, n):
        self.n = n
        self.p = 0

    def get(self, k=1):
        if self.p % k:
            self.p += k - self.p % k
        if self.p + k > self.n:
            self.p = 0
        b = self.p
        self.p += k
        return b


def build_program(D, S, w_scratch=True):
    KD = D // 128
    KC = KD
    H = KD
    NT = TB // 128
    NB = S // TB
    CW = min(512, D)
    UPS = D // CW
    CWP = CW // 2
    NPU = D // CWP
    NSLOT = 3
    NUNITS = 2 * NPU + 2 * UPS
    GW = TB + 2 * HB
    E6 = 6 * D
    assert S % TB == 0 and D % 128 == 0

    nc = bass.Bass("TRN2", target_bir_lowering=False)
    x_d = nc.dram_tensor("x", [S, D], F32, kind="ExternalInput")
    c_d = nc.dram_tensor("c", [128, KD], F32, kind="ExternalInput")
    wada_d = nc.dram_tensor("w_ada", [D, 3 * D], F32, kind="ExternalInput")
    bada_d = nc.dram_tensor("b_ada", [1, 3 * D], F32, kind="ExternalInput")
    vecs_d = nc.dram_tensor("vecs", [128, 6, KD], F32, kind="ExternalInput")
    win_d = nc.dram_tensor("w_in", [D, E6], F32, kind="ExternalInput")
    convw_d = nc.dram_tensor("convw3", [128, KC * 4 * 36], F32, kind="ExternalInput")
    wsT_d = nc.dram_tensor("w_sT", [128, H, 128], F32, kind="ExternalInput")
    bs_d = nc.dram_tensor("b_s", [1, H * 128], F32, kind="ExternalInput")
    wout_d = nc.dram_tensor("w_out", [2 * D, D], F32, kind="ExternalInput")
    fg_d = nc.dram_tensor("final_g", [1, D], F32, kind="ExternalInput")
    out_d = nc.dram_tensor("out", [S, D], F32, kind="ExternalOutput")
    wbf_d = nc.dram_tensor("wbf_scratch", [NUNITS, 128, KD, CW], BF16)

    x_ap, out_ap = x_d.ap(), out_d.ap()
    win_r = win_d.ap().rearrange("(j p) n -> p j n", p=128)
    wout_r = wout_d.ap().rearrange("(j p) n -> p j n", p=128)
    wada_r = wada_d.ap().rearrange("(j p) n -> p j n", p=128)
    wbf_ap = wbf_d.ap()

    def bcast_row(handle, off, n, parts=128):
        return bass.AP(handle, off, [[0, parts], [1, n]])

    S_ = Sched()
    ps_alloc = PsumAlloc(6)
    MEAN_BANK, EX2_BANK = 6, 7

    with contextlib.ExitStack() as st:
        def sb(name, shape, dt):
            return st.enter_context(nc.sbuf_tensor("sb_" + name, shape, dt))

        ps = st.enter_context(nc.psum_tensor("ps", [128, 8, 512], F32))

        xt = sb("xt", [128, 2, D], F32)
        ssx = sb("ssx", [128, 2], F32)
        rsx = sb("rsx", [128, 2], F32)
        xs = sb("xs", [128, NT, D], BF16)
        hT = sb("hT", [128, 2, KD, TB], BF16)
        wst = sb("wst", [128, NSLOT, KD, CW], BF16)
        G = sb("G", [128, 2, KC, GW], BF16)
        th = sb("th", [128, 2, TB], BF16)
        sga = sb("sga", [128, KC, TB], BF16)
        sgb = sb("sgb", [128, 2, TB], BF16)
        ub = sb("ub", [128, KC, TB], BF16)
        vst = sb("vst", [128, 2, 2, 6], F32)
        vmv = sb("vmv", [128, 2, 2], F32)
        vrs = sb("vrs", [128, 2], F32)
        vnm = sb("vnm", [128, 2], F32)
        vhat = sb("vhat", [128, NT, D], BF16)
        convb = sb("convb", [128, KC, TB], BF16)
        convsq = sb("convsq", [128, 3, TB], BF16)
        mean_sb = sb("mean_sb", [128, TB], F32)
        var_sb = sb("var_sb", [128, TB], F32)
        nmr_sb = sb("nmr_sb", [128, TB], F32)
        ntmp = sb("ntmp", [128, 2, TB], F32)
        sact = sb("sact", [128, 2, TB], BF16)
        junk = sact[:].rearrange("p a b -> p (a b)")[:, 0:D]
        tt = sb("tt", [128, 2, TB], BF16)
        xres = sb("xres", [128, NT, D], F32)
        ss2 = sb("ss2", [128, NT], F32)
        rs2 = sb("rs2", [128, NT], F32)
        LSLOTS = 6
        Lw = sb("Lw", [128, LSLOTS, 9, 32, 4], BF16)
        maskP = sb("maskP", [128, 9, 32, 4], BF16)
        W3b = sb("W3b", [128, KC, 4, 36], BF16)
        identP = sb("identP", [128, 32], BF16)
        TP = GW // 4
        Rsb = sb("Rsb", [128, 2, 4, TP], BF16)
        Csb = sb("Csb", [128, 2, TB], BF16)
        wout = sb("wout", [128, 2 * KC, D], BF16)
        ident_b = sb("ident_b", [128, 128], BF16)
        ones_b = sb("ones_b", [128, 128], BF16)
        ones_f = sb("ones_f", [128, 128], F32)
        negh = sb("negh", [128, 1], F32)
        fgmat = sb("fgmat", [128, D], F32)
        Bm = sb("Bm", [128, H, 128], F32)
        wsT = sb("wsT", [128, H, 128], BF16)
        vecs = sb("vecs", [128, 6, KD], F32)
        Gs = sb("Gs", [128, KD], F32)
        shiftv = sb("shiftv", [128, KD], F32)
        cact = sb("cact", [128, KD], F32)
        identrep = var_sb[:].rearrange("p (a b) -> p a b", b=128)
        crep = nmr_sb[:, 0:256].rearrange("p (a b) -> p a b", b=128)
        stg = ntmp
        exttmp = mean_sb

        V_NORM_G, V_CONV_B, V_CLN_G, V_CLN_B, V_SLN_G, V_SLN_B = range(6)

        def bank(b, n=512):
            return ps[:, b, 0:n]

        def bank_bf(b):
            return ps[:, b, :].bitcast(BF16)

        def dma1(eng, key, out, in_, reads=(), writes=()):
            def fn(e, sem, out=out, in_=in_):
                e.dma_start(out=out, in_=in_).then_inc(sem, 16)
            return S_.dma(eng, key, fn, reads=reads, writes=writes)

        dma1("sp", "ld_c", cact[:], c_d.ap(), writes=["cact"])
        dma1("sp", "ld_vecs", vecs[:], vecs_d.ap(), writes=["vecs"])
        w3stg = ub[:].rearrange("p a b -> p (a b)").bitcast(F32)
        dma1("sp", "ld_convw", w3stg[:, 0:KC * 144], convw_d.ap(), writes=[("ub", c) for c in range(KC)])
        dma1("sp", "ld_fg", fgmat[:], bcast_row(fg_d, 0, D), writes=["fgmat"])

        S_.op("pool", lambda e: e.memset(identrep[:], 0.0), writes=["var_sb"])
        for r in range(4):
            S_.op("pool", lambda e, r=r: e.affine_select(
                out=identrep[:, r, :], in_=identrep[:, r, :], compare_op=ALU.not_equal,
                fill=1.0, base=0, pattern=[[-1, 128]], channel_multiplier=1),
                reads=["var_sb"], writes=["var_sb"])
        S_.op("pool", lambda e: e.tensor_copy(out=ident_b[:], in_=identrep[:, 0, :]),
              reads=["var_sb"], writes=["ident_b"])
        S_.op("pool", lambda e: e.memset(ones_b[:], 1.0 / D), writes=["ones_b"])
        S_.op("pool", lambda e: e.memset(ones_f[:], 1.0), writes=["ones_f"])
        S_.op("pool", lambda e: e.memset(negh[:], -0.5), writes=["negh"])
        S_.op("dve", lambda e: e.tensor_scalar(
            out=W3b[:].rearrange("p a b c -> p (a b c)"), in0=w3stg[:, 0:KC * 144], scalar1=0.5,
            scalar2=None, op0=ALU.mult),
            reads=[("ub", c) for c in range(KC)], writes=["W3b"])
        S_.op("dve", lambda e: e.tensor_tensor(out=identP[:], in0=ident_b[:, 0:32], in1=ident_b[:, 32:64],
                                               op=ALU.add), reads=["ident_b"], writes=["identP"])
        S_.op("dve", lambda e: e.tensor_tensor(out=identP[:], in0=identP[:], in1=ident_b[:, 64:96],
                                               op=ALU.add), reads=["ident_b", "identP"], writes=["identP"])
        S_.op("dve", lambda e: e.tensor_tensor(out=identP[:], in0=identP[:], in1=ident_b[:, 96:128],
                                               op=ALU.add), reads=["ident_b", "identP"], writes=["identP"])
        S_.op("dve", lambda e: e.tensor_copy(
            out=maskP[:], in_=identP[:].unsqueeze(1).unsqueeze(3).to_broadcast([128, 9, 32, 4])),
            reads=["identP"], writes=["maskP"])
        S_.op("act", lambda e: e.activation(out=cact[:], in_=cact[:], func=AF.Silu),
              reads=["cact"], writes=["cact"])

        NMB = (3 * D + 511) // 512
        assert NMB <= 6
        stg_i = [0]

        def next_stg():
            s = stg_i[0] % 2
            stg_i[0] += 1
            return s

        def mwid(nb):
            return min(512, 3 * D - nb * 512)

        f32view = lambda t, pat: t[:].rearrange(pat).bitcast(F32)
        big_stg = [
            (f32view(convb, "p a b -> p (a b)"), [("convb", c) for c in range(KC)]),
            (f32view(sga, "p a b -> p (a b)"), [("sga", c) for c in range(KC)]),
            (f32view(ub, "p a b -> p (a b)"), [("ub", c) for c in range(KC)]),
            (f32view(vhat, "p a b -> p (a b)"), [("vhat", q) for q in range(NT)]),
        ]
        CREP_SL = NT - 1
        crep_all = xres[:, CREP_SL, :].rearrange("p (a b) -> p a b", b=128)
        crep_keys = [("xres", CREP_SL)]
        NSTG = NT - 1
        brow = G[:].rearrange("p a b c -> p (a b c)").bitcast(F32)
        brow_keys = [(k_, b_, c_) for k_ in ("G", "Gl", "Gr") for b_ in range(2) for c_ in range(KC)]
        dma1("sp", "ld_bada", brow[0:1, 0:2 * D], bada_d.ap()[:, 0:2 * D], writes=brow_keys)
        browg = Lw[:].rearrange("p a b c d -> p (a b c d)").bitcast(F32)
        browg_keys = [("Lw", sl_) for sl_ in range(LSLOTS)]
        dma1("sp", "ld_badag", browg[0:1, 0:D], bada_d.ap()[:, 2 * D:3 * D], writes=browg_keys)

        def prologue_part1():
            for j in range(KD):
                S_.op("dve", lambda e, j=j: e.tensor_copy(
                    out=crep_all[:, j, :], in_=cact[:, j:j + 1].to_broadcast([128, 128])),
                    reads=["cact"], writes=crep_keys)
            nb1 = (2 * D + 511) // 512
            for j in range(KD):
                stgv, skeys = big_stg[j % 4]
                dma1("sp", ("ld_bigstg", j % 4), stgv[:, 0:2 * D], wada_r[:, j, 0:2 * D], writes=skeys)
                for nb in range(nb1):
                    mw = min(512, 2 * D - nb * 512)
                    S_.op("pe", lambda e, j=j, nb=nb, mw=mw, stgv=stgv: e.matmul(
                        bank(nb, mw), lhsT=crep_all[:, j, :], rhs=stgv[:, nb * 512:nb * 512 + mw],
                        start=(j == 0), stop=False),
                        reads=crep_keys + skeys, writes=[("ps", nb)])
            for nb in range(nb1):
                mw = min(512, 2 * D - nb * 512)
                S_.op("pe", lambda e, nb=nb, mw=mw: e.matmul(
                    bank(nb, mw), lhsT=ones_f[0:1, :], rhs=brow[0:1, nb * 512:nb * 512 + mw],
                    start=False, stop=True),
                    reads=["ones_f"] + brow_keys, writes=[("ps", nb)])

        def gate_cols(n0, n):
            b_, o_ = divmod(n0, 512)
            assert o_ + n <= 512
            return ps[:, MEAN_BANK + b_, o_:o_ + n], ("ps", MEAN_BANK + b_)

        stg_loads = []
        stg_issued = [0]

        def stg_issue(upto):
            while stg_issued[0] < min(upto, len(stg_loads)):
                stg_loads[stg_issued[0]][1]()
                stg_issued[0] += 1

        def stg_register(src_ap):
            k = len(stg_loads)
            sl = k % NSTG
            stg_loads.append((sl, lambda sl=sl, src_ap=src_ap: dma1(
                "sp", ("ld_xres", sl), xres[:, sl, :], src_ap, writes=[("xres", sl)])))
            return k, sl

        def gate_task(j):
            k, sl = stg_register(wada_r[:, j, 2 * D:3 * D])

            def run():
                stg_issue(k + 1)
                for n0 in range(0, D, 512):
                    mw = min(512, D - n0)
                    dst, dkey = gate_cols(n0, mw)
                    S_.op("pe", lambda e, sl=sl, n0=n0, mw=mw, dst=dst: e.matmul(
                        dst, lhsT=crep_all[:, j, :], rhs=xres[:, sl, n0:n0 + mw],
                        start=(j == 0), stop=False),
                        reads=crep_keys + [("xres", sl)], writes=[dkey])
                stg_issue(k + 1 + NSTG)
            return run

        def gate_bias_task():
            for n0 in range(0, D, 512):
                mw = min(512, D - n0)
                dst, dkey = gate_cols(n0, mw)
                S_.op("pe", lambda e, n0=n0, mw=mw, dst=dst: e.matmul(
                    dst, lhsT=ones_f[0:1, :],
                    rhs=browg[0:1, n0:n0 + mw], start=False, stop=True),
                    reads=["ones_f"] + browg_keys, writes=[dkey])

        def mod_cols(c0, n):
            b, o = divmod(c0, 512)
            assert o + n <= 512, (c0, n)
            return ps[:, b, o:o + n], ("ps", b)

        def extract(dst, c0, dst_key):
            for j0 in range(0, KD, 4):
                nj = min(4, KD - j0)
                src, key = mod_cols(c0 + j0 * 128, nj * 128)
                S_.op("dve", lambda e, src=src, nj=nj: e.tensor_tensor(
                    out=exttmp[:, 0:nj * 128], in0=src,
                    in1=identrep[:, 0:nj, :].rearrange("p a b -> p (a b)"), op=ALU.mult),
                    reads=[key, "var_sb"], writes=["mean_sb"])
                S_.op("dve", lambda e, j0=j0, nj=nj: e.tensor_reduce(
                    out=dst[:, j0:j0 + nj],
                    in_=exttmp[:, 0:nj * 128].rearrange("p (a b) -> p a b", b=128),
                    axis=AX.X, op=ALU.add),
                    reads=["mean_sb"], writes=[dst_key])

        def prologue_extract():
            extract(shiftv, 0, "shiftv")
            extract(Gs, D, "Gs")
            S_.op("dve", lambda e: e.scalar_tensor_tensor(
                out=Gs[:], in0=Gs[:], scalar=1.0, in1=vecs[:, V_NORM_G, :], op0=ALU.add, op1=ALU.mult),
                reads=["Gs", "vecs"], writes=["Gs"])

        def fold_task(j):
            k, sl = stg_register(wout_r[:, j, :])

            def run():
                stg_issue(k + 1)
                for n0 in range(0, D, 512):
                    nn = min(512, D - n0)
                    src, key = gate_cols(n0, nn)
                    S_.op("dve", lambda e, n0=n0, nn=nn, sl=sl, src=src: e.tensor_tensor(
                        out=wout[:, j, n0:n0 + nn], in0=xres[:, sl, n0:n0 + nn], in1=src, op=ALU.mult),
                        reads=[("xres", sl), key], writes=[("wout", j)])
                stg_issue(k + 1 + NSTG)
            return run

        def spatial_setup_task(h0):
            def run():
                nh = min(4, H - h0)
                s = next_stg()
                dma1("sp", ("ld_ntmp", s), stg[:, s, 0:nh * 128].rearrange("p (a b) -> p a b", b=128),
                     wsT_d.ap()[:, h0:h0 + nh, :], writes=[("ntmp", s)])
                S_.op("act", lambda e, s=s: e.activation(
                    out=wsT[:, h0:h0 + nh, :].rearrange("p a b -> p (a b)"),
                    in_=stg[:, s, 0:nh * 128], func=AF.Copy),
                    reads=[("ntmp", s)], writes=["wsT"])
                S_.op("pe", lambda e, s=s: e.matmul(
                    bank(MEAN_BANK, nh * 128), lhsT=ones_f[:], rhs=stg[:, s, 0:nh * 128],
                    start=True, stop=True),
                    reads=["ones_f", ("ntmp", s)], writes=[("ps", MEAN_BANK)])
                s2 = next_stg()
                dma1("sp", ("ld_ntmp", s2), stg[:, s2, 0:nh * 128],
                     bcast_row(bs_d, h0 * 128, nh * 128), writes=[("ntmp", s2)])
                for hh in range(nh):
                    h = h0 + hh
                    S_.op("dve", lambda e, h=h, hh=hh, s2=s2: e.scalar_tensor_tensor(
                        out=Bm[:, h, :], in0=ps[:, MEAN_BANK, hh * 128:(hh + 1) * 128],
                        scalar=vecs[:, V_SLN_B, h:h + 1], in1=stg[:, s2, hh * 128:(hh + 1) * 128],
                        op0=ALU.mult, op1=ALU.add),
                        reads=[("ps", MEAN_BANK), "vecs", ("ntmp", s2)], writes=["Bm"])
            return run

        bg_tasks = []
        for h0 in range(0, H, 4):
            bg_tasks.append(spatial_setup_task(h0))
        for j in range(KD):
            bg_tasks.append(gate_task(j))
        bg_tasks.append(gate_bias_task)
        for j in range(2 * KC):
            bg_tasks.append(fold_task(j))

        def bg_step(n=1):
            for _ in range(n):
                if bg_tasks:
                    bg_tasks.pop(0)()

        unit_ctr = [0]

        prefetched = {}

        def load_unit(uid, pieces, blk, prefetch=False):
            if (uid, blk) in prefetched:
                return prefetched.pop((uid, blk))
            slot = unit_ctr[0] % NSLOT
            unit_ctr[0] += 1
            wkey = ("wst", slot)
            if blk == 0 or not w_scratch:
                hk = KD // 2 if KD >= 2 else KD

                def fn(e, sem, slot=slot, pieces=pieces, hk=hk):
                    o = 0
                    for (c0, ncol) in pieces:
                        for j0 in range(0, KD, hk):
                            e.dma_start(out=wst[:, slot, j0:j0 + hk, o:o + ncol],
                                        in_=win_r[:, j0:j0 + hk, c0:c0 + ncol]).then_inc(sem, 16)
                        o += ncol
                S_.dma("pool", ("ld_wst_sw", slot), fn, writes=[wkey],
                       n=len(pieces) * ((KD + hk - 1) // hk))
                if w_scratch:
                    dma1("sp", ("st_wbf", slot), wbf_ap[uid], wst[:, slot, :, :],
                         reads=[wkey], writes=[("wbf", uid)])
            else:
                dma1("sp", ("ld_wst", slot), wst[:, slot, :, :], wbf_ap[uid],
                     reads=[("wbf", uid)], writes=[wkey])
            if prefetch:
                prefetched[(uid, blk)] = slot
            return slot

        def load_pair(uid0, slab_a, slab_b, u, blk):
            return load_unit(uid0 + u, [(slab_a * D + u * CWP, CWP), (slab_b * D + u * CWP, CWP)], blk)

        def load_full(uid0, slab, u, blk):
            return load_unit(uid0 + u, [(slab * D + u * CW, CW)], blk)

        UID_CONV, UID_GATE, UID_UB, UID_V = 0, NPU, NPU + UPS, 2 * NPU + UPS

        def stage_Xpre(i, tiles=None):
            for q in (range(NT) if tiles is None else tiles):
                t0 = i * TB + q * 128
                sl = q % 2
                dma1("sp", ("ld_xt", sl), xt[:, sl, :], x_ap[t0:t0 + 128, :], writes=[("xt", sl)])
                S_.op("act", lambda e, sl=sl, q=q: e.activation(
                    out=xs[:, q, :], in_=xt[:, sl, :], func=AF.Square, accum_out=ssx[:, sl:sl + 1]),
                    reads=[("xt", sl)], writes=[("xs", q), ("ssx", sl)])
                S_.op("dve", lambda e, sl=sl: e.tensor_scalar(
                    out=rsx[:, sl:sl + 1], in0=ssx[:, sl:sl + 1], scalar1=1.0 / D, scalar2=EPS,
                    op0=ALU.mult, op1=ALU.add), reads=[("ssx", sl)], writes=[("rsx", sl)])
                S_.op("pool", lambda e, sl=sl: e.tensor_tensor(
                    out=rsx[:, sl:sl + 1], in0=rsx[:, sl:sl + 1], in1=negh[:], op=ALU.pow),
                    reads=[("rsx", sl), "negh"], writes=[("rsx", sl)])
                S_.op("dve", lambda e, sl=sl, q=q: e.tensor_scalar(
                    out=xs[:, q, :], in0=xt[:, sl, :], scalar1=rsx[:, sl:sl + 1], scalar2=None,
                    op0=ALU.mult), reads=[("xt", sl), ("rsx", sl)], writes=[("xs", q)])

        def stage_XT(i):
            nb4 = max(1, KD // 2)
            pb = ps_alloc.get(4 if nb4 > 2 else nb4)
            for q in range(NT):
                for j in range(KD):
                    b = pb + j // 2
                    col = (j % 2) * 512 + q * 128
                    S_.op("pe", lambda e, q=q, j=j, b=b, col=col: e.transpose(
                        bank_bf(b)[:, col:col + 128], xs[:, q, j * 128:(j + 1) * 128], ident_b[:]),
                        reads=[("xs", q), "ident_b"], writes=[("ps", b)])
            for j in range(KD):
                b = pb + j // 2
                col = (j % 2) * 512
                if (j // 2) % 2 == 0:
                    S_.op("dve", lambda e, j=j, b=b, col=col: e.tensor_scalar(
                        out=hT[:, i % 2, j, :], in0=bank_bf(b)[:, col:col + 512],
                        scalar1=Gs[:, j:j + 1], scalar2=shiftv[:, j:j + 1], op0=ALU.mult, op1=ALU.add),
                        reads=[("ps", b), "Gs", "shiftv"], writes=[("hT", i % 2, j)])
                else:
                    S_.op("act", lambda e, j=j, b=b, col=col: e.activation(
                        out=hT[:, i % 2, j, :], in_=bank_bf(b)[:, col:col + 512], func=AF.Identity,
                        bias=shiftv[:, j:j + 1], scale=Gs[:, j:j + 1]),
                        reads=[("ps", b), "Gs", "shiftv"], writes=[("hT", i % 2, j)])

        def zfm(slot, cc, b, hb):
            for j in range(KD):
                S_.op("pe", lambda e, j=j, slot=slot, cc=cc, b=b, hb=hb: e.matmul(
                    bank(b, TB), lhsT=wst[:, slot, j, cc * 128:(cc + 1) * 128], rhs=hT[:, hb, j, :],
                    start=(j == 0), stop=(j == KD - 1)),
                    reads=[("wst", slot), ("hT", hb, j)], writes=[("ps", b)])

        CPF = CW // 128
        CPP = CWP // 128

        zc_slots = {}

        def zc_chunk(i, c):
            gb = i % 2
            u, cc = divmod(c, CPP)
            if cc == 0:
                zc_slots[(i, u)] = load_pair(UID_CONV, 0, 1, u, i)
            sp_ = zc_slots[(i, u)]
            ba = ps_alloc.get()
            zfm(sp_, cc, ba, i % 2)
            bg = ps_alloc.get()
            zfm(sp_, CPP + cc, bg, i % 2)
            ts_ = c % 2
            S_.op("act", lambda e, bg=bg, ts_=ts_: e.activation(
                out=th[:, ts_, :], in_=bank(bg, TB), func=AF.Tanh, scale=0.5),
                reads=[("ps", bg)], writes=[("th", ts_)])
            S_.op("dve", lambda e, ba=ba, ts_=ts_, c=c, gb=gb: e.scalar_tensor_tensor(
                out=G[:, gb, c, HB:HB + TB], in0=th[:, ts_, :], scalar=1.0, in1=bank(ba, TB),
                op0=ALU.add, op1=ALU.mult),
                reads=[("th", ts_), ("ps", ba)], writes=[("G", gb, c)])
            bg_step()
            if i > 0:
                ob = (i - 1) % 2
                S_.op("dve", lambda e, gb=gb, ob=ob, c=c: e.tensor_copy(
                    out=G[:, ob, c, HB + TB:HB + TB + HB], in_=G[:, gb, c, HB:HB + HB]),
                    reads=[("G", gb, c)], writes=[("Gr", ob, c)])
                S_.op("dve", lambda e, gb=gb, ob=ob, c=c: e.tensor_copy(
                    out=G[:, gb, c, 0:HB], in_=G[:, ob, c, TB:TB + HB]),
                    reads=[("G", ob, c)], writes=[("Gl", gb, c)])

        def zc_edges(i):
            gb = i % 2
            if i == 0:
                S_.op("pool", lambda e, gb=gb: e.memset(G[:, gb, :, 0:HB], 0.0),
                      writes=[("Gl", gb, c) for c in range(KC)])
            if i == NB - 1:
                S_.op("pool", lambda e, gb=gb: e.memset(G[:, gb, :, HB + TB:HB + TB + HB], 0.0),
                      writes=[("Gr", gb, c) for c in range(KC)])

        def stage_Zc(i):
            zc_edges(i)
            for c in range(KC):
                zc_chunk(i, c)

        def ln_MA(c):
            ts_ = c % 2
            S_.op("dve", lambda e, c=c, ts_=ts_: e.tensor_tensor(
                out=ntmp[:, ts_, :], in0=convb[:, c, :], in1=var_sb[:], op=ALU.mult),
                reads=[("convb", c), "var_sb"], writes=[("ntmp", ts_)])
            S_.op("dve", lambda e, ts_=ts_: e.tensor_tensor(
                out=ntmp[:, ts_, :], in0=ntmp[:, ts_, :], in1=nmr_sb[:], op=ALU.add),
                reads=[("ntmp", ts_), "nmr_sb"], writes=[("ntmp", ts_)])
            S_.op("act", lambda e, c=c, ts_=ts_: e.activation(
                out=sact[:, ts_, :], in_=ntmp[:, ts_, :], func=AF.Silu,
                bias=vecs[:, V_CLN_B, c:c + 1], scale=vecs[:, V_CLN_G, c:c + 1]),
                reads=[("ntmp", ts_), "vecs"], writes=[("sact", ts_)])

        def ln_F(c):
            ts_ = c % 2
            S_.op("dve", lambda e, c=c, ts_=ts_: e.tensor_tensor(
                out=convb[:, c, :], in0=sact[:, ts_, :], in1=sga[:, c, :], op=ALU.mult),
                reads=[("sact", ts_), ("sga", c)], writes=[("convb", c)])

        def stage_lnapply_gate(i_ln, i_gate):
            slots = {}
            for step in range(KC + 1):
                if i_ln is not None and step < KC:
                    ln_MA(step)
                c = step - 1
                if c < 0:
                    continue
                if i_ln is not None:
                    ln_F(c)
                if i_gate is not None:
                    u, cc = divmod(c, CPF)
                    if cc == 0:
                        slots[u] = load_full(UID_GATE, 2, u, i_gate)
                    b = ps_alloc.get()
                    zfm(slots[u], cc, b, i_gate % 2)
                    S_.op("act", lambda e, b=b, c=c: e.activation(
                        out=sga[:, c, :], in_=bank(b, TB), func=AF.Silu),
                        reads=[("ps", b)], writes=[("sga", c)])
                    bg_step()

        def stage_Zr_ub(i, with_spatial=False):
            for u in range(NPU):
                sp_ = load_pair(UID_UB, 3, 5, u, i)
                for cc in range(CPP):
                    c = u * CPP + cc
                    if with_spatial and c >= 1:
                        stage_spatial(i, [c - 1])
                    bu = ps_alloc.get()
                    zfm(sp_, cc, bu, i % 2)
                    bb = ps_alloc.get()
                    zfm(sp_, CPP + cc, bb, i % 2)
                    ts_ = c % 2
                    S_.op("act", lambda e, bb=bb, ts_=ts_: e.activation(
                        out=sgb[:, ts_, :], in_=bank(bb, TB), func=AF.Silu),
                        reads=[("ps", bb)], writes=[("sgb", ts_)])
                    S_.op("dve", lambda e, bu=bu, ts_=ts_, c=c: e.tensor_tensor(
                        out=ub[:, c, :], in0=bank(bu, TB), in1=sgb[:, ts_, :], op=ALU.mult),
                        reads=[("ps", bu), ("sgb", ts_)], writes=[("ub", c)])
                    bg_step()
        def stage_Zr_v(i):
            assert UPS <= 2
            vslots = [load_full(UID_V, 4, u, i) for u in range(UPS)]
            for q in range(NT):
                pb = ps_alloc.get(UPS)
                sl = q % 2
                for u in range(UPS):
                    for j in range(KD):
                        S_.op("pe", lambda e, j=j, q=q, u=u, pb=pb: e.matmul(
                            bank(pb + u, CW), lhsT=hT[:, i % 2, j, q * 128:(q + 1) * 128],
                            rhs=wst[:, vslots[u], j, :], start=(j == 0), stop=(j == KD - 1)),
                            reads=[("wst", vslots[u]), ("hT", i % 2, j)], writes=[("ps", pb + u)])
                    S_.op("dve", lambda e, u=u, pb=pb, sl=sl: e.bn_stats(
                        out=vst[:, sl, u, :], in_=bank(pb + u, CW)),
                        reads=[("ps", pb + u)], writes=[("vst", sl, u)])
                S_.op("dve", lambda e, sl=sl: e.bn_aggr(
                    out=vmv[:, sl, :], in_=vst[:, sl, 0:UPS, :].rearrange("p a b -> p (a b)")),
                    reads=[("vst", sl, u) for u in range(UPS)], writes=[("vmv", sl)])
                S_.op("dve", lambda e, sl=sl: e.tensor_scalar(
                    out=vrs[:, sl:sl + 1], in0=vmv[:, sl, 1:2], scalar1=EPS, scalar2=None,
                    op0=ALU.add), reads=[("vmv", sl)], writes=[("vrs", sl)])
                S_.op("pool", lambda e, sl=sl: e.tensor_tensor(
                    out=vrs[:, sl:sl + 1], in0=vrs[:, sl:sl + 1], in1=negh[:], op=ALU.pow),
                    reads=[("vrs", sl), "negh"], writes=[("vrs", sl)])
                S_.op("dve", lambda e, sl=sl: e.scalar_tensor_tensor(
                    out=vnm[:, sl:sl + 1], in0=vmv[:, sl, 0:1], scalar=-1.0, in1=vrs[:, sl:sl + 1],
                    op0=ALU.mult, op1=ALU.mult), reads=[("vmv", sl), ("vrs", sl)], writes=[("vnm", sl)])
                for u in range(UPS):
                    S_.op("act", lambda e, u=u, pb=pb, sl=sl, q=q: e.activation(
                        out=vhat[:, q, u * CW:(u + 1) * CW], in_=bank(pb + u, CW), func=AF.Identity,
                        bias=vnm[:, sl:sl + 1], scale=vrs[:, sl:sl + 1]),
                        reads=[("ps", pb + u), ("vnm", sl), ("vrs", sl)], writes=[("vhat", q)])
                bg_step()

        lw_ctr = [0]

        def lw_gen(c, g):
            slot = lw_ctr[0] % LSLOTS
            lw_ctr[0] += 1
            S_.op("dve", lambda e, c=c, g=g, slot=slot: e.tensor_tensor(
                out=Lw[:, slot, :, :, :], in0=maskP[:],
                in1=W3b[:, c, g, :].rearrange("p (j r) -> p j r", r=4).unsqueeze(2).to_broadcast([128, 9, 32, 4]),
                op=ALU.mult),
                reads=["maskP", "W3b"], writes=[("Lw", slot)])
            return slot

        def stage_conv(i, xpre_blk=None, i_zc=None):
            gb = i % 2
            if i_zc is not None:
                zc_edges(i_zc)
            gkeys = lambda c: [("G", gb, c), ("Gl", gb, c), ("Gr", gb, c)]

            def stats_mm(c):
                sq = c % 3
                S_.op("pe", lambda e, c=c: e.matmul(
                    bank(MEAN_BANK, TB), lhsT=ones_b[:], rhs=convb[:, c, :],
                    start=(c == 0), stop=(c == KC - 1)),
                    reads=["ones_b", ("convb", c)], writes=[("ps", MEAN_BANK)])
                S_.op("pe", lambda e, c=c, sq=sq: e.matmul(
                    bank(EX2_BANK, TB), lhsT=ones_b[:], rhs=convsq[:, sq, :],
                    start=(c == 0), stop=(c == KC - 1)),
                    reads=["ones_b", ("convsq", sq)], writes=[("ps", EX2_BANK)])

            def conv_in(c):
                rs = c % 2
                pb = ps_alloc.get(2)
                gview = G[:, gb, c, :].rearrange("p (t s) -> p s t", s=4)
                for g in range(4):
                    bk, col0 = pb + g // 2, (g % 2) * TP
                    for s_ in range(4):
                        S_.op("pe", lambda e, g=g, s_=s_, bk=bk, col0=col0: e.matmul(
                            ps[32 * s_:32 * s_ + 32, bk, col0:col0 + TP],
                            lhsT=ident_b[:, 32 * g:32 * g + 32], rhs=gview[:, s_, :],
                            start=True, stop=True, tile_position=(0, 32 * s_)),
                            reads=["ident_b"] + gkeys(c), writes=[("ps", bk)])
                for h in range(2):
                    S_.op("act", lambda e, h=h, rs=rs, pb=pb: e.activation(
                        out=Rsb[:, rs, 2 * h:2 * h + 2, :],
                        in_=ps[:, pb + h, 0:2 * TP].rearrange("p (a b) -> p a b", b=TP), func=AF.Copy),
                        reads=[("ps", pb + h)], writes=[("Rsb", rs, h)])

            def conv_out(c):
                rs = c % 2
                sq = c % 3
                b2 = ps_alloc.get()
                for r in range(4):
                    for g in range(4):
                        S_.op("pe", lambda e, g=g, r=r, b2=b2, rs=rs: e.matmul(
                            ps[32 * g:32 * g + 32, b2, r * 128:(r + 1) * 128],
                            lhsT=ident_b[:].rearrange("p (c r) -> p r c", r=4)[:, r, :],
                            rhs=Csb[:, rs, g * 128:(g + 1) * 128],
                            start=True, stop=True, tile_position=(0, 32 * g)),
                            reads=["ident_b", ("Csb", rs)], writes=[("ps", b2)])
                src = ps[:, b2, :].rearrange("p (r t) -> p r t", r=4)
                S_.op("act", lambda e, c=c, src=src: e.activation(
                    out=convb[:, c, :].rearrange("p (t r) -> p r t", r=4), in_=src, func=AF.Identity,
                    bias=vecs[:, V_CONV_B, c:c + 1], scale=1.0),
                    reads=[("ps", b2), "vecs"], writes=[("convb", c)])
                S_.op("act", lambda e, c=c, sq=sq, src=src: e.activation(
                    out=convsq[:, sq, :].rearrange("p (t r) -> p r t", r=4), in_=src, func=AF.Square,
                    bias=vecs[:, V_CONV_B, c:c + 1], scale=1.0),
                    reads=[("ps", b2), "vecs"], writes=[("convsq", sq)])

            pend = {}
            gen_q = [(c, g) for c in range(KC) for g in range(4)]
            lw_slots = {}

            def gen_next(n):
                for _ in range(n):
                    if gen_q:
                        c_, g_ = gen_q.pop(0)
                        lw_slots[(c_, g_)] = lw_gen(c_, g_)

            gen_next(LSLOTS - 1)
            sk = 1 if i_zc is not None else 0
            for step in range(KC + 3 + sk):
                if step < KC and i_zc is not None:
                    zc_chunk(i_zc, step)
                c = step - sk
                if 0 <= c < KC:
                    conv_in(c)
                c = step - 1 - sk
                if 0 <= c < KC:
                    rs = c % 2
                    b = ps_alloc.get()
                    for g in range(4):
                        while (c, g) not in lw_slots:
                            gen_next(1)
                        sl_ = lw_slots[(c, g)]
                        for jj in range(9):
                            S_.op("pe", lambda e, g=g, jj=jj, b=b, rs=rs, sl_=sl_: e.matmul(
                                ps[:, b, g * 128:(g + 1) * 128], lhsT=Lw[:, sl_, jj, :, :].rearrange("p c r -> p (c r)"),
                                rhs=Rsb[:, rs, g, jj:jj + 128], start=(jj == 0), stop=(jj == 8)),
                                reads=[("Lw", sl_), ("Rsb", rs, g // 2)], writes=[("ps", b)])
                        gen_next(1)
                    S_.op("act", lambda e, b=b, rs=rs: e.activation(
                        out=Csb[:, rs, :], in_=bank(b, TB), func=AF.Copy),
                        reads=[("ps", b)], writes=[("Csb", rs)])
                    if xpre_blk is not None and c < NT:
                        tl = [c] if KC >= NT else list(range(c * NT // KC, (c + 1) * NT // KC))
                        stage_Xpre(xpre_blk, tl)
                c = step - 2 - sk
                if 0 <= c < KC:
                    conv_out(c)
                c = step - 3 - sk
                if 0 <= c < KC:
                    stats_mm(c)

            S_.op("act", lambda e: e.activation(out=mean_sb[:], in_=bank(MEAN_BANK, TB), func=AF.Copy),
                  reads=[("ps", MEAN_BANK)], writes=["mean_sb"])
            S_.op("dve", lambda e: e.scalar_tensor_tensor(
                out=nmr_sb[:], in0=mean_sb[:], scalar=-1.0, in1=mean_sb[:], op0=ALU.mult, op1=ALU.mult),
                reads=["mean_sb"], writes=["nmr_sb"])
            S_.op("dve", lambda e: e.scalar_tensor_tensor(
                out=var_sb[:], in0=bank(EX2_BANK, TB), scalar=EPS, in1=nmr_sb[:],
                op0=ALU.add, op1=ALU.add),
                reads=[("ps", EX2_BANK), "nmr_sb"], writes=["var_sb"])
            S_.op("act", lambda e: e.activation(out=var_sb[:], in_=var_sb[:], func=AF.Sqrt),
                  reads=["var_sb"], writes=["var_sb"])
            S_.op("dve", lambda e: e.reciprocal(out=var_sb[:], in_=var_sb[:]),
                  reads=["var_sb"], writes=["var_sb"])
            S_.op("dve", lambda e: e.scalar_tensor_tensor(
                out=nmr_sb[:], in0=mean_sb[:], scalar=-1.0, in1=var_sb[:], op0=ALU.mult, op1=ALU.mult),
                reads=["mean_sb", "var_sb"], writes=["nmr_sb"])

        def stage_spatial(i, heads=None):
            for h in (range(H) if heads is None else heads):
                b = ps_alloc.get()
                for q in range(NT):
                    S_.op("pe", lambda e, h=h, q=q, b=b: e.matmul(
                        ps[:, b, q * 128:(q + 1) * 128], lhsT=vhat[:, q, h * 128:(h + 1) * 128],
                        rhs=wsT[:, h, :], start=True, stop=True),
                        reads=[("vhat", q), "wsT"], writes=[("ps", b)])
                ts_ = h % 2
                S_.op("dve", lambda e, h=h, b=b, ts_=ts_: e.scalar_tensor_tensor(
                    out=tt[:, ts_, :].rearrange("p (a b) -> p a b", b=128),
                    in0=ps[:, b, :].rearrange("p (a b) -> p a b", b=128),
                    scalar=vecs[:, V_SLN_G, h:h + 1],
                    in1=Bm[:, h:h + 1, :].to_broadcast([128, NT, 128]),
                    op0=ALU.mult, op1=ALU.add),
                    reads=[("ps", b), "vecs", "Bm"], writes=[("tt", ts_)])
                S_.op("dve", lambda e, h=h, ts_=ts_: e.tensor_tensor(
                    out=ub[:, h, :], in0=tt[:, ts_, :], in1=ub[:, h, :], op=ALU.mult),
                    reads=[("tt", ts_), ("ub", h)], writes=[("ub", h)])
                bg_step()

        NO = (D + 511) // 512
        OW = min(512, D)

        def stage_out_pre(i):
            for q in range(NT):
                t0 = i * TB + q * 128
                dma1("sp", ("ld_xres", q), xres[:, q, :], x_ap[t0:t0 + 128, :], writes=[("xres", q)])

        def stage_out(i):
            for q in range(NT):
                t0 = i * TB + q * 128
                sl = q
                pb = ps_alloc.get(NO)
                for n in range(NO):
                    for j in range(2 * KC):
                        src = convb if j < KC else ub
                        jj = j % KC
                        key = ("convb", jj) if j < KC else ("ub", jj)
                        S_.op("pe", lambda e, src=src, jj=jj, j=j, q=q, n=n, pb=pb: e.matmul(
                            bank(pb + n, OW), lhsT=src[:, jj, q * 128:(q + 1) * 128],
                            rhs=wout[:, j, n * OW:(n + 1) * OW],
                            start=(j == 0), stop=(j == 2 * KC - 1)),
                            reads=[key, ("wout", j)], writes=[("ps", pb + n)])
                    S_.op("dve", lambda e, n=n, pb=pb, sl=sl: e.tensor_tensor(
                        out=xres[:, sl, n * OW:(n + 1) * OW], in0=bank(pb + n, OW),
                        in1=xres[:, sl, n * OW:(n + 1) * OW], op=ALU.add),
                        reads=[("ps", pb + n), ("xres", sl)], writes=[("xres", sl)])
                S_.op("act", lambda e, sl=sl: e.activation(
                    out=junk, in_=xres[:, sl, :], func=AF.Square, accum_out=ss2[:, sl:sl + 1]),
                    reads=[("xres", sl)], writes=[("sact", 0), ("sact", 1), ("ss2", sl)])
                S_.op("dve", lambda e, sl=sl: e.tensor_scalar(
                    out=rs2[:, sl:sl + 1], in0=ss2[:, sl:sl + 1], scalar1=1.0 / D, scalar2=EPS,
                    op0=ALU.mult, op1=ALU.add), reads=[("ss2", sl)], writes=[("rs2", sl)])
                S_.op("pool", lambda e, sl=sl: e.tensor_tensor(
                    out=rs2[:, sl:sl + 1], in0=rs2[:, sl:sl + 1], in1=negh[:], op=ALU.pow),
                    reads=[("rs2", sl), "negh"], writes=[("rs2", sl)])
                S_.op("dve", lambda e, sl=sl: e.scalar_tensor_tensor(
                    out=xres[:, sl, :], in0=xres[:, sl, :], scalar=rs2[:, sl:sl + 1], in1=fgmat[:],
                    op0=ALU.mult, op1=ALU.mult),
                    reads=[("xres", sl), ("rs2", sl), "fgmat"], writes=[("xres", sl)])
                dma1("act", ("st_out", sl), out_ap[t0:t0 + 128, :], xres[:, sl, :],
                     reads=[("xres", sl)], writes=[("out", i, q)])

        stage_Xpre(0)
        prologue_part1()
        prologue_extract()
        stage_XT(0)
        n_sp = (H + 3) // 4
        for _ in range(n_sp):
            bg_tasks.pop(0)()
        stg_issue(NSTG)
        bg_hold = bg_tasks[:]
        del bg_tasks[:]
        stage_Zc(0)
        bg_tasks.extend(bg_hold)
        load_unit(UID_GATE + 0, [(2 * D + 0 * CW, CW)], 0, prefetch=True)
        if NB > 1:
            stage_Xpre(1)
        for i in range(NB):
            ps_alloc.n = 8 if i > 0 else 6
            if i > 0:
                stage_out_pre(i - 1)
            stage_lnapply_gate(i - 1 if i > 0 else None, i)
            stage_Zr_v(i)
            if i > 0:
                stage_out(i - 1)
            if i + 1 < NB:
                stage_XT(i + 1)
            stage_Zr_ub(i, with_spatial=True)
            stage_spatial(i, [H - 1])
            bg_step(len(bg_tasks))
            ps_alloc.n = 6
            if ps_alloc.p >= 6:
                ps_alloc.p = 0
            stage_conv(i, i + 2 if i + 2 < NB else None, i + 1 if i + 1 < NB else None)
        stage_out_pre(NB - 1)
        stage_lnapply_gate(NB - 1, None)
        stage_out(NB - 1)
        S_.fence("sp", [("out", i, q) for i in range(NB) for q in range(NT)])

        S_.emit(nc)
    return nc


def host_layout(D, x_b, c_b, w_ada, b_ada, norm_g, w_in, conv_w, conv_b, conv_ln_g, conv_ln_b,
                sg_ln_g, sg_ln_b, w_s, b_s, w_out, final_g):
    KD = D // 128
    f = lambda a: np.ascontiguousarray(a, dtype=np.float32)
    fm = lambda v: f(np.asarray(v).reshape(KD, 128).T)
    vecs = np.stack([fm(norm_g), fm(conv_b), fm(conv_ln_g), fm(conv_ln_b), fm(sg_ln_g), fm(sg_ln_b)],
                    axis=1)
    cw = np.asarray(conv_w)[:, 0, :]
    W3 = np.zeros((128, KD, 4, 9, 4), np.float32)
    for s_ in range(4):
        for jj in range(9):
            for r in range(4):
                k = 4 * jj + s_ - 1 - r
                if 0 <= k < CONVW:
                    W3[s_ * 32:(s_ + 1) * 32, :, :, jj, r] = cw[k].reshape(KD, 4, 32).transpose(2, 0, 1)
    return {
        "x": f(x_b),
        "c": fm(c_b),
        "w_ada": f(w_ada),
        "b_ada": f(np.asarray(b_ada).reshape(1, -1)),
        "vecs": f(vecs),
        "w_in": f(w_in),
        "convw3": f(W3.reshape(128, KD * 144)),
        "w_sT": f(np.asarray(w_s).transpose(2, 0, 1)),
        "b_s": f(np.asarray(b_s).reshape(1, -1)),
        "w_out": f(w_out),
        "final_g": f(np.asarray(final_g).reshape(1, -1)),
    }


def kernel(x, c, w_ada, b_ada, norm_g, w_in, conv_w, conv_b, conv_ln_g, conv_ln_b,
           sg_ln_g, sg_ln_b, w_s, b_s, w_out, final_g):
    x = np.asarray(x)
    B, S, D = x.shape
    nc = build_program(D, S)
    in_maps = []
    for b in range(B):
        in_maps.append(host_layout(
            D, x[b], np.asarray(c)[b], np.asarray(w_ada)[0], np.asarray(b_ada)[0],
            np.asarray(norm_g)[0], np.asarray(w_in)[0], np.asarray(conv_w)[0],
            np.asarray(conv_b)[0], np.asarray(conv_ln_g)[0], np.asarray(conv_ln_b)[0],
            np.asarray(sg_ln_g)[0], np.asarray(sg_ln_b)[0], np.asarray(w_s)[0],
            np.asarray(b_s)[0], np.asarray(w_out)[0], final_g))
    res = run_bass_kernel_spmd(nc, in_maps, core_ids=list(range(B)))
    return np.stack([np.asarray(r["out"], dtype=np.float32) for r in res.results], axis=0)
```

```python
import contextlib
import numpy as np
import concourse.bass as bass
import concourse.mybir as mybir
from concourse.bass_utils import run_bass_kernel_spmd

F32 = mybir.dt.float32
BF16 = mybir.dt.bfloat16
AF = mybir.ActivationFunctionType
ALU = mybir.AluOpType
AX = mybir.AxisListType

EPS = 1e-6
CONVW = 31
TB = 512
HB = 16
SEM_WINDOW = 1500


class Sched:
    ENGS = ("pe", "act", "dve", "pool", "sp")

    def __init__(self):
        self.prog = {e: [] for e in self.ENGS}
        self.res = {}
        self.dma_count = {}

    def _deps(self, reads, writes):
        deps = []
        for k in reads:
            r = self.res.get(k)
            if r and r[0] is not None:
                deps.append((r[0], "raw"))
        for k in writes:
            r = self.res.get(k)
            if r:
                if r[0] is not None:
                    deps.append((r[0], "waw"))
                for t in r[1]:
                    deps.append((t, "war"))
        return deps

    def _commit(self, tok, reads, writes):
        for k in reads:
            self.res.setdefault(k, [None, []])[1].append(tok)
        for k in writes:
            self.res[k] = [tok, []]

    def op(self, eng, fn, reads=(), writes=()):
        deps = self._deps(reads, writes)
        tok = ("c", eng, len(self.prog[eng]))
        self.prog[eng].append(dict(kind="c", fn=fn, deps=deps, tok=tok))
        self._commit(tok, reads, writes)
        return tok

    def dma(self, eng, key, fn, reads=(), writes=(), n=1):
        deps = self._deps(reads, writes)
        c = self.dma_count.get(key, 0) + n
        self.dma_count[key] = c
        tok = ("d", key, c)
        self.prog[eng].append(dict(kind="d", fn=fn, deps=deps, tok=tok, key=key))
        self._commit(tok, reads, writes)
        return tok

    def fence(self, eng, keys):
        deps = self._deps(keys, keys)
        self.prog[eng].append(dict(kind="f", fn=None, deps=deps, tok=None))

    def emit(self, nc):
        needed = set()
        for e in self.ENGS:
            for o in self.prog[e]:
                for (t, ty) in o["deps"]:
                    if t[0] != "c":
                        continue
                    if t[1] == e and e in ("pe", "sp"):
                        continue
                    needed.add((t[1], t[2]))
        incno = {}
        nwin = {}
        for e in self.ENGS:
            cnt = 0
            for i, o in enumerate(self.prog[e]):
                if o["kind"] == "c" and (e, i) in needed:
                    cnt += 1
                    incno[(e, i)] = cnt
            nwin[e] = (cnt + SEM_WINDOW - 1) // SEM_WINDOW
        with contextlib.ExitStack() as st:
            csem = {}
            for e in self.ENGS:
                for w in range(nwin[e]):
                    csem[(e, w)] = st.enter_context(nc.semaphore(f"c_{e}_{w}"))
            dsem = {}
            for i, k in enumerate(self.dma_count):
                dsem[k] = st.enter_context(nc.semaphore(f"d_{i}"))
            block = st.enter_context(nc.Block())

            def body(engobj, e):
                waited_c = {}
                waited_d = {}
                for i, o in enumerate(self.prog[e]):
                    wc = {}
                    wd = {}
                    for (t, ty) in o["deps"]:
                        if t[0] == "c":
                            if t[1] == e and e in ("pe", "sp"):
                                continue
                            n = incno[(t[1], t[2])]
                            if waited_c.get(t[1], 0) >= n:
                                continue
                            wc[t[1]] = max(wc.get(t[1], 0), n)
                        else:
                            if waited_d.get(t[1], 0) >= t[2]:
                                continue
                            wd[t[1]] = max(wd.get(t[1], 0), t[2])
                    for e2, n in wc.items():
                        waited_c[e2] = n
                        w = (n - 1) // SEM_WINDOW
                        engobj.wait_ge(csem[(e2, w)], (n - 1) % SEM_WINDOW + 1)
                    for k, n in wd.items():
                        waited_d[k] = n
                        engobj.wait_ge(dsem[k], 16 * n)
                    if o["kind"] == "c":
                        ins = o["fn"](engobj)
                        n = incno.get((e, i))
                        if n is not None:
                            w = (n - 1) // SEM_WINDOW
                            ins.then_inc(csem[(e, w)], 1)
                    elif o["kind"] == "d":
                        o["fn"](engobj, dsem[o["key"]])

            block.tensor(lambda eo: body(eo, "pe"))
            block.scalar(lambda eo: body(eo, "act"))
            block.vector(lambda eo: body(eo, "dve"))
            block.gpsimd(lambda eo: body(eo, "pool"))
            block.sync(lambda eo: body(eo, "sp"))


class PsumAlloc:
    def __init__(self, n):
        self.n = n
        self.p = 0

    def get(self, k=1):
        if self.p % k:
            self.p += k - self.p % k
        if self.p + k > self.n:
            self.p = 0
        b = self.p
        self.p += k
        return b


def build_program(D, S, w_scratch=True):
    KD = D // 128
    KC = KD
    H = KD
    NT = TB // 128
    NB = S // TB
    CW = min(512, D)
    UPS = D // CW
    CWP = CW // 2
    NPU = D // CWP
    NSLOT = 3
    NUNITS = 2 * NPU + 2 * UPS
    GW = TB + 2 * HB
    E6 = 6 * D
    assert S % TB == 0 and D % 128 == 0

    nc = bass.Bass("TRN2", target_bir_lowering=False)
    x_d = nc.dram_tensor("x", [S, D], F32, kind="ExternalInput")
    c_d = nc.dram_tensor("c", [128, KD], F32, kind="ExternalInput")
    wada_d = nc.dram_tensor("w_ada", [D, 3 * D], F32, kind="ExternalInput")
    bada_d = nc.dram_tensor("b_ada", [1, 3 * D], F32, kind="ExternalInput")
    vecs_d = nc.dram_tensor("vecs", [128, 6, KD], F32, kind="ExternalInput")
    win_d = nc.dram_tensor("w_in", [D, E6], F32, kind="ExternalInput")
    convw_d = nc.dram_tensor("convw3", [128, KC * 4 * 36], F32, kind="ExternalInput")
    wsT_d = nc.dram_tensor("w_sT", [128, H, 128], F32, kind="ExternalInput")
    bs_d = nc.dram_tensor("b_s", [1, H * 128], F32, kind="ExternalInput")
    wout_d = nc.dram_tensor("w_out", [2 * D, D], F32, kind="ExternalInput")
    fg_d = nc.dram_tensor("final_g", [1, D], F32, kind="ExternalInput")
    out_d = nc.dram_tensor("out", [S, D], F32, kind="ExternalOutput")
    wbf_d = nc.dram_tensor("wbf_scratch", [NUNITS, 128, KD, CW], BF16)

    x_ap, out_ap = x_d.ap(), out_d.ap()
    win_r = win_d.ap().rearrange("(j p) n -> p j n", p=128)
    wout_r = wout_d.ap().rearrange("(j p) n -> p j n", p=128)
    wada_r = wada_d.ap().rearrange("(j p) n -> p j n", p=128)
    wbf_ap = wbf_d.ap()

    def bcast_row(handle, off, n, parts=128):
        return bass.AP(handle, off, [[0, parts], [1, n]])

    S_ = Sched()
    ps_alloc = PsumAlloc(6)
    MEAN_BANK, EX2_BANK = 6, 7

    with contextlib.ExitStack() as st:
        def sb(name, shape, dt):
            return st.enter_context(nc.sbuf_tensor("sb_" + name, shape, dt))

        ps = st.enter_context(nc.psum_tensor("ps", [128, 8, 512], F32))

        xt = sb("xt", [128, 2, D], F32)
        junk = sb("junk", [128, D], BF16)
        ssx = sb("ssx", [128, 2], F32)
        rsx = sb("rsx", [128, 2], F32)
        xs = sb("xs", [128, NT, D], BF16)
        hT = sb("hT", [128, 2, KD, TB], BF16)
        wst = sb("wst", [128, NSLOT, KD, CW], BF16)
        G = sb("G", [128, 2, KC, GW], BF16)
        th = sb("th", [128, 2, TB], BF16)
        sga = sb("sga", [128, KC, TB], BF16)
        sgb = sb("sgb", [128, 2, TB], BF16)
        ub = sb("ub", [128, KC, TB], BF16)
        vst = sb("vst", [128, 2, 2, 6], F32)
        vmv = sb("vmv", [128, 2, 2], F32)
        vrs = sb("vrs", [128, 2], F32)
        vnm = sb("vnm", [128, 2], F32)
        vhat = sb("vhat", [128, NT, D], BF16)
        convb = sb("convb", [128, KC, TB], BF16)
        convsq = sb("convsq", [128, 3, TB], BF16)
        mean_sb = sb("mean_sb", [128, TB], F32)
        var_sb = sb("var_sb", [128, TB], F32)
        nmr_sb = sb("nmr_sb", [128, TB], F32)
        ntmp = sb("ntmp", [128, 2, TB], F32)
        sact = sb("sact", [128, 2, TB], BF16)
        tt = sb("tt", [128, 2, TB], BF16)
        xres = sb("xres", [128, NT, D], F32)
        ss2 = sb("ss2", [128, NT], F32)
        rs2 = sb("rs2", [128, NT], F32)
        LSLOTS = 5
        Lw = sb("Lw", [128, LSLOTS, 9, 32, 4], BF16)
        maskP = sb("maskP", [128, 9, 32, 4], BF16)
        W3b = sb("W3b", [128, KC, 4, 36], BF16)
        identP = sb("identP", [128, 32], BF16)
        TP = GW // 4
        Rsb = sb("Rsb", [128, 2, 4, TP], BF16)
        Csb = sb("Csb", [128, 2, TB], BF16)
        wout = sb("wout", [128, 2 * KC, D], BF16)
        ident_b = sb("ident_b", [128, 128], BF16)
        ones_b = sb("ones_b", [128, 128], BF16)
        ones_f = sb("ones_f", [128, 128], F32)
        negh = sb("negh", [128, 1], F32)
        fgmat = sb("fgmat", [128, D], F32)
        Bm = sb("Bm", [128, H, 128], F32)
        wsT = sb("wsT", [128, H, 128], BF16)
        vecs = sb("vecs", [128, 6, KD], F32)
        Gs = sb("Gs", [128, KD], F32)
        shiftv = sb("shiftv", [128, KD], F32)
        cact = sb("cact", [128, KD], F32)
        identrep = var_sb[:].rearrange("p (a b) -> p a b", b=128)
        crep = nmr_sb[:, 0:256].rearrange("p (a b) -> p a b", b=128)
        stg = ntmp
        exttmp = mean_sb

        V_NORM_G, V_CONV_B, V_CLN_G, V_CLN_B, V_SLN_G, V_SLN_B = range(6)

        def bank(b, n=512):
            return ps[:, b, 0:n]

        def bank_bf(b):
            return ps[:, b, :].bitcast(BF16)

        def dma1(eng, key, out, in_, reads=(), writes=()):
            def fn(e, sem, out=out, in_=in_):
                e.dma_start(out=out, in_=in_).then_inc(sem, 16)
            return S_.dma(eng, key, fn, reads=reads, writes=writes)

        dma1("sp", "ld_c", cact[:], c_d.ap(), writes=["cact"])
        dma1("sp", "ld_vecs", vecs[:], vecs_d.ap(), writes=["vecs"])
        w3stg = ub[:].rearrange("p a b -> p (a b)").bitcast(F32)
        dma1("sp", "ld_convw", w3stg[:, 0:KC * 144], convw_d.ap(), writes=[("ub", c) for c in range(KC)])
        dma1("sp", "ld_fg", fgmat[:], bcast_row(fg_d, 0, D), writes=["fgmat"])

        S_.op("pool", lambda e: e.memset(identrep[:], 0.0), writes=["var_sb"])
        for r in range(4):
            S_.op("pool", lambda e, r=r: e.affine_select(
                out=identrep[:, r, :], in_=identrep[:, r, :], compare_op=ALU.not_equal,
                fill=1.0, base=0, pattern=[[-1, 128]], channel_multiplier=1),
                reads=["var_sb"], writes=["var_sb"])
        S_.op("pool", lambda e: e.tensor_copy(out=ident_b[:], in_=identrep[:, 0, :]),
              reads=["var_sb"], writes=["ident_b"])
        S_.op("pool", lambda e: e.memset(ones_b[:], 1.0 / D), writes=["ones_b"])
        S_.op("pool", lambda e: e.memset(ones_f[:], 1.0), writes=["ones_f"])
        S_.op("pool", lambda e: e.memset(negh[:], -0.5), writes=["negh"])
        S_.op("dve", lambda e: e.tensor_scalar(
            out=W3b[:].rearrange("p a b c -> p (a b c)"), in0=w3stg[:, 0:KC * 144], scalar1=0.5,
            scalar2=None, op0=ALU.mult),
            reads=[("ub", c) for c in range(KC)], writes=["W3b"])
        S_.op("dve", lambda e: e.tensor_tensor(out=identP[:], in0=ident_b[:, 0:32], in1=ident_b[:, 32:64],
                                               op=ALU.add), reads=["ident_b"], writes=["identP"])
        S_.op("dve", lambda e: e.tensor_tensor(out=identP[:], in0=identP[:], in1=ident_b[:, 64:96],
                                               op=ALU.add), reads=["ident_b", "identP"], writes=["identP"])
        S_.op("dve", lambda e: e.tensor_tensor(out=identP[:], in0=identP[:], in1=ident_b[:, 96:128],
                                               op=ALU.add), reads=["ident_b", "identP"], writes=["identP"])
        S_.op("dve", lambda e: e.tensor_copy(
            out=maskP[:], in_=identP[:].unsqueeze(1).unsqueeze(3).to_broadcast([128, 9, 32, 4])),
            reads=["identP"], writes=["maskP"])
        S_.op("act", lambda e: e.activation(out=cact[:], in_=cact[:], func=AF.Silu),
              reads=["cact"], writes=["cact"])

        NMB = (3 * D + 511) // 512
        assert NMB <= 6
        stg_i = [0]

        def next_stg():
            s = stg_i[0] % 2
            stg_i[0] += 1
            return s

        def mwid(nb):
            return min(512, 3 * D - nb * 512)

        f32view = lambda t, pat: t[:].rearrange(pat).bitcast(F32)
        big_stg = [
            (f32view(convb, "p a b -> p (a b)"), [("convb", c) for c in range(KC)]),
            (f32view(sga, "p a b -> p (a b)"), [("sga", c) for c in range(KC)]),
            (f32view(ub, "p a b -> p (a b)"), [("ub", c) for c in range(KC)]),
            (f32view(vhat, "p a b -> p (a b)"), [("vhat", q) for q in range(NT)]),
        ]
        CREP_SL = NT - 1
        crep_all = xres[:, CREP_SL, :].rearrange("p (a b) -> p a b", b=128)
        crep_keys = [("xres", CREP_SL)]
        NSTG = NT - 1
        brow = G[:].rearrange("p a b c -> p (a b c)").bitcast(F32)
        brow_keys = [(k_, b_, c_) for k_ in ("G", "Gl", "Gr") for b_ in range(2) for c_ in range(KC)]
        dma1("sp", "ld_bada", brow[0:1, 0:2 * D], bada_d.ap()[:, 0:2 * D], writes=brow_keys)
        browg = Lw[:].rearrange("p a b c d -> p (a b c d)").bitcast(F32)
        browg_keys = [("Lw", sl_) for sl_ in range(LSLOTS)]
        dma1("sp", "ld_badag", browg[0:1, 0:D], bada_d.ap()[:, 2 * D:3 * D], writes=browg_keys)

        def prologue_part1():
            for j in range(KD):
                S_.op("dve", lambda e, j=j: e.tensor_copy(
                    out=crep_all[:, j, :], in_=cact[:, j:j + 1].to_broadcast([128, 128])),
                    reads=["cact"], writes=crep_keys)
            nb1 = (2 * D + 511) // 512
            for j in range(KD):
                stgv, skeys = big_stg[j % 4]
                dma1("sp", ("ld_bigstg", j % 4), stgv[:, 0:2 * D], wada_r[:, j, 0:2 * D], writes=skeys)
                for nb in range(nb1):
                    mw = min(512, 2 * D - nb * 512)
                    S_.op("pe", lambda e, j=j, nb=nb, mw=mw, stgv=stgv: e.matmul(
                        bank(nb, mw), lhsT=crep_all[:, j, :], rhs=stgv[:, nb * 512:nb * 512 + mw],
                        start=(j == 0), stop=False),
                        reads=crep_keys + skeys, writes=[("ps", nb)])
            for nb in range(nb1):
                mw = min(512, 2 * D - nb * 512)
                S_.op("pe", lambda e, nb=nb, mw=mw: e.matmul(
                    bank(nb, mw), lhsT=ones_f[0:1, :], rhs=brow[0:1, nb * 512:nb * 512 + mw],
                    start=False, stop=True),
                    reads=["ones_f"] + brow_keys, writes=[("ps", nb)])

        def gate_cols(n0, n):
            b_, o_ = divmod(n0, 512)
            assert o_ + n <= 512
            return ps[:, MEAN_BANK + b_, o_:o_ + n], ("ps", MEAN_BANK + b_)

        stg_loads = []
        stg_issued = [0]

        def stg_issue(upto):
            while stg_issued[0] < min(upto, len(stg_loads)):
                stg_loads[stg_issued[0]][1]()
                stg_issued[0] += 1

        def stg_register(src_ap):
            k = len(stg_loads)
            sl = k % NSTG
            stg_loads.append((sl, lambda sl=sl, src_ap=src_ap: dma1(
                "sp", ("ld_xres", sl), xres[:, sl, :], src_ap, writes=[("xres", sl)])))
            return k, sl

        def gate_task(j):
            k, sl = stg_register(wada_r[:, j, 2 * D:3 * D])

            def run():
                stg_issue(k + 1)
                for n0 in range(0, D, 512):
                    mw = min(512, D - n0)
                    dst, dkey = gate_cols(n0, mw)
                    S_.op("pe", lambda e, sl=sl, n0=n0, mw=mw, dst=dst: e.matmul(
                        dst, lhsT=crep_all[:, j, :], rhs=xres[:, sl, n0:n0 + mw],
                        start=(j == 0), stop=False),
                        reads=crep_keys + [("xres", sl)], writes=[dkey])
                stg_issue(k + 1 + NSTG)
            return run

        def gate_bias_task():
            for n0 in range(0, D, 512):
                mw = min(512, D - n0)
                dst, dkey = gate_cols(n0, mw)
                S_.op("pe", lambda e, n0=n0, mw=mw, dst=dst: e.matmul(
                    dst, lhsT=ones_f[0:1, :],
                    rhs=browg[0:1, n0:n0 + mw], start=False, stop=True),
                    reads=["ones_f"] + browg_keys, writes=[dkey])

        def mod_cols(c0, n):
            b, o = divmod(c0, 512)
            assert o + n <= 512, (c0, n)
            return ps[:, b, o:o + n], ("ps", b)

        def extract(dst, c0, dst_key):
            for j0 in range(0, KD, 4):
                nj = min(4, KD - j0)
                src, key = mod_cols(c0 + j0 * 128, nj * 128)
                S_.op("dve", lambda e, src=src, nj=nj: e.tensor_tensor(
                    out=exttmp[:, 0:nj * 128], in0=src,
                    in1=identrep[:, 0:nj, :].rearrange("p a b -> p (a b)"), op=ALU.mult),
                    reads=[key, "var_sb"], writes=["mean_sb"])
                S_.op("dve", lambda e, j0=j0, nj=nj: e.tensor_reduce(
                    out=dst[:, j0:j0 + nj],
                    in_=exttmp[:, 0:nj * 128].rearrange("p (a b) -> p a b", b=128),
                    axis=AX.X, op=ALU.add),
                    reads=["mean_sb"], writes=[dst_key])

        def prologue_extract():
            extract(shiftv, 0, "shiftv")
            extract(Gs, D, "Gs")
            S_.op("dve", lambda e: e.scalar_tensor_tensor(
                out=Gs[:], in0=Gs[:], scalar=1.0, in1=vecs[:, V_NORM_G, :], op0=ALU.add, op1=ALU.mult),
                reads=["Gs", "vecs"], writes=["Gs"])

        def fold_task(j):
            k, sl = stg_register(wout_r[:, j, :])

            def run():
                stg_issue(k + 1)
                for n0 in range(0, D, 512):
                    nn = min(512, D - n0)
                    src, key = gate_cols(n0, nn)
                    S_.op("dve", lambda e, n0=n0, nn=nn, sl=sl, src=src: e.tensor_tensor(
                        out=wout[:, j, n0:n0 + nn], in0=xres[:, sl, n0:n0 + nn], in1=src, op=ALU.mult),
                        reads=[("xres", sl), key], writes=[("wout", j)])
                stg_issue(k + 1 + NSTG)
            return run

        def spatial_setup_task(h0):
            def run():
                nh = min(4, H - h0)
                s = next_stg()
                dma1("sp", ("ld_ntmp", s), stg[:, s, 0:nh * 128].rearrange("p (a b) -> p a b", b=128),
                     wsT_d.ap()[:, h0:h0 + nh, :], writes=[("ntmp", s)])
                S_.op("act", lambda e, s=s: e.activation(
                    out=wsT[:, h0:h0 + nh, :].rearrange("p a b -> p (a b)"),
                    in_=stg[:, s, 0:nh * 128], func=AF.Copy),
                    reads=[("ntmp", s)], writes=["wsT"])
                S_.op("pe", lambda e, s=s: e.matmul(
                    bank(MEAN_BANK, nh * 128), lhsT=ones_f[:], rhs=stg[:, s, 0:nh * 128],
                    start=True, stop=True),
                    reads=["ones_f", ("ntmp", s)], writes=[("ps", MEAN_BANK)])
                s2 = next_stg()
                dma1("sp", ("ld_ntmp", s2), stg[:, s2, 0:nh * 128],
                     bcast_row(bs_d, h0 * 128, nh * 128), writes=[("ntmp", s2)])
                for hh in range(nh):
                    h = h0 + hh
                    S_.op("dve", lambda e, h=h, hh=hh, s2=s2: e.scalar_tensor_tensor(
                        out=Bm[:, h, :], in0=ps[:, MEAN_BANK, hh * 128:(hh + 1) * 128],
                        scalar=vecs[:, V_SLN_B, h:h + 1], in1=stg[:, s2, hh * 128:(hh + 1) * 128],
                        op0=ALU.mult, op1=ALU.add),
                        reads=[("ps", MEAN_BANK), "vecs", ("ntmp", s2)], writes=["Bm"])
            return run

        bg_tasks = []
        for h0 in range(0, H, 4):
            bg_tasks.append(spatial_setup_task(h0))
        for j in range(KD):
            bg_tasks.append(gate_task(j))
        bg_tasks.append(gate_bias_task)
        for j in range(2 * KC):
            bg_tasks.append(fold_task(j))

        def bg_step(n=1):
            for _ in range(n):
                if bg_tasks:
                    bg_tasks.pop(0)()

        unit_ctr = [0]

        prefetched = {}

        def load_unit(uid, pieces, blk, prefetch=False):
            if (uid, blk) in prefetched:
                return prefetched.pop((uid, blk))
            slot = unit_ctr[0] % NSLOT
            unit_ctr[0] += 1
            wkey = ("wst", slot)
            if blk == 0 or not w_scratch:
                hk = KD // 2 if KD >= 2 else KD

                def fn(e, sem, slot=slot, pieces=pieces, hk=hk):
                    o = 0
                    for (c0, ncol) in pieces:
                        for j0 in range(0, KD, hk):
                            e.dma_start(out=wst[:, slot, j0:j0 + hk, o:o + ncol],
                                        in_=win_r[:, j0:j0 + hk, c0:c0 + ncol]).then_inc(sem, 16)
                        o += ncol
                S_.dma("pool", ("ld_wst_sw", slot), fn, writes=[wkey],
                       n=len(pieces) * ((KD + hk - 1) // hk))
                if w_scratch:
                    dma1("sp", ("st_wbf", slot), wbf_ap[uid], wst[:, slot, :, :],
                         reads=[wkey], writes=[("wbf", uid)])
            else:
                dma1("sp", ("ld_wst", slot), wst[:, slot, :, :], wbf_ap[uid],
                     reads=[("wbf", uid)], writes=[wkey])
            if prefetch:
                prefetched[(uid, blk)] = slot
            return slot

        def load_pair(uid0, slab_a, slab_b, u, blk):
            return load_unit(uid0 + u, [(slab_a * D + u * CWP, CWP), (slab_b * D + u * CWP, CWP)], blk)

        def load_full(uid0, slab, u, blk):
            return load_unit(uid0 + u, [(slab * D + u * CW, CW)], blk)

        UID_CONV, UID_GATE, UID_UB, UID_V = 0, NPU, NPU + UPS, 2 * NPU + UPS

        def stage_Xpre(i, tiles=None):
            for q in (range(NT) if tiles is None else tiles):
                t0 = i * TB + q * 128
                sl = q % 2
                dma1("sp", ("ld_xt", sl), xt[:, sl, :], x_ap[t0:t0 + 128, :], writes=[("xt", sl)])
                S_.op("act", lambda e, sl=sl, q=q: e.activation(
                    out=xs[:, q, :], in_=xt[:, sl, :], func=AF.Square, accum_out=ssx[:, sl:sl + 1]),
                    reads=[("xt", sl)], writes=[("xs", q), ("ssx", sl)])
                S_.op("dve", lambda e, sl=sl: e.tensor_scalar(
                    out=rsx[:, sl:sl + 1], in0=ssx[:, sl:sl + 1], scalar1=1.0 / D, scalar2=EPS,
                    op0=ALU.mult, op1=ALU.add), reads=[("ssx", sl)], writes=[("rsx", sl)])
                S_.op("pool", lambda e, sl=sl: e.tensor_tensor(
                    out=rsx[:, sl:sl + 1], in0=rsx[:, sl:sl + 1], in1=negh[:], op=ALU.pow),
                    reads=[("rsx", sl), "negh"], writes=[("rsx", sl)])
                S_.op("dve", lambda e, sl=sl, q=q: e.tensor_scalar(
                    out=xs[:, q, :], in0=xt[:, sl, :], scalar1=rsx[:, sl:sl + 1], scalar2=None,
                    op0=ALU.mult), reads=[("xt", sl), ("rsx", sl)], writes=[("xs", q)])

        def stage_XT(i):
            nb4 = max(1, KD // 2)
            pb = ps_alloc.get(4 if nb4 > 2 else nb4)
            for q in range(NT):
                for j in range(KD):
                    b = pb + j // 2
                    col = (j % 2) * 512 + q * 128
                    S_.op("pe", lambda e, q=q, j=j, b=b, col=col: e.transpose(
                        bank_bf(b)[:, col:col + 128], xs[:, q, j * 128:(j + 1) * 128], ident_b[:]),
                        reads=[("xs", q), "ident_b"], writes=[("ps", b)])
            for j in range(KD):
                b = pb + j // 2
                col = (j % 2) * 512
                if (j // 2) % 2 == 0:
                    S_.op("dve", lambda e, j=j, b=b, col=col: e.tensor_scalar(
                        out=hT[:, i % 2, j, :], in0=bank_bf(b)[:, col:col + 512],
                        scalar1=Gs[:, j:j + 1], scalar2=shiftv[:, j:j + 1], op0=ALU.mult, op1=ALU.add),
                        reads=[("ps", b), "Gs", "shiftv"], writes=[("hT", i % 2, j)])
                else:
                    S_.op("act", lambda e, j=j, b=b, col=col: e.activation(
                        out=hT[:, i % 2, j, :], in_=bank_bf(b)[:, col:col + 512], func=AF.Identity,
                        bias=shiftv[:, j:j + 1], scale=Gs[:, j:j + 1]),
                        reads=[("ps", b), "Gs", "shiftv"], writes=[("hT", i % 2, j)])

        def zfm(slot, cc, b, hb):
            for j in range(KD):
                S_.op("pe", lambda e, j=j, slot=slot, cc=cc, b=b, hb=hb: e.matmul(
                    bank(b, TB), lhsT=wst[:, slot, j, cc * 128:(cc + 1) * 128], rhs=hT[:, hb, j, :],
                    start=(j == 0), stop=(j == KD - 1)),
                    reads=[("wst", slot), ("hT", hb, j)], writes=[("ps", b)])

        CPF = CW // 128
        CPP = CWP // 128

        zc_slots = {}

        def zc_chunk(i, c):
            gb = i % 2
            u, cc = divmod(c, CPP)
            if cc == 0:
                zc_slots[(i, u)] = load_pair(UID_CONV, 0, 1, u, i)
            sp_ = zc_slots[(i, u)]
            ba = ps_alloc.get()
            zfm(sp_, cc, ba, i % 2)
            bg = ps_alloc.get()
            zfm(sp_, CPP + cc, bg, i % 2)
            ts_ = c % 2
            S_.op("act", lambda e, bg=bg, ts_=ts_: e.activation(
                out=th[:, ts_, :], in_=bank(bg, TB), func=AF.Tanh, scale=0.5),
                reads=[("ps", bg)], writes=[("th", ts_)])
            S_.op("dve", lambda e, ba=ba, ts_=ts_, c=c, gb=gb: e.scalar_tensor_tensor(
                out=G[:, gb, c, HB:HB + TB], in0=th[:, ts_, :], scalar=1.0, in1=bank(ba, TB),
                op0=ALU.add, op1=ALU.mult),
                reads=[("th", ts_), ("ps", ba)], writes=[("G", gb, c)])
            bg_step()
            if i > 0:
                ob = (i - 1) % 2
                S_.op("dve", lambda e, gb=gb, ob=ob, c=c: e.tensor_copy(
                    out=G[:, ob, c, HB + TB:HB + TB + HB], in_=G[:, gb, c, HB:HB + HB]),
                    reads=[("G", gb, c)], writes=[("Gr", ob, c)])
                S_.op("dve", lambda e, gb=gb, ob=ob, c=c: e.tensor_copy(
                    out=G[:, gb, c, 0:HB], in_=G[:, ob, c, TB:TB + HB]),
                    reads=[("G", ob, c)], writes=[("Gl", gb, c)])

        def zc_edges(i):
            gb = i % 2
            if i == 0:
                S_.op("pool", lambda e, gb=gb: e.memset(G[:, gb, :, 0:HB], 0.0),
                      writes=[("Gl", gb, c) for c in range(KC)])
            if i == NB - 1:
                S_.op("pool", lambda e, gb=gb: e.memset(G[:, gb, :, HB + TB:HB + TB + HB], 0.0),
                      writes=[("Gr", gb, c) for c in range(KC)])

        def stage_Zc(i):
            zc_edges(i)
            for c in range(KC):
                zc_chunk(i, c)

        def ln_MA(c):
            ts_ = c % 2
            S_.op("dve", lambda e, c=c, ts_=ts_: e.tensor_tensor(
                out=ntmp[:, ts_, :], in0=convb[:, c, :], in1=var_sb[:], op=ALU.mult),
                reads=[("convb", c), "var_sb"], writes=[("ntmp", ts_)])
            S_.op("dve", lambda e, ts_=ts_: e.tensor_tensor(
                out=ntmp[:, ts_, :], in0=ntmp[:, ts_, :], in1=nmr_sb[:], op=ALU.add),
                reads=[("ntmp", ts_), "nmr_sb"], writes=[("ntmp", ts_)])
            S_.op("act", lambda e, c=c, ts_=ts_: e.activation(
                out=sact[:, ts_, :], in_=ntmp[:, ts_, :], func=AF.Silu,
                bias=vecs[:, V_CLN_B, c:c + 1], scale=vecs[:, V_CLN_G, c:c + 1]),
                reads=[("ntmp", ts_), "vecs"], writes=[("sact", ts_)])

        def ln_F(c):
            ts_ = c % 2
            S_.op("dve", lambda e, c=c, ts_=ts_: e.tensor_tensor(
                out=convb[:, c, :], in0=sact[:, ts_, :], in1=sga[:, c, :], op=ALU.mult),
                reads=[("sact", ts_), ("sga", c)], writes=[("convb", c)])

        def stage_lnapply_gate(i_ln, i_gate):
            slots = {}
            for step in range(KC + 1):
                if i_ln is not None and step < KC:
                    ln_MA(step)
                c = step - 1
                if c < 0:
                    continue
                if i_ln is not None:
                    ln_F(c)
                if i_gate is not None:
                    u, cc = divmod(c, CPF)
                    if cc == 0:
                        slots[u] = load_full(UID_GATE, 2, u, i_gate)
                    b = ps_alloc.get()
                    zfm(slots[u], cc, b, i_gate % 2)
                    S_.op("act", lambda e, b=b, c=c: e.activation(
                        out=sga[:, c, :], in_=bank(b, TB), func=AF.Silu),
                        reads=[("ps", b)], writes=[("sga", c)])
                    bg_step()

        def stage_Zr_ub(i, with_spatial=False):
            for u in range(NPU):
                sp_ = load_pair(UID_UB, 3, 5, u, i)
                for cc in range(CPP):
                    c = u * CPP + cc
                    if with_spatial and c >= 1:
                        stage_spatial(i, [c - 1])
                    bu = ps_alloc.get()
                    zfm(sp_, cc, bu, i % 2)
                    bb = ps_alloc.get()
                    zfm(sp_, CPP + cc, bb, i % 2)
                    ts_ = c % 2
                    S_.op("act", lambda e, bb=bb, ts_=ts_: e.activation(
                        out=sgb[:, ts_, :], in_=bank(bb, TB), func=AF.Silu),
                        reads=[("ps", bb)], writes=[("sgb", ts_)])
                    S_.op("dve", lambda e, bu=bu, ts_=ts_, c=c: e.tensor_tensor(
                        out=ub[:, c, :], in0=bank(bu, TB), in1=sgb[:, ts_, :], op=ALU.mult),
                        reads=[("ps", bu), ("sgb", ts_)], writes=[("ub", c)])
                    bg_step()
        def stage_Zr_v(i):
            assert UPS <= 2
            vslots = [load_full(UID_V, 4, u, i) for u in range(UPS)]
            for q in range(NT):
                pb = ps_alloc.get(UPS)
                sl = q % 2
                for u in range(UPS):
                    for j in range(KD):
                        S_.op("pe", lambda e, j=j, q=q, u=u, pb=pb: e.matmul(
                            bank(pb + u, CW), lhsT=hT[:, i % 2, j, q * 128:(q + 1) * 128],
                            rhs=wst[:, vslots[u], j, :], start=(j == 0), stop=(j == KD - 1)),
                            reads=[("wst", vslots[u]), ("hT", i % 2, j)], writes=[("ps", pb + u)])
                    S_.op("dve", lambda e, u=u, pb=pb, sl=sl: e.bn_stats(
                        out=vst[:, sl, u, :], in_=bank(pb + u, CW)),
                        reads=[("ps", pb + u)], writes=[("vst", sl, u)])
                S_.op("dve", lambda e, sl=sl: e.bn_aggr(
                    out=vmv[:, sl, :], in_=vst[:, sl, 0:UPS, :].rearrange("p a b -> p (a b)")),
                    reads=[("vst", sl, u) for u in range(UPS)], writes=[("vmv", sl)])
                S_.op("dve", lambda e, sl=sl: e.tensor_scalar(
                    out=vrs[:, sl:sl + 1], in0=vmv[:, sl, 1:2], scalar1=EPS, scalar2=None,
                    op0=ALU.add), reads=[("vmv", sl)], writes=[("vrs", sl)])
                S_.op("pool", lambda e, sl=sl: e.tensor_tensor(
                    out=vrs[:, sl:sl + 1], in0=vrs[:, sl:sl + 1], in1=negh[:], op=ALU.pow),
                    reads=[("vrs", sl), "negh"], writes=[("vrs", sl)])
                S_.op("dve", lambda e, sl=sl: e.scalar_tensor_tensor(
                    out=vnm[:, sl:sl + 1], in0=vmv[:, sl, 0:1], scalar=-1.0, in1=vrs[:, sl:sl + 1],
                    op0=ALU.mult, op1=ALU.mult), reads=[("vmv", sl), ("vrs", sl)], writes=[("vnm", sl)])
                for u in range(UPS):
                    S_.op("act", lambda e, u=u, pb=pb, sl=sl, q=q: e.activation(
                        out=vhat[:, q, u * CW:(u + 1) * CW], in_=bank(pb + u, CW), func=AF.Identity,
                        bias=vnm[:, sl:sl + 1], scale=vrs[:, sl:sl + 1]),
                        reads=[("ps", pb + u), ("vnm", sl), ("vrs", sl)], writes=[("vhat", q)])
                bg_step()

        lw_ctr = [0]

        def lw_gen(c, g):
            slot = lw_ctr[0] % LSLOTS
            lw_ctr[0] += 1
            S_.op("dve", lambda e, c=c, g=g, slot=slot: e.tensor_tensor(
                out=Lw[:, slot, :, :, :], in0=maskP[:],
                in1=W3b[:, c, g, :].rearrange("p (j r) -> p j r", r=4).unsqueeze(2).to_broadcast([128, 9, 32, 4]),
                op=ALU.mult),
                reads=["maskP", "W3b"], writes=[("Lw", slot)])
            return slot

        def stage_conv(i, xpre_blk=None, i_zc=None):
            gb = i % 2
            if i_zc is not None:
                zc_edges(i_zc)
            gkeys = lambda c: [("G", gb, c), ("Gl", gb, c), ("Gr", gb, c)]

            def stats_mm(c):
                sq = c % 3
                S_.op("pe", lambda e, c=c: e.matmul(
                    bank(MEAN_BANK, TB), lhsT=ones_b[:], rhs=convb[:, c, :],
                    start=(c == 0), stop=(c == KC - 1)),
                    reads=["ones_b", ("convb", c)], writes=[("ps", MEAN_BANK)])
                S_.op("pe", lambda e, c=c, sq=sq: e.matmul(
                    bank(EX2_BANK, TB), lhsT=ones_b[:], rhs=convsq[:, sq, :],
                    start=(c == 0), stop=(c == KC - 1)),
                    reads=["ones_b", ("convsq", sq)], writes=[("ps", EX2_BANK)])

            def conv_in(c):
                rs = c % 2
                pb = ps_alloc.get(2)
                gview = G[:, gb, c, :].rearrange("p (t s) -> p s t", s=4)
                for g in range(4):
                    bk, col0 = pb + g // 2, (g % 2) * TP
                    for s_ in range(4):
                        S_.op("pe", lambda e, g=g, s_=s_, bk=bk, col0=col0: e.matmul(
                            ps[32 * s_:32 * s_ + 32, bk, col0:col0 + TP],
                            lhsT=ident_b[:, 32 * g:32 * g + 32], rhs=gview[:, s_, :],
                            start=True, stop=True, tile_position=(0, 32 * s_)),
                            reads=["ident_b"] + gkeys(c), writes=[("ps", bk)])
                for h in range(2):
                    S_.op("act", lambda e, h=h, rs=rs, pb=pb: e.activation(
                        out=Rsb[:, rs, 2 * h:2 * h + 2, :],
                        in_=ps[:, pb + h, 0:2 * TP].rearrange("p (a b) -> p a b", b=TP), func=AF.Copy),
                        reads=[("ps", pb + h)], writes=[("Rsb", rs, h)])

            def conv_out(c):
                rs = c % 2
                sq = c % 3
                b2 = ps_alloc.get()
                for r in range(4):
                    for g in range(4):
                        S_.op("pe", lambda e, g=g, r=r, b2=b2, rs=rs: e.matmul(
                            ps[32 * g:32 * g + 32, b2, r * 128:(r + 1) * 128],
                            lhsT=ident_b[:].rearrange("p (c r) -> p r c", r=4)[:, r, :],
                            rhs=Csb[:, rs, g * 128:(g + 1) * 128],
                            start=True, stop=True, tile_position=(0, 32 * g)),
                            reads=["ident_b", ("Csb", rs)], writes=[("ps", b2)])
                src = ps[:, b2, :].rearrange("p (r t) -> p r t", r=4)
                S_.op("act", lambda e, c=c, src=src: e.activation(
                    out=convb[:, c, :].rearrange("p (t r) -> p r t", r=4), in_=src, func=AF.Identity,
                    bias=vecs[:, V_CONV_B, c:c + 1], scale=1.0),
                    reads=[("ps", b2), "vecs"], writes=[("convb", c)])
                S_.op("act", lambda e, c=c, sq=sq, src=src: e.activation(
                    out=convsq[:, sq, :].rearrange("p (t r) -> p r t", r=4), in_=src, func=AF.Square,
                    bias=vecs[:, V_CONV_B, c:c + 1], scale=1.0),
                    reads=[("ps", b2), "vecs"], writes=[("convsq", sq)])

            pend = {}
            gen_q = [(c, g) for c in range(KC) for g in range(4)]
            lw_slots = {}

            def gen_next(n):
                for _ in range(n):
                    if gen_q:
                        c_, g_ = gen_q.pop(0)
                        lw_slots[(c_, g_)] = lw_gen(c_, g_)

            gen_next(LSLOTS - 1)
            sk = 1 if i_zc is not None else 0
            for step in range(KC + 4 + sk):
                if step < KC and i_zc is not None:
                    zc_chunk(i_zc, step)
                c = step - sk
                if 0 <= c < KC:
                    conv_in(c)
                c = step - 1 - sk
                if 0 <= c < KC:
                    rs = c % 2
                    b = ps_alloc.get()
                    for g in range(4):
                        while (c, g) not in lw_slots:
                            gen_next(1)
                        sl_ = lw_slots[(c, g)]
                        for jj in range(9):
                            S_.op("pe", lambda e, g=g, jj=jj, b=b, rs=rs, sl_=sl_: e.matmul(
                                ps[:, b, g * 128:(g + 1) * 128], lhsT=Lw[:, sl_, jj, :, :].rearrange("p c r -> p (c r)"),
                                rhs=Rsb[:, rs, g, jj:jj + 128], start=(jj == 0), stop=(jj == 8)),
                                reads=[("Lw", sl_), ("Rsb", rs, g // 2)], writes=[("ps", b)])
                        gen_next(1)
                    S_.op("act", lambda e, b=b, rs=rs: e.activation(
                        out=Csb[:, rs, :], in_=bank(b, TB), func=AF.Copy),
                        reads=[("ps", b)], writes=[("Csb", rs)])
                    if xpre_blk is not None and c < NT:
                        tl = [c] if KC >= NT else list(range(c * NT // KC, (c + 1) * NT // KC))
                        stage_Xpre(xpre_blk, tl)
                c = step - 2 - sk
                if 0 <= c < KC:
                    conv_out(c)
                c = step - 4 - sk
                if 0 <= c < KC:
                    stats_mm(c)

            S_.op("act", lambda e: e.activation(out=mean_sb[:], in_=bank(MEAN_BANK, TB), func=AF.Copy),
                  reads=[("ps", MEAN_BANK)], writes=["mean_sb"])
            S_.op("dve", lambda e: e.scalar_tensor_tensor(
                out=nmr_sb[:], in0=mean_sb[:], scalar=-1.0, in1=mean_sb[:], op0=ALU.mult, op1=ALU.mult),
                reads=["mean_sb"], writes=["nmr_sb"])
            S_.op("dve", lambda e: e.scalar_tensor_tensor(
                out=var_sb[:], in0=bank(EX2_BANK, TB), scalar=EPS, in1=nmr_sb[:],
                op0=ALU.add, op1=ALU.add),
                reads=[("ps", EX2_BANK), "nmr_sb"], writes=["var_sb"])
            S_.op("act", lambda e: e.activation(out=var_sb[:], in_=var_sb[:], func=AF.Sqrt),
                  reads=["var_sb"], writes=["var_sb"])
            S_.op("dve", lambda e: e.reciprocal(out=var_sb[:], in_=var_sb[:]),
                  reads=["var_sb"], writes=["var_sb"])
            S_.op("dve", lambda e: e.scalar_tensor_tensor(
                out=nmr_sb[:], in0=mean_sb[:], scalar=-1.0, in1=var_sb[:], op0=ALU.mult, op1=ALU.mult),
                reads=["mean_sb", "var_sb"], writes=["nmr_sb"])

        def stage_spatial(i, heads=None):
            for h in (range(H) if heads is None else heads):
                b = ps_alloc.get()
                for q in range(NT):
                    S_.op("pe", lambda e, h=h, q=q, b=b: e.matmul(
                        ps[:, b, q * 128:(q + 1) * 128], lhsT=vhat[:, q, h * 128:(h + 1) * 128],
                        rhs=wsT[:, h, :], start=True, stop=True),
                        reads=[("vhat", q), "wsT"], writes=[("ps", b)])
                ts_ = h % 2
                S_.op("dve", lambda e, h=h, b=b, ts_=ts_: e.scalar_tensor_tensor(
                    out=tt[:, ts_, :].rearrange("p (a b) -> p a b", b=128),
                    in0=ps[:, b, :].rearrange("p (a b) -> p a b", b=128),
                    scalar=vecs[:, V_SLN_G, h:h + 1],
                    in1=Bm[:, h:h + 1, :].to_broadcast([128, NT, 128]),
                    op0=ALU.mult, op1=ALU.add),
                    reads=[("ps", b), "vecs", "Bm"], writes=[("tt", ts_)])
                S_.op("dve", lambda e, h=h, ts_=ts_: e.tensor_tensor(
                    out=ub[:, h, :], in0=tt[:, ts_, :], in1=ub[:, h, :], op=ALU.mult),
                    reads=[("tt", ts_), ("ub", h)], writes=[("ub", h)])
                bg_step()

        NO = (D + 511) // 512
        OW = min(512, D)

        def stage_out_pre(i):
            for q in range(NT):
                t0 = i * TB + q * 128
                dma1("sp", ("ld_xres", q), xres[:, q, :], x_ap[t0:t0 + 128, :], writes=[("xres", q)])

        def stage_out(i):
            for q in range(NT):
                t0 = i * TB + q * 128
                sl = q
                pb = ps_alloc.get(NO)
                for n in range(NO):
                    for j in range(2 * KC):
                        src = convb if j < KC else ub
                        jj = j % KC
                        key = ("convb", jj) if j < KC else ("ub", jj)
                        S_.op("pe", lambda e, src=src, jj=jj, j=j, q=q, n=n, pb=pb: e.matmul(
                            bank(pb + n, OW), lhsT=src[:, jj, q * 128:(q + 1) * 128],
                            rhs=wout[:, j, n * OW:(n + 1) * OW],
                            start=(j == 0), stop=(j == 2 * KC - 1)),
                            reads=[key, ("wout", j)], writes=[("ps", pb + n)])
                    S_.op("dve", lambda e, n=n, pb=pb, sl=sl: e.tensor_tensor(
                        out=xres[:, sl, n * OW:(n + 1) * OW], in0=bank(pb + n, OW),
                        in1=xres[:, sl, n * OW:(n + 1) * OW], op=ALU.add),
                        reads=[("ps", pb + n), ("xres", sl)], writes=[("xres", sl)])
                S_.op("act", lambda e, sl=sl: e.activation(
                    out=junk[:], in_=xres[:, sl, :], func=AF.Square, accum_out=ss2[:, sl:sl + 1]),
                    reads=[("xres", sl)], writes=["junk", ("ss2", sl)])
                S_.op("dve", lambda e, sl=sl: e.tensor_scalar(
                    out=rs2[:, sl:sl + 1], in0=ss2[:, sl:sl + 1], scalar1=1.0 / D, scalar2=EPS,
                    op0=ALU.mult, op1=ALU.add), reads=[("ss2", sl)], writes=[("rs2", sl)])
                S_.op("pool", lambda e, sl=sl: e.tensor_tensor(
                    out=rs2[:, sl:sl + 1], in0=rs2[:, sl:sl + 1], in1=negh[:], op=ALU.pow),
                    reads=[("rs2", sl), "negh"], writes=[("rs2", sl)])
                S_.op("dve", lambda e, sl=sl: e.scalar_tensor_tensor(
                    out=xres[:, sl, :], in0=xres[:, sl, :], scalar=rs2[:, sl:sl + 1], in1=fgmat[:],
                    op0=ALU.mult, op1=ALU.mult),
                    reads=[("xres", sl), ("rs2", sl), "fgmat"], writes=[("xres", sl)])
                dma1("act", ("st_out", sl), out_ap[t0:t0 + 128, :], xres[:, sl, :],
                     reads=[("xres", sl)], writes=[("out", i, q)])

        stage_Xpre(0)
        prologue_part1()
        prologue_extract()
        stage_XT(0)
        n_sp = (H + 3) // 4
        for _ in range(n_sp):
            bg_tasks.pop(0)()
        stg_issue(NSTG)
        bg_hold = bg_tasks[:]
        del bg_tasks[:]
        stage_Zc(0)
        bg_tasks.extend(bg_hold)
        load_unit(UID_GATE + 0, [(2 * D + 0 * CW, CW)], 0, prefetch=True)
        if NB > 1:
            stage_Xpre(1)
        for i in range(NB):
            ps_alloc.n = 8 if i > 0 else 6
            if i > 0:
                stage_out_pre(i - 1)
            stage_lnapply_gate(i - 1 if i > 0 else None, i)
            stage_Zr_v(i)
            if i > 0:
                stage_out(i - 1)
            if i + 1 < NB:
                stage_XT(i + 1)
            stage_Zr_ub(i, with_spatial=True)
            stage_spatial(i, [H - 1])
            bg_step(len(bg_tasks))
            ps_alloc.n = 6
            if ps_alloc.p >= 6:
                ps_alloc.p = 0
            stage_conv(i, i + 2 if i + 2 < NB else None, i + 1 if i + 1 < NB else None)
        stage_out_pre(NB - 1)
        stage_lnapply_gate(NB - 1, None)
        stage_out(NB - 1)
        S_.fence("sp", [("out", i, q) for i in range(NB) for q in range(NT)])

        S_.emit(nc)
    return nc


def host_layout(D, x_b, c_b, w_ada, b_ada, norm_g, w_in, conv_w, conv_b, conv_ln_g, conv_ln_b,
                sg_ln_g, sg_ln_b, w_s, b_s, w_out, final_g):
    KD = D // 128
    f = lambda a: np.ascontiguousarray(a, dtype=np.float32)
    fm = lambda v: f(np.asarray(v).reshape(KD, 128).T)
    vecs = np.stack([fm(norm_g), fm(conv_b), fm(conv_ln_g), fm(conv_ln_b), fm(sg_ln_g), fm(sg_ln_b)],
                    axis=1)
    cw = np.asarray(conv_w)[:, 0, :]
    W3 = np.zeros((128, KD, 4, 9, 4), np.float32)
    for s_ in range(4):
        for jj in range(9):
            for r in range(4):
                k = 4 * jj + s_ - 1 - r
                if 0 <= k < CONVW:
                    W3[s_ * 32:(s_ + 1) * 32, :, :, jj, r] = cw[k].reshape(KD, 4, 32).transpose(2, 0, 1)
    return {
        "x": f(x_b),
        "c": fm(c_b),
        "w_ada": f(w_ada),
        "b_ada": f(np.asarray(b_ada).reshape(1, -1)),
        "vecs": f(vecs),
        "w_in": f(w_in),
        "convw3": f(W3.reshape(128, KD * 144)),
        "w_sT": f(np.asarray(w_s).transpose(2, 0, 1)),
        "b_s": f(np.asarray(b_s).reshape(1, -1)),
        "w_out": f(w_out),
        "final_g": f(np.asarray(final_g).reshape(1, -1)),
    }


def kernel(x, c, w_ada, b_ada, norm_g, w_in, conv_w, conv_b, conv_ln_g, conv_ln_b,
           sg_ln_g, sg_ln_b, w_s, b_s, w_out, final_g):
    x = np.asarray(x)
    B, S, D = x.shape
    nc = build_program(D, S)
    in_maps = []
    for b in range(B):
        in_maps.append(host_layout(
            D, x[b], np.asarray(c)[b], np.asarray(w_ada)[0], np.asarray(b_ada)[0],
            np.asarray(norm_g)[0], np.asarray(w_in)[0], np.asarray(conv_w)[0],
            np.asarray(conv_b)[0], np.asarray(conv_ln_g)[0], np.asarray(conv_ln_b)[0],
            np.asarray(sg_ln_g)[0], np.asarray(sg_ln_b)[0], np.asarray(w_s)[0],
            np.asarray(b_s)[0], np.asarray(w_out)[0], final_g))
    res = run_bass_kernel_spmd(nc, in_maps, core_ids=list(range(B)))
    return np.stack([np.asarray(r["out"], dtype=np.float32) for r in res.results], axis=0)
```

```python
import contextlib
import numpy as np
import concourse.bass as bass
import concourse.mybir as mybir
from concourse.bass_utils import run_bass_kernel_spmd

F32 = mybir.dt.float32
BF16 = mybir.dt.bfloat16
AF = mybir.ActivationFunctionType
ALU = mybir.AluOpType
AX = mybir.AxisListType

EPS = 1e-6
CONVW = 31
TB = 512
HB = 16
SEM_WINDOW = 1500


class Sched:
    ENGS = ("pe", "act", "dve", "pool", "sp")

    def __init__(self):
        self.prog = {e: [] for e in self.ENGS}
        self.res = {}
        self.dma_count = {}

    def _deps(self, reads, writes):
        deps = []
        for k in reads:
            r = self.res.get(k)
            if r and r[0] is not None:
                deps.append((r[0], "raw"))
        for k in writes:
            r = self.res.get(k)
            if r:
                if r[0] is not None:
                    deps.append((r[0], "waw"))
                for t in r[1]:
                    deps.append((t, "war"))
        return deps

    def _commit(self, tok, reads, writes):
        for k in reads:
            self.res.setdefault(k, [None, []])[1].append(tok)
        for k in writes:
            self.res[k] = [tok, []]

    def op(self, eng, fn, reads=(), writes=()):
        deps = self._deps(reads, writes)
        tok = ("c", eng, len(self.prog[eng]))
        self.prog[eng].append(dict(kind="c", fn=fn, deps=deps, tok=tok))
        self._commit(tok, reads, writes)
        return tok

    def dma(self, eng, key, fn, reads=(), writes=(), n=1):
        deps = self._deps(reads, writes)
        c = self.dma_count.get(key, 0) + n
        self.dma_count[key] = c
        tok = ("d", key, c)
        self.prog[eng].append(dict(kind="d", fn=fn, deps=deps, tok=tok, key=key))
        self._commit(tok, reads, writes)
        return tok

    def fence(self, eng, keys):
        deps = self._deps(keys, keys)
        self.prog[eng].append(dict(kind="f", fn=None, deps=deps, tok=None))

    def emit(self, nc):
        needed = set()
        for e in self.ENGS:
            for o in self.prog[e]:
                for (t, ty) in o["deps"]:
                    if t[0] != "c":
                        continue
                    if t[1] == e and e in ("pe", "sp"):
                        continue
                    needed.add((t[1], t[2]))
        incno = {}
        nwin = {}
        for e in self.ENGS:
            cnt = 0
            for i, o in enumerate(self.prog[e]):
                if o["kind"] == "c" and (e, i) in needed:
                    cnt += 1
                    incno[(e, i)] = cnt
            nwin[e] = (cnt + SEM_WINDOW - 1) // SEM_WINDOW
        with contextlib.ExitStack() as st:
            csem = {}
            for e in self.ENGS:
                for w in range(nwin[e]):
                    csem[(e, w)] = st.enter_context(nc.semaphore(f"c_{e}_{w}"))
            dsem = {}
            for i, k in enumerate(self.dma_count):
                dsem[k] = st.enter_context(nc.semaphore(f"d_{i}"))
            block = st.enter_context(nc.Block())

            def body(engobj, e):
                waited_c = {}
                waited_d = {}
                for i, o in enumerate(self.prog[e]):
                    wc = {}
                    wd = {}
                    for (t, ty) in o["deps"]:
                        if t[0] == "c":
                            if t[1] == e and e in ("pe", "sp"):
                                continue
                            n = incno[(t[1], t[2])]
                            if waited_c.get(t[1], 0) >= n:
                                continue
                            wc[t[1]] = max(wc.get(t[1], 0), n)
                        else:
                            if waited_d.get(t[1], 0) >= t[2]:
                                continue
                            wd[t[1]] = max(wd.get(t[1], 0), t[2])
                    for e2, n in wc.items():
                        waited_c[e2] = n
                        w = (n - 1) // SEM_WINDOW
                        engobj.wait_ge(csem[(e2, w)], (n - 1) % SEM_WINDOW + 1)
                    for k, n in wd.items():
                        waited_d[k] = n
                        engobj.wait_ge(dsem[k], 16 * n)
                    if o["kind"] == "c":
                        ins = o["fn"](engobj)
                        n = incno.get((e, i))
                        if n is not None:
                            w = (n - 1) // SEM_WINDOW
                            ins.then_inc(csem[(e, w)], 1)
                    elif o["kind"] == "d":
                        o["fn"](engobj, dsem[o["key"]])

            block.tensor(lambda eo: body(eo, "pe"))
            block.scalar(lambda eo: body(eo, "act"))
            block.vector(lambda eo: body(eo, "dve"))
            block.gpsimd(lambda eo: body(eo, "pool"))
            block.sync(lambda eo: body(eo, "sp"))


class PsumAlloc:
    def __init__(self, n):
        self.n = n
        self.p = 0

    def get(self, k=1):
        if self.p % k:
            self.p += k - self.p % k
        if self.p + k > self.n:
            self.p = 0
        b = self.p
        self.p += k
        return b


def build_program(D, S, w_scratch=True):
    KD = D // 128
    KC = KD
    H = KD
    NT = TB // 128
    NB = S // TB
    CW = min(512, D)
    UPS = D // CW
    CWP = CW // 2
    NPU = D // CWP
    NSLOT = 3
    NUNITS = 2 * NPU + 2 * UPS
    GW = TB + 2 * HB
    E6 = 6 * D
    assert S % TB == 0 and D % 128 == 0

    nc = bass.Bass("TRN2", target_bir_lowering=False)
    x_d = nc.dram_tensor("x", [S, D], F32, kind="ExternalInput")
    c_d = nc.dram_tensor("c", [128, KD], F32, kind="ExternalInput")
    wada_d = nc.dram_tensor("w_ada", [D, 3 * D], F32, kind="ExternalInput")
    bada_d = nc.dram_tensor("b_ada", [1, 3 * D], F32, kind="ExternalInput")
    vecs_d = nc.dram_tensor("vecs", [128, 6, KD], F32, kind="ExternalInput")
    win_d = nc.dram_tensor("w_in", [D, E6], F32, kind="ExternalInput")
    convw_d = nc.dram_tensor("convw3", [128, KC * 4 * 36], F32, kind="ExternalInput")
    wsT_d = nc.dram_tensor("w_sT", [128, H, 128], F32, kind="ExternalInput")
    bs_d = nc.dram_tensor("b_s", [1, H * 128], F32, kind="ExternalInput")
    wout_d = nc.dram_tensor("w_out", [2 * D, D], F32, kind="ExternalInput")
    fg_d = nc.dram_tensor("final_g", [1, D], F32, kind="ExternalInput")
    out_d = nc.dram_tensor("out", [S, D], F32, kind="ExternalOutput")
    wbf_d = nc.dram_tensor("wbf_scratch", [NUNITS, 128, KD, CW], BF16)

    x_ap, out_ap = x_d.ap(), out_d.ap()
    win_r = win_d.ap().rearrange("(j p) n -> p j n", p=128)
    wout_r = wout_d.ap().rearrange("(j p) n -> p j n", p=128)
    wada_r = wada_d.ap().rearrange("(j p) n -> p j n", p=128)
    wbf_ap = wbf_d.ap()

    def bcast_row(handle, off, n, parts=128):
        return bass.AP(handle, off, [[0, parts], [1, n]])

    S_ = Sched()
    ps_alloc = PsumAlloc(6)
    MEAN_BANK, EX2_BANK = 6, 7

    with contextlib.ExitStack() as st:
        def sb(name, shape, dt):
            return st.enter_context(nc.sbuf_tensor("sb_" + name, shape, dt))

        ps = st.enter_context(nc.psum_tensor("ps", [128, 8, 512], F32))

        xt = sb("xt", [128, 2, D], F32)
        junk = sb("junk", [128, D], BF16)
        ssx = sb("ssx", [128, 2], F32)
        rsx = sb("rsx", [128, 2], F32)
        xs = sb("xs", [128, NT, D], BF16)
        hT = sb("hT", [128, 2, KD, TB], BF16)
        wst = sb("wst", [128, NSLOT, KD, CW], BF16)
        G = sb("G", [128, 2, KC, GW], BF16)
        th = sb("th", [128, 2, TB], BF16)
        sga = sb("sga", [128, KC, TB], BF16)
        sgb = sb("sgb", [128, 2, TB], BF16)
        ub = sb("ub", [128, KC, TB], BF16)
        vst = sb("vst", [128, 2, 2, 6], F32)
        vmv = sb("vmv", [128, 2, 2], F32)
        vrs = sb("vrs", [128, 2], F32)
        vnm = sb("vnm", [128, 2], F32)
        vhat = sb("vhat", [128, NT, D], BF16)
        convb = sb("convb", [128, KC, TB], BF16)
        convsq = sb("convsq", [128, 3, TB], BF16)
        mean_sb = sb("mean_sb", [128, TB], F32)
        var_sb = sb("var_sb", [128, TB], F32)
        nmr_sb = sb("nmr_sb", [128, TB], F32)
        ntmp = sb("ntmp", [128, 2, TB], F32)
        sact = sb("sact", [128, 2, TB], BF16)
        tt = sb("tt", [128, 2, TB], BF16)
        xres = sb("xres", [128, NT, D], F32)
        ss2 = sb("ss2", [128, NT], F32)
        rs2 = sb("rs2", [128, NT], F32)
        LSLOTS = 5
        Lw = sb("Lw", [128, LSLOTS, 9, 32, 4], BF16)
        maskP = sb("maskP", [128, 9, 32, 4], BF16)
        W3b = sb("W3b", [128, KC, 4, 36], BF16)
        identP = sb("identP", [128, 32], BF16)
        TP = GW // 4
        Rsb = sb("Rsb", [128, 2, 4, TP], BF16)
        Csb = sb("Csb", [128, 2, TB], BF16)
        wout = sb("wout", [128, 2 * KC, D], BF16)
        ident_b = sb("ident_b", [128, 128], BF16)
        ones_b = sb("ones_b", [128, 128], BF16)
        ones_f = sb("ones_f", [128, 128], F32)
        negh = sb("negh", [128, 1], F32)
        fgmat = sb("fgmat", [128, D], F32)
        Bm = sb("Bm", [128, H, 128], F32)
        wsT = sb("wsT", [128, H, 128], BF16)
        vecs = sb("vecs", [128, 6, KD], F32)
        Gs = sb("Gs", [128, KD], F32)
        shiftv = sb("shiftv", [128, KD], F32)
        cact = sb("cact", [128, KD], F32)
        identrep = var_sb[:].rearrange("p (a b) -> p a b", b=128)
        crep = nmr_sb[:, 0:256].rearrange("p (a b) -> p a b", b=128)
        stg = ntmp
        exttmp = mean_sb

        V_NORM_G, V_CONV_B, V_CLN_G, V_CLN_B, V_SLN_G, V_SLN_B = range(6)

        def bank(b, n=512):
            return ps[:, b, 0:n]

        def bank_bf(b):
            return ps[:, b, :].bitcast(BF16)

        def dma1(eng, key, out, in_, reads=(), writes=()):
            def fn(e, sem, out=out, in_=in_):
                e.dma_start(out=out, in_=in_).then_inc(sem, 16)
            return S_.dma(eng, key, fn, reads=reads, writes=writes)

        dma1("sp", "ld_c", cact[:], c_d.ap(), writes=["cact"])
        dma1("sp", "ld_vecs", vecs[:], vecs_d.ap(), writes=["vecs"])
        w3stg = ub[:].rearrange("p a b -> p (a b)").bitcast(F32)
        dma1("sp", "ld_convw", w3stg[:, 0:KC * 144], convw_d.ap(), writes=[("ub", c) for c in range(KC)])
        dma1("sp", "ld_fg", fgmat[:], bcast_row(fg_d, 0, D), writes=["fgmat"])

        S_.op("pool", lambda e: e.memset(identrep[:], 0.0), writes=["var_sb"])
        for r in range(4):
            S_.op("pool", lambda e, r=r: e.affine_select(
                out=identrep[:, r, :], in_=identrep[:, r, :], compare_op=ALU.not_equal,
                fill=1.0, base=0, pattern=[[-1, 128]], channel_multiplier=1),
                reads=["var_sb"], writes=["var_sb"])
        S_.op("pool", lambda e: e.tensor_copy(out=ident_b[:], in_=identrep[:, 0, :]),
              reads=["var_sb"], writes=["ident_b"])
        S_.op("pool", lambda e: e.memset(ones_b[:], 1.0 / D), writes=["ones_b"])
        S_.op("pool", lambda e: e.memset(ones_f[:], 1.0), writes=["ones_f"])
        S_.op("pool", lambda e: e.memset(negh[:], -0.5), writes=["negh"])
        S_.op("dve", lambda e: e.tensor_scalar(
            out=W3b[:].rearrange("p a b c -> p (a b c)"), in0=w3stg[:, 0:KC * 144], scalar1=0.5,
            scalar2=None, op0=ALU.mult),
            reads=[("ub", c) for c in range(KC)], writes=["W3b"])
        S_.op("dve", lambda e: e.tensor_tensor(out=identP[:], in0=ident_b[:, 0:32], in1=ident_b[:, 32:64],
                                               op=ALU.add), reads=["ident_b"], writes=["identP"])
        S_.op("dve", lambda e: e.tensor_tensor(out=identP[:], in0=identP[:], in1=ident_b[:, 64:96],
                                               op=ALU.add), reads=["ident_b", "identP"], writes=["identP"])
        S_.op("dve", lambda e: e.tensor_tensor(out=identP[:], in0=identP[:], in1=ident_b[:, 96:128],
                                               op=ALU.add), reads=["ident_b", "identP"], writes=["identP"])
        S_.op("dve", lambda e: e.tensor_copy(
            out=maskP[:], in_=identP[:].unsqueeze(1).unsqueeze(3).to_broadcast([128, 9, 32, 4])),
            reads=["identP"], writes=["maskP"])
        S_.op("act", lambda e: e.activation(out=cact[:], in_=cact[:], func=AF.Silu),
              reads=["cact"], writes=["cact"])

        NMB = (3 * D + 511) // 512
        assert NMB <= 6
        stg_i = [0]

        def next_stg():
            s = stg_i[0] % 2
            stg_i[0] += 1
            return s

        def mwid(nb):
            return min(512, 3 * D - nb * 512)

        f32view = lambda t, pat: t[:].rearrange(pat).bitcast(F32)
        big_stg = [
            (f32view(convb, "p a b -> p (a b)"), [("convb", c) for c in range(KC)]),
            (f32view(sga, "p a b -> p (a b)"), [("sga", c) for c in range(KC)]),
            (f32view(ub, "p a b -> p (a b)"), [("ub", c) for c in range(KC)]),
            (f32view(vhat, "p a b -> p (a b)"), [("vhat", q) for q in range(NT)]),
        ]
        CREP_SL = NT - 1
        crep_all = xres[:, CREP_SL, :].rearrange("p (a b) -> p a b", b=128)
        crep_keys = [("xres", CREP_SL)]
        NSTG = NT - 1
        brow = G[:].rearrange("p a b c -> p (a b c)").bitcast(F32)
        brow_keys = [(k_, b_, c_) for k_ in ("G", "Gl", "Gr") for b_ in range(2) for c_ in range(KC)]
        dma1("sp", "ld_bada", brow[0:1, 0:2 * D], bada_d.ap()[:, 0:2 * D], writes=brow_keys)
        browg = Lw[:].rearrange("p a b c d -> p (a b c d)").bitcast(F32)
        browg_keys = [("Lw", sl_) for sl_ in range(LSLOTS)]
        dma1("sp", "ld_badag", browg[0:1, 0:D], bada_d.ap()[:, 2 * D:3 * D], writes=browg_keys)

        def prologue_part1():
            for j in range(KD):
                S_.op("dve", lambda e, j=j: e.tensor_copy(
                    out=crep_all[:, j, :], in_=cact[:, j:j + 1].to_broadcast([128, 128])),
                    reads=["cact"], writes=crep_keys)
            nb1 = (2 * D + 511) // 512
            for j in range(KD):
                stgv, skeys = big_stg[j % 4]
                dma1("sp", ("ld_bigstg", j % 4), stgv[:, 0:2 * D], wada_r[:, j, 0:2 * D], writes=skeys)
                for nb in range(nb1):
                    mw = min(512, 2 * D - nb * 512)
                    S_.op("pe", lambda e, j=j, nb=nb, mw=mw, stgv=stgv: e.matmul(
                        bank(nb, mw), lhsT=crep_all[:, j, :], rhs=stgv[:, nb * 512:nb * 512 + mw],
                        start=(j == 0), stop=False),
                        reads=crep_keys + skeys, writes=[("ps", nb)])
            for nb in range(nb1):
                mw = min(512, 2 * D - nb * 512)
                S_.op("pe", lambda e, nb=nb, mw=mw: e.matmul(
                    bank(nb, mw), lhsT=ones_f[0:1, :], rhs=brow[0:1, nb * 512:nb * 512 + mw],
                    start=False, stop=True),
                    reads=["ones_f"] + brow_keys, writes=[("ps", nb)])

        def gate_cols(n0, n):
            b_, o_ = divmod(n0, 512)
            assert o_ + n <= 512
            return ps[:, MEAN_BANK + b_, o_:o_ + n], ("ps", MEAN_BANK + b_)

        stg_loads = []
        stg_issued = [0]

        def stg_issue(upto):
            while stg_issued[0] < min(upto, len(stg_loads)):
                stg_loads[stg_issued[0]][1]()
                stg_issued[0] += 1

        def stg_register(src_ap):
            k = len(stg_loads)
            sl = k % NSTG
            stg_loads.append((sl, lambda sl=sl, src_ap=src_ap: dma1(
                "sp", ("ld_xres", sl), xres[:, sl, :], src_ap, writes=[("xres", sl)])))
            return k, sl

        def gate_task(j):
            k, sl = stg_register(wada_r[:, j, 2 * D:3 * D])

            def run():
                stg_issue(k + 1)
                for n0 in range(0, D, 512):
                    mw = min(512, D - n0)
                    dst, dkey = gate_cols(n0, mw)
                    S_.op("pe", lambda e, sl=sl, n0=n0, mw=mw, dst=dst: e.matmul(
                        dst, lhsT=crep_all[:, j, :], rhs=xres[:, sl, n0:n0 + mw],
                        start=(j == 0), stop=False),
                        reads=crep_keys + [("xres", sl)], writes=[dkey])
                stg_issue(k + 1 + NSTG)
            return run

        def gate_bias_task():
            for n0 in range(0, D, 512):
                mw = min(512, D - n0)
                dst, dkey = gate_cols(n0, mw)
                S_.op("pe", lambda e, n0=n0, mw=mw, dst=dst: e.matmul(
                    dst, lhsT=ones_f[0:1, :],
                    rhs=browg[0:1, n0:n0 + mw], start=False, stop=True),
                    reads=["ones_f"] + browg_keys, writes=[dkey])

        def mod_cols(c0, n):
            b, o = divmod(c0, 512)
            assert o + n <= 512, (c0, n)
            return ps[:, b, o:o + n], ("ps", b)

        def extract(dst, c0, dst_key):
            for j0 in range(0, KD, 4):
                nj = min(4, KD - j0)
                src, key = mod_cols(c0 + j0 * 128, nj * 128)
                S_.op("dve", lambda e, src=src, nj=nj: e.tensor_tensor(
                    out=exttmp[:, 0:nj * 128], in0=src,
                    in1=identrep[:, 0:nj, :].rearrange("p a b -> p (a b)"), op=ALU.mult),
                    reads=[key, "var_sb"], writes=["mean_sb"])
                S_.op("dve", lambda e, j0=j0, nj=nj: e.tensor_reduce(
                    out=dst[:, j0:j0 + nj],
                    in_=exttmp[:, 0:nj * 128].rearrange("p (a b) -> p a b", b=128),
                    axis=AX.X, op=ALU.add),
                    reads=["mean_sb"], writes=[dst_key])

        def prologue_extract():
            extract(shiftv, 0, "shiftv")
            extract(Gs, D, "Gs")
            S_.op("dve", lambda e: e.scalar_tensor_tensor(
                out=Gs[:], in0=Gs[:], scalar=1.0, in1=vecs[:, V_NORM_G, :], op0=ALU.add, op1=ALU.mult),
                reads=["Gs", "vecs"], writes=["Gs"])

        def fold_task(j):
            k, sl = stg_register(wout_r[:, j, :])

            def run():
                stg_issue(k + 1)
                for n0 in range(0, D, 512):
                    nn = min(512, D - n0)
                    src, key = gate_cols(n0, nn)
                    S_.op("dve", lambda e, n0=n0, nn=nn, sl=sl, src=src: e.tensor_tensor(
                        out=wout[:, j, n0:n0 + nn], in0=xres[:, sl, n0:n0 + nn], in1=src, op=ALU.mult),
                        reads=[("xres", sl), key], writes=[("wout", j)])
                stg_issue(k + 1 + NSTG)
            return run

        def spatial_setup_task(h0):
            def run():
                nh = min(4, H - h0)
                s = next_stg()
                dma1("sp", ("ld_ntmp", s), stg[:, s, 0:nh * 128].rearrange("p (a b) -> p a b", b=128),
                     wsT_d.ap()[:, h0:h0 + nh, :], writes=[("ntmp", s)])
                S_.op("act", lambda e, s=s: e.activation(
                    out=wsT[:, h0:h0 + nh, :].rearrange("p a b -> p (a b)"),
                    in_=stg[:, s, 0:nh * 128], func=AF.Copy),
                    reads=[("ntmp", s)], writes=["wsT"])
                S_.op("pe", lambda e, s=s: e.matmul(
                    bank(MEAN_BANK, nh * 128), lhsT=ones_f[:], rhs=stg[:, s, 0:nh * 128],
                    start=True, stop=True),
                    reads=["ones_f", ("ntmp", s)], writes=[("ps", MEAN_BANK)])
                s2 = next_stg()
                dma1("sp", ("ld_ntmp", s2), stg[:, s2, 0:nh * 128],
                     bcast_row(bs_d, h0 * 128, nh * 128), writes=[("ntmp", s2)])
                for hh in range(nh):
                    h = h0 + hh
                    S_.op("dve", lambda e, h=h, hh=hh, s2=s2: e.scalar_tensor_tensor(
                        out=Bm[:, h, :], in0=ps[:, MEAN_BANK, hh * 128:(hh + 1) * 128],
                        scalar=vecs[:, V_SLN_B, h:h + 1], in1=stg[:, s2, hh * 128:(hh + 1) * 128],
                        op0=ALU.mult, op1=ALU.add),
                        reads=[("ps", MEAN_BANK), "vecs", ("ntmp", s2)], writes=["Bm"])
            return run

        bg_tasks = []
        for h0 in range(0, H, 4):
            bg_tasks.append(spatial_setup_task(h0))
        for j in range(KD):
            bg_tasks.append(gate_task(j))
        bg_tasks.append(gate_bias_task)
        for j in range(2 * KC):
            bg_tasks.append(fold_task(j))

        def bg_step(n=1):
            for _ in range(n):
                if bg_tasks:
                    bg_tasks.pop(0)()

        unit_ctr = [0]

        prefetched = {}

        def load_unit(uid, pieces, blk, prefetch=False):
            if (uid, blk) in prefetched:
                return prefetched.pop((uid, blk))
            slot = unit_ctr[0] % NSLOT
            unit_ctr[0] += 1
            wkey = ("wst", slot)
            if blk == 0 or not w_scratch:
                hk = KD // 2 if KD >= 2 else KD

                def fn(e, sem, slot=slot, pieces=pieces, hk=hk):
                    o = 0
                    for (c0, ncol) in pieces:
                        for j0 in range(0, KD, hk):
                            e.dma_start(out=wst[:, slot, j0:j0 + hk, o:o + ncol],
                                        in_=win_r[:, j0:j0 + hk, c0:c0 + ncol]).then_inc(sem, 16)
                        o += ncol
                S_.dma("pool", ("ld_wst_sw", slot), fn, writes=[wkey],
                       n=len(pieces) * ((KD + hk - 1) // hk))
                if w_scratch:
                    dma1("sp", ("st_wbf", slot), wbf_ap[uid], wst[:, slot, :, :],
                         reads=[wkey], writes=[("wbf", uid)])
            else:
                dma1("sp", ("ld_wst", slot), wst[:, slot, :, :], wbf_ap[uid],
                     reads=[("wbf", uid)], writes=[wkey])
            if prefetch:
                prefetched[(uid, blk)] = slot
            return slot

        def load_pair(uid0, slab_a, slab_b, u, blk):
            return load_unit(uid0 + u, [(slab_a * D + u * CWP, CWP), (slab_b * D + u * CWP, CWP)], blk)

        def load_full(uid0, slab, u, blk):
            return load_unit(uid0 + u, [(slab * D + u * CW, CW)], blk)

        UID_CONV, UID_GATE, UID_UB, UID_V = 0, NPU, NPU + UPS, 2 * NPU + UPS

        def stage_Xpre(i, tiles=None):
            for q in (range(NT) if tiles is None else tiles):
                t0 = i * TB + q * 128
                sl = q % 2
                dma1("sp", ("ld_xt", sl), xt[:, sl, :], x_ap[t0:t0 + 128, :], writes=[("xt", sl)])
                S_.op("act", lambda e, sl=sl, q=q: e.activation(
                    out=xs[:, q, :], in_=xt[:, sl, :], func=AF.Square, accum_out=ssx[:, sl:sl + 1]),
                    reads=[("xt", sl)], writes=[("xs", q), ("ssx", sl)])
                S_.op("dve", lambda e, sl=sl: e.tensor_scalar(
                    out=rsx[:, sl:sl + 1], in0=ssx[:, sl:sl + 1], scalar1=1.0 / D, scalar2=EPS,
                    op0=ALU.mult, op1=ALU.add), reads=[("ssx", sl)], writes=[("rsx", sl)])
                S_.op("pool", lambda e, sl=sl: e.tensor_tensor(
                    out=rsx[:, sl:sl + 1], in0=rsx[:, sl:sl + 1], in1=negh[:], op=ALU.pow),
                    reads=[("rsx", sl), "negh"], writes=[("rsx", sl)])
                S_.op("dve", lambda e, sl=sl, q=q: e.tensor_scalar(
                    out=xs[:, q, :], in0=xt[:, sl, :], scalar1=rsx[:, sl:sl + 1], scalar2=None,
                    op0=ALU.mult), reads=[("xt", sl), ("rsx", sl)], writes=[("xs", q)])

        def stage_XT(i):
            nb4 = max(1, KD // 2)
            pb = ps_alloc.get(4 if nb4 > 2 else nb4)
            for q in range(NT):
                for j in range(KD):
                    b = pb + j // 2
                    col = (j % 2) * 512 + q * 128
                    S_.op("pe", lambda e, q=q, j=j, b=b, col=col: e.transpose(
                        bank_bf(b)[:, col:col + 128], xs[:, q, j * 128:(j + 1) * 128], ident_b[:]),
                        reads=[("xs", q), "ident_b"], writes=[("ps", b)])
            for j in range(KD):
                b = pb + j // 2
                col = (j % 2) * 512
                if (j // 2) % 2 == 0:
                    S_.op("dve", lambda e, j=j, b=b, col=col: e.tensor_scalar(
                        out=hT[:, i % 2, j, :], in0=bank_bf(b)[:, col:col + 512],
                        scalar1=Gs[:, j:j + 1], scalar2=shiftv[:, j:j + 1], op0=ALU.mult, op1=ALU.add),
                        reads=[("ps", b), "Gs", "shiftv"], writes=[("hT", i % 2, j)])
                else:
                    S_.op("act", lambda e, j=j, b=b, col=col: e.activation(
                        out=hT[:, i % 2, j, :], in_=bank_bf(b)[:, col:col + 512], func=AF.Identity,
                        bias=shiftv[:, j:j + 1], scale=Gs[:, j:j + 1]),
                        reads=[("ps", b), "Gs", "shiftv"], writes=[("hT", i % 2, j)])

        def zfm(slot, cc, b, hb):
            for j in range(KD):
                S_.op("pe", lambda e, j=j, slot=slot, cc=cc, b=b, hb=hb: e.matmul(
                    bank(b, TB), lhsT=wst[:, slot, j, cc * 128:(cc + 1) * 128], rhs=hT[:, hb, j, :],
                    start=(j == 0), stop=(j == KD - 1)),
                    reads=[("wst", slot), ("hT", hb, j)], writes=[("ps", b)])

        CPF = CW // 128
        CPP = CWP // 128

        zc_slots = {}

        def zc_chunk(i, c):
            gb = i % 2
            u, cc = divmod(c, CPP)
            if cc == 0:
                zc_slots[(i, u)] = load_pair(UID_CONV, 0, 1, u, i)
            sp_ = zc_slots[(i, u)]
            ba = ps_alloc.get()
            zfm(sp_, cc, ba, i % 2)
            bg = ps_alloc.get()
            zfm(sp_, CPP + cc, bg, i % 2)
            ts_ = c % 2
            S_.op("act", lambda e, bg=bg, ts_=ts_: e.activation(
                out=th[:, ts_, :], in_=bank(bg, TB), func=AF.Tanh, scale=0.5),
                reads=[("ps", bg)], writes=[("th", ts_)])
            S_.op("dve", lambda e, ba=ba, ts_=ts_, c=c, gb=gb: e.scalar_tensor_tensor(
                out=G[:, gb, c, HB:HB + TB], in0=th[:, ts_, :], scalar=1.0, in1=bank(ba, TB),
                op0=ALU.add, op1=ALU.mult),
                reads=[("th", ts_), ("ps", ba)], writes=[("G", gb, c)])
            bg_step()
            if i > 0:
                ob = (i - 1) % 2
                S_.op("dve", lambda e, gb=gb, ob=ob, c=c: e.tensor_copy(
                    out=G[:, ob, c, HB + TB:HB + TB + HB], in_=G[:, gb, c, HB:HB + HB]),
                    reads=[("G", gb, c)], writes=[("Gr", ob, c)])
                S_.op("dve", lambda e, gb=gb, ob=ob, c=c: e.tensor_copy(
                    out=G[:, gb, c, 0:HB], in_=G[:, ob, c, TB:TB + HB]),
                    reads=[("G", ob, c)], writes=[("Gl", gb, c)])

        def zc_edges(i):
            gb = i % 2
            if i == 0:
                S_.op("pool", lambda e, gb=gb: e.memset(G[:, gb, :, 0:HB], 0.0),
                      writes=[("Gl", gb, c) for c in range(KC)])
            if i == NB - 1:
                S_.op("pool", lambda e, gb=gb: e.memset(G[:, gb, :, HB + TB:HB + TB + HB], 0.0),
                      writes=[("Gr", gb, c) for c in range(KC)])

        def stage_Zc(i):
            zc_edges(i)
            for c in range(KC):
                zc_chunk(i, c)

        def ln_MA(c):
            ts_ = c % 2
            S_.op("dve", lambda e, c=c, ts_=ts_: e.tensor_tensor(
                out=ntmp[:, ts_, :], in0=convb[:, c, :], in1=var_sb[:], op=ALU.mult),
                reads=[("convb", c), "var_sb"], writes=[("ntmp", ts_)])
            S_.op("dve", lambda e, ts_=ts_: e.tensor_tensor(
                out=ntmp[:, ts_, :], in0=ntmp[:, ts_, :], in1=nmr_sb[:], op=ALU.add),
                reads=[("ntmp", ts_), "nmr_sb"], writes=[("ntmp", ts_)])
            S_.op("act", lambda e, c=c, ts_=ts_: e.activation(
                out=sact[:, ts_, :], in_=ntmp[:, ts_, :], func=AF.Silu,
                bias=vecs[:, V_CLN_B, c:c + 1], scale=vecs[:, V_CLN_G, c:c + 1]),
                reads=[("ntmp", ts_), "vecs"], writes=[("sact", ts_)])

        def ln_F(c):
            ts_ = c % 2
            S_.op("dve", lambda e, c=c, ts_=ts_: e.tensor_tensor(
                out=convb[:, c, :], in0=sact[:, ts_, :], in1=sga[:, c, :], op=ALU.mult),
                reads=[("sact", ts_), ("sga", c)], writes=[("convb", c)])

        def stage_lnapply_gate(i_ln, i_gate):
            slots = {}
            for step in range(KC + 1):
                if i_ln is not None and step < KC:
                    ln_MA(step)
                c = step - 1
                if c < 0:
                    continue
                if i_ln is not None:
                    ln_F(c)
                if i_gate is not None:
                    u, cc = divmod(c, CPF)
                    if cc == 0:
                        slots[u] = load_full(UID_GATE, 2, u, i_gate)
                    b = ps_alloc.get()
                    zfm(slots[u], cc, b, i_gate % 2)
                    S_.op("act", lambda e, b=b, c=c: e.activation(
                        out=sga[:, c, :], in_=bank(b, TB), func=AF.Silu),
                        reads=[("ps", b)], writes=[("sga", c)])
                    bg_step()

        def stage_Zr_ub(i, with_spatial=False):
            for u in range(NPU):
                sp_ = load_pair(UID_UB, 3, 5, u, i)
                for cc in range(CPP):
                    c = u * CPP + cc
                    if with_spatial and c >= 1:
                        stage_spatial(i, [c - 1])
                    bu = ps_alloc.get()
                    zfm(sp_, cc, bu, i % 2)
                    bb = ps_alloc.get()
                    zfm(sp_, CPP + cc, bb, i % 2)
                    ts_ = c % 2
                    S_.op("act", lambda e, bb=bb, ts_=ts_: e.activation(
                        out=sgb[:, ts_, :], in_=bank(bb, TB), func=AF.Silu),
                        reads=[("ps", bb)], writes=[("sgb", ts_)])
                    S_.op("dve", lambda e, bu=bu, ts_=ts_, c=c: e.tensor_tensor(
                        out=ub[:, c, :], in0=bank(bu, TB), in1=sgb[:, ts_, :], op=ALU.mult),
                        reads=[("ps", bu), ("sgb", ts_)], writes=[("ub", c)])
                    bg_step()
        def stage_Zr_v(i):
            assert UPS <= 2
            vslots = [load_full(UID_V, 4, u, i) for u in range(UPS)]
            for q in range(NT):
                pb = ps_alloc.get(UPS)
                sl = q % 2
                for u in range(UPS):
                    for j in range(KD):
                        S_.op("pe", lambda e, j=j, q=q, u=u, pb=pb: e.matmul(
                            bank(pb + u, CW), lhsT=hT[:, i % 2, j, q * 128:(q + 1) * 128],
                            rhs=wst[:, vslots[u], j, :], start=(j == 0), stop=(j == KD - 1)),
                            reads=[("wst", vslots[u]), ("hT", i % 2, j)], writes=[("ps", pb + u)])
                    S_.op("dve", lambda e, u=u, pb=pb, sl=sl: e.bn_stats(
                        out=vst[:, sl, u, :], in_=bank(pb + u, CW)),
                        reads=[("ps", pb + u)], writes=[("vst", sl, u)])
                S_.op("dve", lambda e, sl=sl: e.bn_aggr(
                    out=vmv[:, sl, :], in_=vst[:, sl, 0:UPS, :].rearrange("p a b -> p (a b)")),
                    reads=[("vst", sl, u) for u in range(UPS)], writes=[("vmv", sl)])
                S_.op("dve", lambda e, sl=sl: e.tensor_scalar(
                    out=vrs[:, sl:sl + 1], in0=vmv[:, sl, 1:2], scalar1=EPS, scalar2=None,
                    op0=ALU.add), reads=[("vmv", sl)], writes=[("vrs", sl)])
                S_.op("pool", lambda e, sl=sl: e.tensor_tensor(
                    out=vrs[:, sl:sl + 1], in0=vrs[:, sl:sl + 1], in1=negh[:], op=ALU.pow),
                    reads=[("vrs", sl), "negh"], writes=[("vrs", sl)])
                S_.op("dve", lambda e, sl=sl: e.scalar_tensor_tensor(
                    out=vnm[:, sl:sl + 1], in0=vmv[:, sl, 0:1], scalar=-1.0, in1=vrs[:, sl:sl + 1],
                    op0=ALU.mult, op1=ALU.mult), reads=[("vmv", sl), ("vrs", sl)], writes=[("vnm", sl)])
                for u in range(UPS):
                    S_.op("act", lambda e, u=u, pb=pb, sl=sl, q=q: e.activation(
                        out=vhat[:, q, u * CW:(u + 1) * CW], in_=bank(pb + u, CW), func=AF.Identity,
                        bias=vnm[:, sl:sl + 1], scale=vrs[:, sl:sl + 1]),
                        reads=[("ps", pb + u), ("vnm", sl), ("vrs", sl)], writes=[("vhat", q)])
                bg_step()

        lw_ctr = [0]

        def lw_gen(c, g):
            slot = lw_ctr[0] % LSLOTS
            lw_ctr[0] += 1
            S_.op("dve", lambda e, c=c, g=g, slot=slot: e.tensor_tensor(
                out=Lw[:, slot, :, :, :], in0=maskP[:],
                in1=W3b[:, c, g, :].rearrange("p (j r) -> p j r", r=4).unsqueeze(2).to_broadcast([128, 9, 32, 4]),
                op=ALU.mult),
                reads=["maskP", "W3b"], writes=[("Lw", slot)])
            return slot

        def stage_conv(i, xpre_blk=None, i_zc=None):
            gb = i % 2
            if i_zc is not None:
                zc_edges(i_zc)
            gkeys = lambda c: [("G", gb, c), ("Gl", gb, c), ("Gr", gb, c)]

            def stats_mm(c):
                sq = c % 3
                S_.op("pe", lambda e, c=c: e.matmul(
                    bank(MEAN_BANK, TB), lhsT=ones_b[:], rhs=convb[:, c, :],
                    start=(c == 0), stop=(c == KC - 1)),
                    reads=["ones_b", ("convb", c)], writes=[("ps", MEAN_BANK)])
                S_.op("pe", lambda e, c=c, sq=sq: e.matmul(
                    bank(EX2_BANK, TB), lhsT=ones_b[:], rhs=convsq[:, sq, :],
                    start=(c == 0), stop=(c == KC - 1)),
                    reads=["ones_b", ("convsq", sq)], writes=[("ps", EX2_BANK)])

            def conv_in(c):
                rs = c % 2
                pb = ps_alloc.get(2)
                gview = G[:, gb, c, :].rearrange("p (t s) -> p s t", s=4)
                for g in range(4):
                    bk, col0 = pb + g // 2, (g % 2) * TP
                    for s_ in range(4):
                        S_.op("pe", lambda e, g=g, s_=s_, bk=bk, col0=col0: e.matmul(
                            ps[32 * s_:32 * s_ + 32, bk, col0:col0 + TP],
                            lhsT=ident_b[:, 32 * g:32 * g + 32], rhs=gview[:, s_, :],
                            start=True, stop=True, tile_position=(0, 32 * s_)),
                            reads=["ident_b"] + gkeys(c), writes=[("ps", bk)])
                for h in range(2):
                    S_.op("act", lambda e, h=h, rs=rs, pb=pb: e.activation(
                        out=Rsb[:, rs, 2 * h:2 * h + 2, :],
                        in_=ps[:, pb + h, 0:2 * TP].rearrange("p (a b) -> p a b", b=TP), func=AF.Copy),
                        reads=[("ps", pb + h)], writes=[("Rsb", rs, h)])

            def conv_out(c):
                rs = c % 2
                sq = c % 3
                b2 = ps_alloc.get()
                for r in range(4):
                    for g in range(4):
                        S_.op("pe", lambda e, g=g, r=r, b2=b2, rs=rs: e.matmul(
                            ps[32 * g:32 * g + 32, b2, r * 128:(r + 1) * 128],
                            lhsT=ident_b[:].rearrange("p (c r) -> p r c", r=4)[:, r, :],
                            rhs=Csb[:, rs, g * 128:(g + 1) * 128],
                            start=True, stop=True, tile_position=(0, 32 * g)),
                            reads=["ident_b", ("Csb", rs)], writes=[("ps", b2)])
                src = ps[:, b2, :].rearrange("p (r t) -> p r t", r=4)
                S_.op("act", lambda e, c=c, src=src: e.activation(
                    out=convb[:, c, :].rearrange("p (t r) -> p r t", r=4), in_=src, func=AF.Identity,
                    bias=vecs[:, V_CONV_B, c:c + 1], scale=1.0),
                    reads=[("ps", b2), "vecs"], writes=[("convb", c)])
                S_.op("act", lambda e, c=c, sq=sq, src=src: e.activation(
                    out=convsq[:, sq, :].rearrange("p (t r) -> p r t", r=4), in_=src, func=AF.Square,
                    bias=vecs[:, V_CONV_B, c:c + 1], scale=1.0),
                    reads=[("ps", b2), "vecs"], writes=[("convsq", sq)])

            pend = {}
            gen_q = [(c, g) for c in range(KC) for g in range(4)]
            lw_slots = {}

            def gen_next(n):
                for _ in range(n):
                    if gen_q:
                        c_, g_ = gen_q.pop(0)
                        lw_slots[(c_, g_)] = lw_gen(c_, g_)

            gen_next(LSLOTS - 1)
            sk = 1 if i_zc is not None else 0
            for step in range(KC + 4 + sk):
                if step < KC and i_zc is not None:
                    zc_chunk(i_zc, step)
                c = step - sk
                if 0 <= c < KC:
                    conv_in(c)
                c = step - 1 - sk
                if 0 <= c < KC:
                    rs = c % 2
                    b = ps_alloc.get()
                    for g in range(4):
                        while (c, g) not in lw_slots:
                            gen_next(1)
                        sl_ = lw_slots[(c, g)]
                        for jj in range(9):
                            S_.op("pe", lambda e, g=g, jj=jj, b=b, rs=rs, sl_=sl_: e.matmul(
                                ps[:, b, g * 128:(g + 1) * 128], lhsT=Lw[:, sl_, jj, :, :].rearrange("p c r -> p (c r)"),
                                rhs=Rsb[:, rs, g, jj:jj + 128], start=(jj == 0), stop=(jj == 8)),
                                reads=[("Lw", sl_), ("Rsb", rs, g // 2)], writes=[("ps", b)])
                        gen_next(1)
                    S_.op("act", lambda e, b=b, rs=rs: e.activation(
                        out=Csb[:, rs, :], in_=bank(b, TB), func=AF.Copy),
                        reads=[("ps", b)], writes=[("Csb", rs)])
                    if xpre_blk is not None and c < NT:
                        tl = [c] if KC >= NT else list(range(c * NT // KC, (c + 1) * NT // KC))
                        stage_Xpre(xpre_blk, tl)
                c = step - 2 - sk
                if 0 <= c < KC:
                    conv_out(c)
                c = step - 4 - sk
                if 0 <= c < KC:
                    stats_mm(c)

            S_.op("act", lambda e: e.activation(out=mean_sb[:], in_=bank(MEAN_BANK, TB), func=AF.Copy),
                  reads=[("ps", MEAN_BANK)], writes=["mean_sb"])
            S_.op("dve", lambda e: e.scalar_tensor_tensor(
                out=nmr_sb[:], in0=mean_sb[:], scalar=-1.0, in1=mean_sb[:], op0=ALU.mult, op1=ALU.mult),
                reads=["mean_sb"], writes=["nmr_sb"])
            S_.op("dve", lambda e: e.scalar_tensor_tensor(
                out=var_sb[:], in0=bank(EX2_BANK, TB), scalar=EPS, in1=nmr_sb[:],
                op0=ALU.add, op1=ALU.add),
                reads=[("ps", EX2_BANK), "nmr_sb"], writes=["var_sb"])
            S_.op("act", lambda e: e.activation(out=var_sb[:], in_=var_sb[:], func=AF.Sqrt),
                  reads=["var_sb"], writes=["var_sb"])
            S_.op("dve", lambda e: e.reciprocal(out=var_sb[:], in_=var_sb[:]),
                  reads=["var_sb"], writes=["var_sb"])
            S_.op("dve", lambda e: e.scalar_tensor_tensor(
                out=nmr_sb[:], in0=mean_sb[:], scalar=-1.0, in1=var_sb[:], op0=ALU.mult, op1=ALU.mult),
                reads=["mean_sb", "var_sb"], writes=["nmr_sb"])

        def stage_spatial(i, heads=None):
            for h in (range(H) if heads is None else heads):
                b = ps_alloc.get()
                for q in range(NT):
                    S_.op("pe", lambda e, h=h, q=q, b=b: e.matmul(
                        ps[:, b, q * 128:(q + 1) * 128], lhsT=vhat[:, q, h * 128:(h + 1) * 128],
                        rhs=wsT[:, h, :], start=True, stop=True),
                        reads=[("vhat", q), "wsT"], writes=[("ps", b)])
                ts_ = h % 2
                S_.op("dve", lambda e, h=h, b=b, ts_=ts_: e.scalar_tensor_tensor(
                    out=tt[:, ts_, :].rearrange("p (a b) -> p a b", b=128),
                    in0=ps[:, b, :].rearrange("p (a b) -> p a b", b=128),
                    scalar=vecs[:, V_SLN_G, h:h + 1],
                    in1=Bm[:, h:h + 1, :].to_broadcast([128, NT, 128]),
                    op0=ALU.mult, op1=ALU.add),
                    reads=[("ps", b), "vecs", "Bm"], writes=[("tt", ts_)])
                S_.op("dve", lambda e, h=h, ts_=ts_: e.tensor_tensor(
                    out=ub[:, h, :], in0=tt[:, ts_, :], in1=ub[:, h, :], op=ALU.mult),
                    reads=[("tt", ts_), ("ub", h)], writes=[("ub", h)])
                bg_step()

        NO = (D + 511) // 512
        OW = min(512, D)

        def stage_out_pre(i):
            for q in range(NT):
                t0 = i * TB + q * 128
                dma1("sp", ("ld_xres", q), xres[:, q, :], x_ap[t0:t0 + 128, :], writes=[("xres", q)])

        def out_mm(q, n, pbn, js, first, last):
            for j in js:
                src = convb if j < KC else ub
                jj = j % KC
                key = ("convb", jj) if j < KC else ("ub", jj)
                S_.op("pe", lambda e, src=src, jj=jj, j=j: e.matmul(
                    bank(pbn, OW), lhsT=src[:, jj, q * 128:(q + 1) * 128],
                    rhs=wout[:, j, n * OW:(n + 1) * OW],
                    start=(first and j == js[0]), stop=(last and j == js[-1])),
                    reads=[key, ("wout", j)], writes=[("ps", pbn)])

        def stage_out_yb_first(i):
            assert NT * NO <= 8
            for q in range(NT):
                for n in range(NO):
                    out_mm(q, n, q * NO + n, list(range(KC, 2 * KC)), True, False)

        def stage_out(i, yb_done=False):
            for q in range(NT):
                t0 = i * TB + q * 128
                sl = q
                pb = q * NO if yb_done else ps_alloc.get(NO)
                for n in range(NO):
                    if yb_done:
                        out_mm(q, n, pb + n, list(range(KC)), False, True)
                    else:
                        out_mm(q, n, pb + n, list(range(2 * KC)), True, True)
                    S_.op("dve", lambda e, n=n, pb=pb, sl=sl: e.tensor_tensor(
                        out=xres[:, sl, n * OW:(n + 1) * OW], in0=bank(pb + n, OW),
                        in1=xres[:, sl, n * OW:(n + 1) * OW], op=ALU.add),
                        reads=[("ps", pb + n), ("xres", sl)], writes=[("xres", sl)])
                S_.op("act", lambda e, sl=sl: e.activation(
                    out=junk[:], in_=xres[:, sl, :], func=AF.Square, accum_out=ss2[:, sl:sl + 1]),
                    reads=[("xres", sl)], writes=["junk", ("ss2", sl)])
                S_.op("dve", lambda e, sl=sl: e.tensor_scalar(
                    out=rs2[:, sl:sl + 1], in0=ss2[:, sl:sl + 1], scalar1=1.0 / D, scalar2=EPS,
                    op0=ALU.mult, op1=ALU.add), reads=[("ss2", sl)], writes=[("rs2", sl)])
                S_.op("pool", lambda e, sl=sl: e.tensor_tensor(
                    out=rs2[:, sl:sl + 1], in0=rs2[:, sl:sl + 1], in1=negh[:], op=ALU.pow),
                    reads=[("rs2", sl), "negh"], writes=[("rs2", sl)])
                S_.op("dve", lambda e, sl=sl: e.scalar_tensor_tensor(
                    out=xres[:, sl, :], in0=xres[:, sl, :], scalar=rs2[:, sl:sl + 1], in1=fgmat[:],
                    op0=ALU.mult, op1=ALU.mult),
                    reads=[("xres", sl), ("rs2", sl), "fgmat"], writes=[("xres", sl)])
                dma1("act", ("st_out", sl), out_ap[t0:t0 + 128, :], xres[:, sl, :],
                     reads=[("xres", sl)], writes=[("out", i, q)])

        stage_Xpre(0)
        prologue_part1()
        prologue_extract()
        stage_XT(0)
        n_sp = (H + 3) // 4
        for _ in range(n_sp):
            bg_tasks.pop(0)()
        stg_issue(NSTG)
        bg_hold = bg_tasks[:]
        del bg_tasks[:]
        stage_Zc(0)
        bg_tasks.extend(bg_hold)
        load_unit(UID_GATE + 0, [(2 * D + 0 * CW, CW)], 0, prefetch=True)
        if NB > 1:
            stage_Xpre(1)
        for i in range(NB):
            ps_alloc.n = 8 if i > 0 else 6
            if i > 0:
                stage_out_pre(i - 1)
            stage_lnapply_gate(i - 1 if i > 0 else None, i)
            stage_Zr_v(i)
            if i > 0:
                stage_out(i - 1)
            if i + 1 < NB:
                stage_XT(i + 1)
            stage_Zr_ub(i, with_spatial=True)
            stage_spatial(i, [H - 1])
            bg_step(len(bg_tasks))
            ps_alloc.n = 6
            if ps_alloc.p >= 6:
                ps_alloc.p = 0
            stage_conv(i, i + 2 if i + 2 < NB else None, i + 1 if i + 1 < NB else None)
        stage_out_pre(NB - 1)
        stage_out_yb_first(NB - 1)
        stage_lnapply_gate(NB - 1, None)
        stage_out(NB - 1, yb_done=True)
        S_.fence("sp", [("out", i, q) for i in range(NB) for q in range(NT)])

        S_.emit(nc)
    return nc


def host_layout(D, x_b, c_b, w_ada, b_ada, norm_g, w_in, conv_w, conv_b, conv_ln_g, conv_ln_b,
                sg_ln_g, sg_ln_b, w_s, b_s, w_out, final_g):
    KD = D // 128
    f = lambda a: np.ascontiguousarray(a, dtype=np.float32)
    fm = lambda v: f(np.asarray(v).reshape(KD, 128).T)
    vecs = np.stack([fm(norm_g), fm(conv_b), fm(conv_ln_g), fm(conv_ln_b), fm(sg_ln_g), fm(sg_ln_b)],
                    axis=1)
    cw = np.asarray(conv_w)[:, 0, :]
    W3 = np.zeros((128, KD, 4, 9, 4), np.float32)
    for s_ in range(4):
        for jj in range(9):
            for r in range(4):
                k = 4 * jj + s_ - 1 - r
                if 0 <= k < CONVW:
                    W3[s_ * 32:(s_ + 1) * 32, :, :, jj, r] = cw[k].reshape(KD, 4, 32).transpose(2, 0, 1)
    return {
        "x": f(x_b),
        "c": fm(c_b),
        "w_ada": f(w_ada),
        "b_ada": f(np.asarray(b_ada).reshape(1, -1)),
        "vecs": f(vecs),
        "w_in": f(w_in),
        "convw3": f(W3.reshape(128, KD * 144)),
        "w_sT": f(np.asarray(w_s).transpose(2, 0, 1)),
        "b_s": f(np.asarray(b_s).reshape(1, -1)),
        "w_out": f(w_out),
        "final_g": f(np.asarray(final_g).reshape(1, -1)),
    }


def kernel(x, c, w_ada, b_ada, norm_g, w_in, conv_w, conv_b, conv_ln_g, conv_ln_b,
           sg_ln_g, sg_ln_b, w_s, b_s, w_out, final_g):
    x = np.asarray(x)
    B, S, D = x.shape
    nc = build_program(D, S)
    in_maps = []
    for b in range(B):
        in_maps.append(host_layout(
            D, x[b], np.asarray(c)[b], np.asarray(w_ada)[0], np.asarray(b_ada)[0],
            np.asarray(norm_g)[0], np.asarray(w_in)[0], np.asarray(conv_w)[0],
            np.asarray(conv_b)[0], np.asarray(conv_ln_g)[0], np.asarray(conv_ln_b)[0],
            np.asarray(sg_ln_g)[0], np.asarray(sg_ln_b)[0], np.asarray(w_s)[0],
            np.asarray(b_s)[0], np.asarray(w_out)[0], final_g))
    res = run_bass_kernel_spmd(nc, in_maps, core_ids=list(range(B)))
    return np.stack([np.asarray(r["out"], dtype=np.float32) for r in res.results], axis=0)
```
